# Optimizing a Trainium2 kernel written in Bass

```python
import jax, jax.numpy as jnp
from jax import lax
import numpy as np


D_MODEL = 4096
BATCH = 1
SEQ = 8192
DEPTH = 1

CHUNK = 64
Q_BLOCK = 128
ML_HEADS = 8
ML_V_DIM = 256
ML_QK_DIM = 128
ML_WIDTH = ML_HEADS * ML_V_DIM
ML_QK_WIDTH = ML_HEADS * ML_QK_DIM
SB_HEADS = 16
SB_HEAD_DIM = 128
SB_WIDTH = SB_HEADS * SB_HEAD_DIM
CONV_WIDTH = 4
D_FF = 11008
N_BRANCH = 2
EPS = 1e-6
IN_SIZES = (ML_QK_WIDTH, ML_QK_WIDTH, ML_WIDTH, ML_WIDTH, ML_HEADS, ML_HEADS,
            SB_WIDTH, SB_WIDTH, SB_WIDTH, D_MODEL, D_MODEL)
D_IN = 2 * ML_QK_WIDTH + 2 * ML_WIDTH + 2 * ML_HEADS + 3 * SB_WIDTH + N_BRANCH * D_MODEL

kernel_name = 'hybrid_mlstm_stickbreak_macaron'


def rmsnorm(x, g):
    xf = x.astype(jnp.float32)
    y = xf * lax.rsqrt(jnp.mean(xf * xf, axis=-1, keepdims=True) + EPS)
    return (y * g.astype(jnp.float32)).astype(x.dtype)


def swiglu(h, w1, w3, w2):
    return (jax.nn.silu(h @ w1) * (h @ w3)) @ w2


def split_cols(p, sizes):
    offs = []
    acc = 0
    for s in sizes[:-1]:
        acc += s
        offs.append(acc)
    return jnp.split(p, offs, axis=-1)


def causal_depthwise_conv(x, w):
    k = w.shape[0]
    return lax.conv_general_dilated(
        x, w[:, None, :].astype(x.dtype), window_strides=(1,), padding=[(k - 1, 0)],
        dimension_numbers=('NWC', 'WIO', 'NWC'), feature_group_count=x.shape[-1])


def mlstm(q, k, v, log_i, log_f):
    B, S, H, dk = q.shape
    dv = v.shape[-1]
    nc = S // CHUNK
    k = k * (dk ** -0.5)

    def chunks(a):
        a = a.reshape((B, nc, CHUNK) + a.shape[2:])
        return jnp.moveaxis(jnp.moveaxis(a, 1, 0), 3, 2)

    tril = jnp.tril(jnp.ones((CHUNK, CHUNK), dtype=bool))

    def step(carry, inp):
        C, n, m = carry
        qc, kc, vc, ic, fc = inp
        b = jnp.cumsum(fc, axis=-1)
        log_d = jnp.where(tril, b[..., :, None] - b[..., None, :] + ic[..., None, :], -jnp.inf)
        m_inter = b + m[..., None]
        m_t = jnp.maximum(m_inter, jnp.max(log_d, axis=-1))
        w = jnp.einsum('bhtd,bhsd->bhts', qc, kc) * jnp.exp(log_d - m_t[..., None])
        decay = jnp.exp(m_inter - m_t)
        num = (decay[..., None] * jnp.einsum('bhtd,bhdv->bhtv', qc, C)
               + jnp.einsum('bhts,bhsv->bhtv', w, vc))
        den = decay * jnp.einsum('bhtd,bhd->bht', qc, n) + jnp.sum(w, axis=-1)
        h = num / jnp.maximum(jnp.abs(den), jnp.exp(-m_t))[..., None]
        b_last = b[..., -1]
        a = b_last[..., None] - b + ic
        m_new = jnp.maximum(b_last + m, jnp.max(a, axis=-1))
        carry_scale = jnp.exp(b_last + m - m_new)
        src = jnp.exp(a - m_new[..., None])
        C = carry_scale[..., None, None] * C + jnp.einsum('bhs,bhsd,bhsv->bhdv', src, kc, vc)
        n = carry_scale[..., None] * n + jnp.einsum('bhs,bhsd->bhd', src, kc)
        return (C, n, m_new), h

    init = (jnp.zeros((B, H, dk, dv), q.dtype), jnp.zeros((B, H, dk), q.dtype),
            jnp.zeros((B, H), q.dtype))
    _, hs = lax.scan(step, init, (chunks(q), chunks(k), chunks(v), chunks(log_i), chunks(log_f)))
    hs = jnp.moveaxis(jnp.moveaxis(hs, 2, 3), 0, 1)
    return hs.reshape(B, S, H, dv)


def stick_breaking(q, k, v):
    B, S, H, d = q.shape
    nb = S // Q_BLOCK
    q = jnp.transpose(q, (0, 2, 1, 3)) * (d ** -0.5)
    k = jnp.transpose(k, (0, 2, 1, 3))
    v = jnp.transpose(v, (0, 2, 1, 3))
    q_blocks = jnp.transpose(q.reshape(B, H, nb, Q_BLOCK, d), (2, 0, 1, 3, 4))
    starts = jnp.arange(nb, dtype=jnp.int32) * Q_BLOCK
    key_pos = jnp.arange(S, dtype=jnp.int32)

    def block(args):
        qb, start = args
        z = jnp.einsum('bhqd,bhkd->bhqk', qb, k)
        qpos = start + jnp.arange(Q_BLOCK, dtype=jnp.int32)
        strict = key_pos[None, :] < qpos[:, None]
        log_beta = jax.nn.log_sigmoid(z)
        log_rest = jnp.where(strict, jax.nn.log_sigmoid(-z), 0.0)
        after = lax.cumsum(log_rest, axis=3, reverse=True) - log_rest
        weights = jnp.where(strict, jnp.exp(log_beta + after), 0.0)
        return jnp.einsum('bhqk,bhkd->bhqd', weights, v)

    out = lax.map(block, (q_blocks, starts))
    return jnp.transpose(out, (1, 0, 3, 2, 4)).reshape(B, S, H, d)


def setup_inputs(seed: int = 0) -> dict:
    key = jax.random.key(seed)
    ks = jax.random.split(key, 20)
    f32 = jnp.float32

    def w(k, shape, fan_in):
        return jax.random.normal(k, shape, f32) * (fan_in ** -0.5)

    def gain(k, shape):
        return 1.0 + 0.01 * jax.random.normal(k, shape, f32)

    return {
        'x': jax.random.normal(ks[0], (BATCH, SEQ, D_MODEL), f32),
        'g_ffn1': gain(ks[1], (DEPTH, D_MODEL)),
        'w1_ffn1': w(ks[2], (DEPTH, D_MODEL, D_FF), D_MODEL),
        'w3_ffn1': w(ks[3], (DEPTH, D_MODEL, D_FF), D_MODEL),
        'w2_ffn1': w(ks[4], (DEPTH, D_FF, D_MODEL), D_FF),
        'g_mix': gain(ks[5], (DEPTH, D_MODEL)),
        'w_in': w(ks[6], (DEPTH, D_MODEL, D_IN), D_MODEL),
        'conv_qk': w(ks[7], (DEPTH, CONV_WIDTH, 2 * ML_QK_WIDTH), CONV_WIDTH),
        'b_igate': 0.1 * jax.random.normal(ks[8], (DEPTH, ML_HEADS), f32),
        'b_fgate': 3.0 + 0.5 * jax.random.normal(ks[9], (DEPTH, ML_HEADS), f32),
        'g_mlstm_out': gain(ks[10], (DEPTH, ML_WIDTH)),
        'w_proj_a': w(ks[11], (DEPTH, ML_WIDTH, D_MODEL), ML_WIDTH),
        'w_proj_b': w(ks[12], (DEPTH, SB_WIDTH, D_MODEL), SB_WIDTH),
        'w_out': w(ks[13], (DEPTH, D_MODEL, D_MODEL), D_MODEL),
        'g_ffn2': gain(ks[14], (DEPTH, D_MODEL)),
        'w1_ffn2': w(ks[15], (DEPTH, D_MODEL, D_FF), D_MODEL),
        'w3_ffn2': w(ks[16], (DEPTH, D_MODEL, D_FF), D_MODEL),
        'w2_ffn2': w(ks[17], (DEPTH, D_FF, D_MODEL), D_FF),
        'g_final': gain(ks[18], (D_MODEL,)),
    }


def reference(x, g_ffn1, w1_ffn1, w3_ffn1, w2_ffn1, g_mix, w_in, conv_qk, b_igate, b_fgate,
              g_mlstm_out, w_proj_a, w_proj_b, w_out, g_ffn2, w1_ffn2, w3_ffn2, w2_ffn2, g_final):
    B, S, _ = x.shape
    f32 = jnp.float32
    for l in range(DEPTH):
        x = x + 0.5 * swiglu(rmsnorm(x, g_ffn1[l]), w1_ffn1[l], w3_ffn1[l], w2_ffn1[l])

        h = rmsnorm(x, g_mix[l])
        p = h @ w_in[l]
        mq, mk, mv, mo, mi, mf, sq, sk, sv, ga, gb = split_cols(p, IN_SIZES)

        qk = jax.nn.silu(causal_depthwise_conv(jnp.concatenate([mq, mk], axis=-1), conv_qk[l]))
        mq, mk = jnp.split(qk.astype(f32), 2, axis=-1)
        log_i = mi.astype(f32) + b_igate[l].astype(f32)
        log_f = jax.nn.log_sigmoid(mf.astype(f32) + b_fgate[l].astype(f32))
        ya = mlstm(mq.reshape(B, S, ML_HEADS, ML_QK_DIM), mk.reshape(B, S, ML_HEADS, ML_QK_DIM),
                   mv.astype(f32).reshape(B, S, ML_HEADS, ML_V_DIM), log_i, log_f)
        ya = ya * lax.rsqrt(jnp.mean(ya * ya, axis=-1, keepdims=True) + EPS)
        ya = ya * g_mlstm_out[l].astype(f32).reshape(ML_HEADS, ML_V_DIM)
        ya = (ya.reshape(B, S, ML_WIDTH) * jax.nn.sigmoid(mo.astype(f32))).astype(x.dtype)

        yb = stick_breaking(sq.astype(f32).reshape(B, S, SB_HEADS, SB_HEAD_DIM),
                            sk.astype(f32).reshape(B, S, SB_HEADS, SB_HEAD_DIM),
                            sv.astype(f32).reshape(B, S, SB_HEADS, SB_HEAD_DIM))
        yb = yb.reshape(B, S, SB_WIDTH).astype(x.dtype)

        merged = jax.nn.sigmoid(ga) * (ya @ w_proj_a[l]) + jax.nn.sigmoid(gb) * (yb @ w_proj_b[l])
        x = x + merged @ w_out[l]

        x = x + 0.5 * swiglu(rmsnorm(x, g_ffn2[l]), w1_ffn2[l], w3_ffn2[l], w2_ffn2[l])
    return rmsnorm(x, g_final)
```

```python
from contextlib import ExitStack
import numpy as np
import ml_dtypes
import concourse.bass as bass
import concourse.mybir as mybir
from concourse.bass_utils import run_bass_kernel_spmd

F32 = mybir.dt.float32
F32R = mybir.dt.float32r
SB_CUMSUM_DT = F32R
BF16 = mybir.dt.bfloat16
AF = mybir.ActivationFunctionType
ALU = mybir.AluOpType
NCORES = 8
EPS = 1e-6
NEG = -30000.0
NCONST = 6 * 128 + 4 * 512


class Cfg:
    def __init__(self, D=4096, T=1024, DFF=11008, G=4, PIECE=512 * 1024):
        self.D, self.T, self.DFF, self.G = D, T, DFF, G
        self.S = NCORES * T
        self.KC = D // 128
        self.FC = DFF // 128
        self.TH = T // 512
        self.NCH = self.S // 128
        self.NQT = self.S // 512
        base, rem = divmod(self.FC, G)
        self.groups = []
        f0 = 0
        for g in range(G):
            n = base + (1 if g < rem else 0)
            self.groups.append((f0, n))
            f0 += n
        self.PR1 = PIECE // (2 * T)
        self.NP1 = D // self.PR1
        self.KPP = self.PR1 // 128
        self.ND2 = PIECE // (128 * T * 2)
        self.NPF = NCORES // self.ND2
        self.NP2 = 4 * self.NPF
        self.PR2 = self.ND2 * 128
        assert self.PR1 % 128 == 0 and D % self.PR1 == 0 and self.ND2 >= 1


class Sched:
    def __init__(self, nc, enter):
        self.nc = nc
        self.enter = enter
        self.streams = {k: [] for k in ("pe", "act", "dve", "pool", "sp")}
        self.sems = {}
        self.count = {}
        for k in ("pe", "act", "dve", "pool"):
            self.sems[k] = enter(nc.semaphore("s_" + k))
            self.count[k] = 0
        self.waited = {k: {} for k in self.streams}
        self.lastw = {}
        self.readers = {}

    def _need(self, e, tok, waits):
        if tok is None:
            return
        k, v = tok
        if e == "pe" and k == "pe":
            return
        if k not in ("pe", "act", "dve", "pool"):
            v = self.count[k]
        if self.waited[e].get(k, 0) >= v:
            return
        self.waited[e][k] = v
        waits[k] = max(waits.get(k, 0), v)

    def _deps(self, e, reads, writes):
        waits = {}
        for r in reads:
            self._need(e, self.lastw.get(r), waits)
        for r in writes:
            self._need(e, self.lastw.get(r), waits)
            for k, v in self.readers.get(r, {}).items():
                self._need(e, (k, v), waits)
        return waits

    def _commit(self, tok, reads, writes):
        for r in reads:
            d = self.readers.setdefault(r, {})
            d[tok[0]] = max(d.get(tok[0], 0), tok[1])
        for r in writes:
            self.lastw[r] = tok
            self.readers[r] = {}

    def op(self, e, fn, reads=(), writes=()):
        waits = self._deps(e, reads, writes)
        self.count[e] += 1
        tok = (e, self.count[e])
        self.streams[e].append((waits, fn, e, 1))
        self._commit(tok, reads, writes)

    def dma(self, q, key, fn, reads=(), writes=(), inc=16):
        if key not in self.sems:
            self.sems[key] = self.enter(self.nc.semaphore("d%d" % len(self.sems)))
            self.count[key] = 0
        waits = self._deps(q, reads, writes)
        self.count[key] += inc
        tok = (key, self.count[key])
        self.streams[q].append((waits, fn, key, inc))
        self._commit(tok, reads, writes)

    def barrier(self):
        for e in self.streams:
            waits = {}
            for k, v in self.count.items():
                if v > 0:
                    self._need(e, (k, v), waits)
            if waits:
                self.streams[e].append((waits, None, None, 0))
        self.lastw = {}
        self.readers = {}

    def emit(self, block):
        def make(name):
            def body(engine):
                for waits, fn, semkey, inc in self.streams[name]:
                    for k, v in waits.items():
                        engine.wait_ge(self.sems[k], v)
                    if fn is None:
                        continue
                    inst = fn(engine)
                    inst.then_inc(self.sems[semkey], inc)
            return body
        block.tensor(make("pe"))
        block.scalar(make("act"))
        block.vector(make("dve"))
        block.gpsimd(make("pool"))
        block.sync(make("sp"))


class Mem:
    def __init__(self, big, start, nbytes):
        self.big, self.start, self.cap, self.off = big, start, nbytes, 0

    def alloc(self, shape, dtype):
        esz = 4 if dtype == F32 else 2
        n = int(np.prod(shape[1:]))
        nb = (n * esz + 63) // 64 * 64
        assert self.off + nb <= self.cap, ("SBUF overflow", self.off, nb, self.cap)
        a = (self.start + self.off) // 2
        ap = self.big[0:shape[0], a:a + n * esz // 2]
        self.off += nb
        if dtype == F32:
            ap = ap.bitcast(F32)
        if len(shape) == 3:
            ap = ap.rearrange("p (a b) -> p a b", a=shape[1])
        return ap

    def reset(self):
        self.off = 0


def build_program(cfg, stop_after=None):
    nc = bass.Bass("TRN2", target_bir_lowering=False)
    D, T, S, KC, FC, TH, NCH, NQT = cfg.D, cfg.T, cfg.S, cfg.KC, cfg.FC, cfg.TH, cfg.NCH, cfg.NQT
    NPF, ND2 = cfg.NPF, cfg.ND2

    def din(name, shape, dt=F32):
        return nc.dram_tensor(name, list(shape), dt, kind="ExternalInput").ap()

    def dscr(name, shape, dt):
        return nc.dram_tensor(name, list(shape), dt).ap()

    xT = din("xT", [KC, 128, T])
    gv = din("gv", [128, 4 * KC])
    w1 = [din("w1a", [FC, 128, KC * 128]), din("w1b", [FC, 128, KC * 128])]
    w3 = [din("w3a", [FC, 128, KC * 128]), din("w3b", [FC, 128, KC * 128])]
    w2 = [din("w2a", [KC, 128, FC * 128]), din("w2b", [KC, 128, FC * 128])]
    wA = din("wA", [128, KC * 768])
    wGt = din("wGt", [128, KC * 64])
    wV = din("wV", [128, KC * 512])
    NGC = 16 + 2 * KC
    wG = din("wG", [NGC, 128, KC * 128])
    wpa = din("wpa", [KC, 128, 16 * 128])
    wpb = din("wpb", [KC, 128, 16 * 128])
    wo = din("wo", [KC, 128, KC * 128])
    small = din("small", [128, 16 + 256])
    cf32 = din("cf32", [128, NCONST])
    identb = din("identb", [128, 128], BF16)
    outT = nc.dram_tensor("outT", [KC, 128, T], F32, kind="ExternalOutput").ap()
    dbg = nc.dram_tensor("dbg", [KC, 128, T], F32, kind="ExternalOutput").ap() if stop_after else None
    dbg2 = nc.dram_tensor("dbg2", [cfg.NP2, cfg.PR2, T], BF16, kind="ExternalOutput").ap() if stop_after else None

    xres = dscr("xres", [KC, 128, T], F32)
    gts = dscr("gts", [NGC, 128, T], BF16)
    h2loc = dscr("h2loc", [cfg.NP1, cfg.PR1, T], BF16)
    ex1mid = dscr("ex1mid", [cfg.NP1, 4 * cfg.PR1, T], BF16)
    ex1out = dscr("ex1out", [cfg.NP1, 8 * cfg.PR1, T], BF16)
    qkpre = dscr("qkpre", [2, 128, S], F32)
    sqk = dscr("sqk", [4, 128, S], BF16)
    gates_d = dscr("gates_d", [2, S], F32)
    vm_d = dscr("vm_d", [NCH, 128, 256], BF16)
    vs_d = dscr("vs_d", [2, NCH, 128, 128], BF16)
    rows_d = dscr("rows_d", [2, S], F32)
    yloc = dscr("yloc", [cfg.NP2, cfg.PR2, T], BF16)
    ex2mid = dscr("ex2mid", [cfg.NP2, 4 * cfg.PR2, T], BF16)
    ex2out = dscr("ex2out", [5 * cfg.NPF * 8 * cfg.PR2, T], BF16)
    ex2stage = dscr("ex2stage", [4, 8, 128, T], BF16)

    with ExitStack() as es:
        enter = es.enter_context
        CB = 16 * 1024
        HB = max(KC * T * 2, 64 * 1024)
        TOTAL = 200 * 1024
        big = enter(nc.sbuf_tensor("big", [128, TOTAL // 2], BF16))
        ps = [enter(nc.psum_tensor("ps%d" % i, [128, 512], F32)) for i in range(8)]
        psb = [p[:, :].bitcast(BF16) for p in ps]
        sc = Sched(nc, enter)
        mA = Mem(big, 0, CB)
        mB = Mem(big, CB, HB)
        mC = Mem(big, CB + HB, TOTAL - CB - HB)
        P = lambda i: ("ps", i)

        c_gv = mA.alloc([128, 4 * KC], F32)
        c_small = mA.alloc([128, 16 + 256], F32)
        c_f32 = mA.alloc([128, NCONST], F32)
        c_id = mA.alloc([128, 128], BF16)
        sc.dma("sp", "c0", lambda e: e.dma_start(out=c_gv, in_=gv), writes=["c_gv"])
        sc.dma("sp", "c1", lambda e: e.dma_start(out=c_small, in_=small), writes=["c_small"])
        sc.dma("sp", "c2", lambda e: e.dma_start(out=c_f32, in_=cf32), writes=["c_f32"])
        sc.dma("sp", "c3", lambda e: e.dma_start(out=c_id, in_=identb), writes=["c_id"])
        sc.barrier()
        ones_f = c_f32[:, 0:128]
        triN = c_f32[:, 128:256]
        restN = c_f32[:, 256:384]
        maskneg = c_f32[:, 384:512]
        Umat = c_f32[:, 512:640]
        identf = c_f32[:, 640:768]
        dmask = [c_f32[:, 768 + 512 * j:768 + 512 * (j + 1)] for j in range(4)]
        gml_b = c_small[:, 16:272]
        c_tr = mA.alloc([128, 256], BF16)
        sc.op("dve", lambda e: e.tensor_copy(out=c_tr, in_=c_f32[:, 128:384]), writes=["c_tr"])
        sc.barrier()
        triR = c_tr[:, 0:128]
        restR = c_tr[:, 128:256]
        ones_b = mA.alloc([128, 128], BF16)
        sc.op("dve", lambda e: e.tensor_copy(out=ones_b, in_=c_f32[:, 0:128]), writes=["ones_b"])
        sc.barrier()

        def finish(dump=None):
            if dump is not None:
                for c in range(KC):
                    sc.dma("sp", "dump", lambda e, c=c: e.dma_start(out=dbg[c], in_=dump[c]))
                for p in range(cfg.NP2):
                    sc.dma("sp", "dump", lambda e, p=p: e.dma_start(out=dbg2[p], in_=yloc[p]))
            sc.barrier()
            with nc.Block() as block:
                sc.emit(block)
            return nc

        def rmsnorm(src, gidx, hT, dst_dram=None):
            xs = [mC.alloc([128, T], F32) for _ in range(3)]
            sq = [mC.alloc([128, T], BF16) for _ in range(2)]
            rstd = mC.alloc([128, T], F32)
            outs = [mC.alloc([128, T], F32) for _ in range(2)] if hT is None else None
            for c in range(KC):
                s = c % 3
                sc.dma("sp", ("xs", s), lambda e, c=c, s=s: e.dma_start(out=xs[s], in_=src[c]),
                       writes=[("xs", s)])
                q = c % 2
                sc.op("act", lambda e, s=s, q=q: e.activation(out=sq[q], in_=xs[s], func=AF.Square),
                      reads=[("xs", s)], writes=[("sq", q)])
                for th in range(TH):
                    sc.op("pe", lambda e, q=q, th=th, c=c: e.matmul(
                        ps[4 + th][:, :], ones_b, sq[q][:, th * 512:(th + 1) * 512],
                        start=(c == 0), stop=(c == KC - 1)),
                        reads=[("sq", q)], writes=[P(4 + th)])
            for th in range(TH):
                sl = slice(th * 512, (th + 1) * 512)
                sc.op("act", lambda e, th=th, sl=sl: e.activation(
                    out=rstd[:, sl], in_=ps[4 + th][:, :], func=AF.Sqrt, bias=EPS, scale=1.0 / D),
                    reads=[P(4 + th)], writes=[("rstd", th)])
                sc.op("dve", lambda e, sl=sl: e.reciprocal(out=rstd[:, sl], in_=rstd[:, sl]),
                      reads=[("rstd", th)], writes=[("rstd", th)])
            for c in range(KC):
                s = c % 3
                sc.dma("sp", ("xs", s), lambda e, c=c, s=s: e.dma_start(out=xs[s], in_=src[c]),
                       writes=[("xs", s)])
                gcol = c_gv[:, gidx * KC + c:gidx * KC + c + 1]
                rd = [("xs", s)] + [("rstd", th) for th in range(TH)]
                if hT is not None:
                    sc.op("dve", lambda e, c=c, s=s, gcol=gcol: e.scalar_tensor_tensor(
                        out=hT[:, c, :], in0=xs[s], scalar=gcol, in1=rstd, op0=ALU.mult, op1=ALU.mult),
                        reads=rd, writes=[("hT", c)])
                else:
                    o = c % 2
                    sc.op("dve", lambda e, o=o, s=s, gcol=gcol: e.scalar_tensor_tensor(
                        out=outs[o], in0=xs[s], scalar=gcol, in1=rstd, op0=ALU.mult, op1=ALU.mult),
                        reads=rd, writes=[("nout", o)])
                    sc.dma("sp", ("nout", o), lambda e, o=o, c=c: e.dma_start(out=dst_dram[c], in_=outs[o]),
                           reads=[("nout", o)])
            sc.barrier()
            mC.reset()

        def ffn(l, src, hT):
            ngmax = max(n for _, n in cfg.groups)
            aT = mC.alloc([128, ngmax, T], BF16)
            w1t = [mC.alloc([128, KC, 128], BF16) for _ in range(2)]
            w3t = [mC.alloc([128, KC, 128], BF16) for _ in range(2)]
            w2t = [mC.alloc([128, ngmax, 128], BF16) for _ in range(2)]
            sil = [mC.alloc([128, 512], F32) for _ in range(2)]
            xo = [mC.alloc([128, 512], F32) for _ in range(4)]
            cnt_u = 0
            cnt_d = 0
            issued = set()

            def wload(f):
                if f in issued or f >= FC:
                    return
                issued.add(f)
                s = f % 2
                sc.dma("pool", ("w1t", s), lambda e, f=f, s=s: e.dma_start(
                    out=w1t[s].rearrange("p a b -> p (a b)"), in_=w1[l][f], max_dma_last_dim=8192),
                    writes=[("w1t", s)])
                sc.dma("pool", ("w3t", s), lambda e, f=f, s=s: e.dma_start(
                    out=w3t[s].rearrange("p a b -> p (a b)"), in_=w3[l][f], max_dma_last_dim=8192),
                    writes=[("w3t", s)])
            for g, (f0, ng) in enumerate(cfg.groups):
                for fi in range(ng):
                    f = f0 + fi
                    s = f % 2
                    wload(f)
                    for th in range(TH):
                        sl = slice(th * 512, (th + 1) * 512)
                        bu = (cnt_u % 2) * 2
                        q = cnt_u % 2
                        cnt_u += 1

                        def mm(e, s=s, sl=sl, bu=bu):
                            for kc in range(KC):
                                e.matmul(ps[bu][:, :], w1t[s][:, kc, :], hT[:, kc, sl],
                                         start=(kc == 0), stop=(kc == KC - 1))
                            for kc in range(KC):
                                r = e.matmul(ps[bu + 1][:, :], w3t[s][:, kc, :], hT[:, kc, sl],
                                             start=(kc == 0), stop=(kc == KC - 1))
                            return r
                        sc.op("pe", mm, reads=[("w1t", s), ("w3t", s)], writes=[P(bu), P(bu + 1)])
                        sc.op("act", lambda e, q=q, bu=bu: e.activation(out=sil[q], in_=ps[bu][:, :], func=AF.Silu),
                              reads=[P(bu)], writes=[("sil", q)])
                        sc.op("dve", lambda e, q=q, bu=bu, fi=fi, sl=sl: e.tensor_tensor(
                            out=aT[:, fi, sl], in0=ps[bu + 1][:, :], in1=sil[q], op=ALU.mult),
                            reads=[P(bu + 1), ("sil", q)], writes=[("aT", fi, th)])
                wload(f0 + ng)
                wload(f0 + ng + 1)
                for o in range(KC):
                    s = o % 2
                    sc.dma("pool", ("w2t", s), lambda e, o=o, s=s, f0=f0, ng=ng: e.dma_start(
                        out=w2t[s][:, 0:ng, :].rearrange("p a b -> p (a b)"),
                        in_=w2[l][o][:, f0 * 128:(f0 + ng) * 128], max_dma_last_dim=8192),
                        writes=[("w2t", s)])
                    for th in range(TH):
                        sl = slice(th * 512, (th + 1) * 512)
                        bd = 4 + (cnt_d % 4)
                        xs_ = cnt_d % 4
                        cnt_d += 1

                        def mm2(e, s=s, sl=sl, bd=bd, ng=ng):
                            for fi in range(ng):
                                r = e.matmul(ps[bd][:, :], w2t[s][:, fi, :], aT[:, fi, sl],
                                             start=(fi == 0), stop=(fi == ng - 1))
                            return r
                        sc.op("pe", mm2, reads=[("w2t", s)] + [("aT", fi, th) for fi in range(ng)], writes=[P(bd)])
                        srcd = src if g == 0 else xres
                        sc.dma("sp", ("xo", xs_), lambda e, o=o, sl=sl, xs_=xs_, srcd=srcd: e.dma_start(
                            out=xo[xs_], in_=srcd[o][:, sl]), reads=[("xres", o, th)], writes=[("xo", xs_)])
                        sc.op("dve", lambda e, bd=bd, xs_=xs_: e.scalar_tensor_tensor(
                            out=xo[xs_], in0=ps[bd][:, :], scalar=0.5, in1=xo[xs_], op0=ALU.mult, op1=ALU.add),
                            reads=[P(bd), ("xo", xs_)], writes=[("xo", xs_)])
                        sc.dma("sp", ("xo", xs_), lambda e, o=o, sl=sl, xs_=xs_: e.dma_start(
                            out=xres[o][:, sl], in_=xo[xs_]), reads=[("xo", xs_)], writes=[("xres", o, th)])
            sc.barrier()
            mC.reset()

        hT = mB.alloc([128, KC, T], BF16)
        rmsnorm(xT, 0, hT)
        ffn(0, xT, hT)
        if stop_after == "ffn1":
            return finish(xres)

        rmsnorm(xres, 1, hT)
        KPP = cfg.KPP
        for i in range(cfg.NP1):
            sc.dma("sp", ("h2st", i % 4), lambda e, i=i: e.dma_start(
                out=h2loc[i].rearrange("(k p) t -> p k t", p=128), in_=hT[:, i * KPP:(i + 1) * KPP, :]),
                writes=[("h2loc", i)])
        wA_t = mC.alloc([128, KC, 768], BF16)
        wGt_t = mC.alloc([128, KC, 64], BF16)
        wV_t = mC.alloc([128, KC, 512], BF16)
        head_mark = mC.off
        for kc0 in range(0, KC, 2):
            sc.dma("pool", "wA", lambda e, kc0=kc0: e.dma_start(
                out=wA_t[:, kc0:kc0 + 2, :].rearrange("p a b -> p (a b)"),
                in_=wA[:, kc0 * 768:(kc0 + 2) * 768], max_dma_last_dim=6144))
        sc.dma("pool", "wA", lambda e: e.dma_start(out=wGt_t.rearrange("p a b -> p (a b)"), in_=wGt,
                                                   max_dma_last_dim=8192))
        for kc0 in range(0, KC, 2):
            sc.dma("pool", "wA", lambda e, kc0=kc0: e.dma_start(
                out=wV_t[:, kc0:kc0 + 2, :].rearrange("p a b -> p (a b)"),
                in_=wV[:, kc0 * 512:(kc0 + 2) * 512], max_dma_last_dim=8192))
        for i in range(cfg.NP1):
            sc.dma("pool", ("cc", i % 16), lambda e, i=i: e.collective_compute(
                "AllGather", ALU.bypass, replica_groups=[[0, 1, 2, 3], [4, 5, 6, 7]],
                ins=[h2loc[i]], outs=[ex1mid[i]]),
                reads=[("h2loc", i)], writes=[("ex1mid", i)], inc=1)
        NWG = 3
        wgt = [mC.alloc([128, KC, 128], BF16) for _ in range(NWG)]
        gst = [mC.alloc([128, 512], BF16) for _ in range(4)]
        cnt = 0
        def ex1_stage2(i):
            sc.dma("pool", ("cc", i % 16), lambda e, i=i: e.collective_compute(
                "AllGather", ALU.bypass, replica_groups=[[0, 4], [1, 5], [2, 6], [3, 7]],
                ins=[ex1mid[i]], outs=[ex1out[i]]),
                reads=[("ex1mid", i)], writes=[("ex1out", i)], inc=1)
        st2_next = 0
        for f in range(NGC):
            s = f % NWG
            sc.dma("pool", ("wgt", s), lambda e, f=f, s=s: e.dma_start(
                out=wgt[s].rearrange("p a b -> p (a b)"), in_=wG[f], max_dma_last_dim=8192), writes=[("wgt", s)])
            if f >= 3 and f % 2 == 1 and st2_next < cfg.NP1:
                ex1_stage2(st2_next)
                st2_next += 1
            for th in range(TH):
                sl = slice(th * 512, (th + 1) * 512)
                b = cnt % 4
                cnt += 1

                def mm(e, s=s, sl=sl, b=b):
                    for kc in range(KC):
                        r = e.matmul(ps[b][:, :], wgt[s][:, kc, :], hT[:, kc, sl], start=(kc == 0), stop=(kc == KC - 1))
                    return r
                sc.op("pe", mm, reads=[("wgt", s)], writes=[P(b)])
                sc.op("act", lambda e, b=b: e.activation(out=gst[b], in_=ps[b][:, :], func=AF.Sigmoid),
                      reads=[P(b)], writes=[("gst", b)])
                sc.dma("sp", ("gst", b), lambda e, b=b, f=f, sl=sl: e.dma_start(out=gts[f][:, sl], in_=gst[b]),
                       reads=[("gst", b)])
        while st2_next < cfg.NP1:
            ex1_stage2(st2_next)
            st2_next += 1
        sc.barrier()
        if stop_after == "ex1":
            return finish(xres)

        mB.reset()
        mC.off = head_mark
        h2b = [mB.alloc([128, KC, 512], BF16) for _ in range(2)] if 2 * KC * 512 * 2 <= HB else \
            [mC.alloc([128, KC, 512], BF16) for _ in range(2)]
        stf = [mC.alloc([128, 512], F32) for _ in range(2)]
        stb = [mC.alloc([128, 512], BF16) for _ in range(4)]
        stg = [mC.alloc([64, 512], F32) for _ in range(2)]
        vst = [mC.alloc([128, 512], BF16) for _ in range(2)]
        pcnt = 0
        for tb in range(S // 512):
            r = (tb * 512) // T
            toff = (tb * 512) % T
            hs = tb % 2
            cs = slice(tb * 512, (tb + 1) * 512)
            def load_blk(tb_):
                r_ = (tb_ * 512) // T
                toff_ = (tb_ * 512) % T
                hs_ = tb_ % 2
                for i in range(cfg.NP1):
                    sc.dma("sp", ("h2b", hs_, i % 4), lambda e, i=i, r_=r_, toff_=toff_, hs_=hs_: e.dma_start(
                        out=h2b[hs_][:, i * KPP:(i + 1) * KPP, :],
                        in_=ex1out[i][r_ * cfg.PR1:(r_ + 1) * cfg.PR1, toff_:toff_ + 512].rearrange("(k p) t -> p k t", p=128)),
                        writes=[("h2b", hs_, i)])
            if tb == 0:
                load_blk(0)
            if tb + 1 < S // 512:
                load_blk(tb + 1)
            hkeys = [("h2b", hs, i) for i in range(cfg.NP1)]
            for j in range(7):
                b = pcnt % 4
                pcnt += 1
                if j < 6:
                    def mm(e, j=j, hs=hs, b=b):
                        for kc in range(KC):
                            r_ = e.matmul(ps[b][:, :], wA_t[:, kc, j * 128:(j + 1) * 128], h2b[hs][:, kc, :],
                                          start=(kc == 0), stop=(kc == KC - 1))
                        return r_
                else:
                    def mm(e, hs=hs, b=b):
                        for kc in range(KC):
                            r_ = e.matmul(ps[b][0:64, :], wGt_t[:, kc, :], h2b[hs][:, kc, :],
                                          start=(kc == 0), stop=(kc == KC - 1))
                        return r_
                sc.op("pe", mm, reads=hkeys, writes=[P(b)])
                if j < 2:
                    q = j
                    sc.op("act", lambda e, q=q, b=b: e.activation(out=stf[q], in_=ps[b][:, :], func=AF.Copy),
                          reads=[P(b)], writes=[("stf", q)])
                    sc.dma("sp", ("stf", q), lambda e, q=q, j=j, cs=cs: e.dma_start(out=qkpre[j][:, cs], in_=stf[q]),
                           reads=[("stf", q)])
                elif j < 6:
                    q = j - 2
                    sc.op("dve", lambda e, q=q, b=b: e.tensor_copy(out=stb[q], in_=ps[b][:, :]),
                          reads=[P(b)], writes=[("stb", q)])
                    sc.dma("sp", ("stb", q), lambda e, q=q, cs=cs: e.dma_start(out=sqk[q][:, cs], in_=stb[q]),
                           reads=[("stb", q)])
                else:
                    q = tb % 2
                    sc.op("act", lambda e, q=q, b=b: e.activation(out=stg[q], in_=ps[b][0:64, :], func=AF.Copy),
                          reads=[P(b)], writes=[("stg", q)])
                    sc.dma("sp", ("stg", q), lambda e, q=q, cs=cs: e.dma_start(out=gates_d[0:1, cs], in_=stg[q][0:1, :]),
                           reads=[("stg", q)])
                    sc.dma("sp", ("stg", q), lambda e, q=q, cs=cs: e.dma_start(out=gates_d[1:2, cs], in_=stg[q][32:33, :]),
                           reads=[("stg", q)])
            for sub in range(4):
                ch = tb * 4 + sub
                b = 4 + ch % 4
                vq = ch % 2

                def mmv(e, hs=hs, b=b, sub=sub):
                    for kc in range(KC):
                        r_ = e.matmul(ps[b][:, :], h2b[hs][:, kc, sub * 128:(sub + 1) * 128], wV_t[:, kc, :],
                                      start=(kc == 0), stop=(kc == KC - 1))
                    return r_
                sc.op("pe", mmv, reads=hkeys, writes=[P(b)])
                sc.op("dve", lambda e, b=b, vq=vq: e.tensor_copy(out=vst[vq], in_=ps[b][:, :]),
                      reads=[P(b)], writes=[("vst", vq)])
                sc.dma("sp", ("vst", vq), lambda e, vq=vq, ch=ch: e.dma_start(out=vm_d[ch], in_=vst[vq][:, 0:256]),
                       reads=[("vst", vq)])
                for h in range(2):
                    sc.dma("sp", ("vst", vq), lambda e, vq=vq, ch=ch, h=h: e.dma_start(
                        out=vs_d[h][ch], in_=vst[vq][:, 256 + 128 * h:384 + 128 * h]), reads=[("vst", vq)])
        sc.barrier()
        mB.reset()
        mC.reset()
        if stop_after == "heads":
            return finish(xres)

        def yloc_ap(fc, tok0, n):
            dc = tok0 // T
            to = tok0 % T
            piece = fc * NPF + dc // ND2
            row0 = (dc % ND2) * 128
            return piece, yloc[piece][row0:row0 + 128, to:to + n]

        ywrites = {p: [] for p in range(cfg.NP2)}
        qT = mB.alloc([128, S], BF16)
        kT = mB.alloc([128, S], BF16)
        negGb = mC.alloc([128, S], F32)
        vext = mC.alloc([128, NCH, 264], BF16)
        for c0 in range(0, NCH, 8):
            n = min(8, NCH - c0)
            sc.dma("sp", "vext", lambda e, c0=c0, n=n: e.dma_start(
                out=vext[:, c0:c0 + n, 0:256], in_=vm_d[c0:c0 + n].rearrange("c p d -> p c d")),
                writes=[("vext", c0)])
        sc.op("dve", lambda e: e.memset(vext[:, :, 256:257], 1.0), writes=["vones"])
        vkeys = [("vext", c0) for c0 in range(0, NCH, 8)] + ["vones"]
        SEG = min(2048, S)
        pre = [mC.alloc([128, SEG + 4], F32) for _ in range(2)]
        acc = [mC.alloc([128, SEG], F32) for _ in range(2)]
        it = 0
        for j in range(2):
            for sg in range(S // SEG):
                s = it % 2
                it += 1
                if sg == 0:
                    sc.op("dve", lambda e, s=s: e.memset(pre[s][:, 0:3], 0.0), writes=[("pre0", s)])
                    sc.dma("sp", ("pre", s), lambda e, s=s, j=j: e.dma_start(out=pre[s][:, 3:3 + SEG], in_=qkpre[j][:, 0:SEG]),
                           reads=[("pre0", s)], writes=[("pre", s)])
                else:
                    sc.dma("sp", ("pre", s), lambda e, s=s, j=j, sg=sg: e.dma_start(
                        out=pre[s][:, 0:3 + SEG], in_=qkpre[j][:, sg * SEG - 3:(sg + 1) * SEG]),
                        reads=[("pre0", s)], writes=[("pre", s)])
                wc = lambda i, j=j: c_small[:, 4 * j + i:4 * j + i + 1]
                sc.op("dve", lambda e, s=s, wc=wc: e.tensor_scalar(
                    out=acc[s], in0=pre[s][:, 3:3 + SEG], scalar1=wc(3), scalar2=None, op0=ALU.mult),
                    reads=[("pre", s), ("pre0", s)], writes=[("acc", s)])
                for i in (2, 1, 0):
                    sc.op("dve", lambda e, s=s, wc=wc, i=i: e.scalar_tensor_tensor(
                        out=acc[s], in0=pre[s][:, i:i + SEG], scalar=wc(i), in1=acc[s], op0=ALU.mult, op1=ALU.add),
                        reads=[("pre", s), ("acc", s)], writes=[("acc", s)])
                seg = slice(sg * SEG, (sg + 1) * SEG)
                if j == 0:
                    sc.op("act", lambda e, s=s, seg=seg: e.activation(out=qT[:, seg], in_=acc[s], func=AF.Silu),
                          reads=[("acc", s)], writes=[("qT", sg), ("pre", s), ("pre0", s)])
                else:
                    sc.op("act", lambda e, s=s: e.activation(out=acc[s], in_=acc[s], func=AF.Silu),
                          reads=[("acc", s)], writes=[("acc", s)])
                    sc.op("dve", lambda e, s=s, seg=seg: e.tensor_scalar(
                        out=kT[:, seg], in0=acc[s], scalar1=128.0 ** -0.5, scalar2=None, op0=ALU.mult),
                        reads=[("acc", s)], writes=[("kT", sg), ("pre", s), ("pre0", s)])
        qkkeys = [("qT", sg) for sg in range(S // SEG)] + [("kT", sg) for sg in range(S // SEG)]
        def cm():
            return mC.alloc([NCH, 128], F32)
        gi, gf, spt, Floc, Fp, gg, Gloc, Gt, tmpc = [cm() for _ in range(9)]
        colt = mC.alloc([128, 8], F32)
        rowt = mC.alloc([1, 4 * NCH], F32)
        Mb = mC.alloc([128, 2 * NCH], F32)
        gcol = mC.alloc([128, NCH], F32)
        ecol = mC.alloc([128, NCH], F32)
        srccol = mC.alloc([128, NCH], F32)
        csb = mC.alloc([128, NCH], F32)
        tcol = mC.alloc([128, NCH], F32)
        sc.dma("sp", "gld", lambda e: e.dma_start(out=gi, in_=gates_d[0].rearrange("(c p) -> c p", p=128)), writes=["gi"])
        sc.dma("sp", "gld2", lambda e: e.dma_start(out=gf, in_=gates_d[1].rearrange("(c p) -> c p", p=128)), writes=["gf"])
        sc.op("dve", lambda e: e.tensor_scalar(out=colt[:, 0:1], in0=c_small[:, 9:10], scalar1=-1.0, scalar2=None, op0=ALU.mult),
              writes=["nbf"])
        sc.op("act", lambda e: e.activation(out=spt, in_=gf, func=AF.Exp, bias=colt[0:NCH, 0:1], scale=-1.0),
              reads=["gf", "nbf"], writes=["spt"])
        sc.op("act", lambda e: e.activation(out=spt, in_=spt, func=AF.Ln, bias=1.0, scale=1.0), reads=["spt"], writes=["spt"])
        sc.op("dve", lambda e: e.tensor_tensor_scan(out=Floc, data0=ones_f[0:NCH, :], data1=spt, initial=0.0,
                                                    op0=ALU.mult, op1=ALU.add), reads=["spt"], writes=["Floc"])
        sc.op("pe", lambda e: e.matmul(ps[0][0:NCH, 0:1], Umat[0:NCH, 0:NCH], Floc[:, 127:128], start=True, stop=True),
              reads=["Floc"], writes=[P(0)])
        sc.op("act", lambda e: e.activation(out=colt[0:NCH, 1:2], in_=ps[0][0:NCH, 0:1], func=AF.Copy), reads=[P(0)], writes=["offs"])
        sc.op("dve", lambda e: e.tensor_scalar(out=Fp, in0=Floc, scalar1=colt[0:NCH, 1:2], scalar2=None, op0=ALU.add),
              reads=["Floc", "offs"], writes=["Fp"])
        sc.op("dve", lambda e: e.scalar_tensor_tensor(out=gg, in0=gi, scalar=c_small[0:NCH, 8:9], in1=Fp, op0=ALU.add, op1=ALU.add),
              reads=["gi", "Fp"], writes=["gg"])
        sc.op("dve", lambda e: e.tensor_tensor_scan(out=Gloc, data0=gg, data1=gg, initial=-1.0e30, op0=ALU.max, op1=ALU.max),
              reads=["gg"], writes=["Gloc"])
        sc.op("pe", lambda e: e.matmul(ps[1][0:1, 0:NCH], Gloc[:, 127:128], identf[0:NCH, 0:NCH], start=True, stop=True),
              reads=["Gloc"], writes=[P(1)])
        sc.op("act", lambda e: e.activation(out=rowt[:, 0:NCH], in_=ps[1][0:1, 0:NCH], func=AF.Copy), reads=[P(1)], writes=["rowmax"])
        sc.op("dve", lambda e: e.tensor_tensor_scan(out=rowt[:, NCH:2 * NCH], data0=rowt[:, 0:NCH], data1=rowt[:, 0:NCH],
                                                    initial=0.0, op0=ALU.max, op1=ALU.max), reads=["rowmax"], writes=["Mrow"])
        sc.op("dve", lambda e: e.memset(rowt[:, 2 * NCH:2 * NCH + 1], 0.0), writes=["Mp0"])
        sc.op("dve", lambda e: e.tensor_copy(out=rowt[:, 2 * NCH + 1:3 * NCH], in_=rowt[:, NCH:2 * NCH - 1]),
              reads=["Mrow", "Mp0"], writes=["Mprow"])
        sc.op("pe", lambda e: e.matmul(ps[2][:, 0:2 * NCH], ones_f[0:1, 0:128], rowt[:, NCH:3 * NCH], start=True, stop=True),
              reads=["Mrow", "Mprow"], writes=[P(2)])
        sc.op("act", lambda e: e.activation(out=Mb, in_=ps[2][:, 0:2 * NCH], func=AF.Copy), reads=[P(2)], writes=["Mb"])
        sc.op("pe", lambda e: e.matmul(ps[3][0:NCH, 0:1], rowt[:, 2 * NCH:3 * NCH], ones_f[0:1, 0:1], start=True, stop=True),
              reads=["Mprow"], writes=[P(3)])
        sc.op("act", lambda e: e.activation(out=colt[0:NCH, 2:3], in_=ps[3][0:NCH, 0:1], func=AF.Copy), reads=[P(3)], writes=["Pcol"])
        sc.op("dve", lambda e: e.tensor_scalar(out=Gt, in0=Gloc, scalar1=colt[0:NCH, 2:3], scalar2=None, op0=ALU.max),
              reads=["Gloc", "Pcol"], writes=["Gt"])
        sc.op("dve", lambda e: e.tensor_scalar(out=tmpc, in0=Gt, scalar1=-1.0, scalar2=None, op0=ALU.mult),
              reads=["Gt"], writes=["tmpc"])
        sc.dma("sp", "rowsd", lambda e: e.dma_start(out=rows_d[0].rearrange("(c p) -> c p", p=128), in_=tmpc),
               reads=["tmpc"], writes=["rows0"])
        sc.dma("sp", "negGb", lambda e: e.dma_start(out=negGb, in_=rows_d[0:1, :].broadcast_to([128, S])),
               reads=["rows0"], writes=["negGb"])
        sc.op("dve", lambda e: e.tensor_tensor(out=Fp, in0=Fp, in1=Gt, op=ALU.subtract), reads=["Fp", "Gt", "gg"], writes=["Fp"])
        sc.op("pe", lambda e: e.matmul(ps[0][:, 0:NCH], gg, identf[0:NCH, 0:NCH], start=True, stop=True),
              reads=["gg"], writes=[P(0)])
        sc.op("act", lambda e: e.activation(out=gcol, in_=ps[0][:, 0:NCH], func=AF.Copy), reads=[P(0)], writes=["gcol"])
        sc.op("pe", lambda e: e.matmul(ps[1][:, 0:NCH], Fp, identf[0:NCH, 0:NCH], start=True, stop=True),
              reads=["Fp"], writes=[P(1)])
        sc.op("act", lambda e: e.activation(out=ecol, in_=ps[1][:, 0:NCH], func=AF.Exp), reads=[P(1)], writes=["ecol"])
        sc.op("dve", lambda e: e.tensor_tensor(out=tcol, in0=gcol, in1=Mb[:, 0:NCH], op=ALU.subtract), reads=["gcol", "Mb"], writes=["tcol"])
        sc.op("act", lambda e: e.activation(out=srccol, in_=tcol, func=AF.Exp), reads=["tcol"], writes=["srccol"])
        sc.op("dve", lambda e: e.tensor_tensor(out=tcol, in0=Mb[:, NCH:2 * NCH], in1=Mb[:, 0:NCH], op=ALU.subtract),
              reads=["Mb", "srccol"], writes=["tcol"])
        sc.op("act", lambda e: e.activation(out=csb, in_=tcol, func=AF.Exp), reads=["tcol"], writes=["csb"])
        argd = [mC.alloc([128, 128], F32) for _ in range(2)]
        dect = [mC.alloc([128, 128], F32) for _ in range(2)]
        wTt = [mC.alloc([128, 128], BF16) for _ in range(2)]
        qd = [mC.alloc([128, 128], BF16) for _ in range(2)]
        ksc = [mC.alloc([128, 128], BF16) for _ in range(2)]
        hh = [mC.alloc([128, 256], F32) for _ in range(2)]
        junk = mC.alloc([128, 256], F32)
        ya = [mC.alloc([128, 256], BF16) for _ in range(2)]
        yaT = [mC.alloc([128, 2, 128], BF16) for _ in range(2)]
        sm = [mC.alloc([128, 4], F32) for _ in range(2)]
        C32 = mC.alloc([128, 264], F32)
        Cbf = mC.alloc([128, 264], BF16)
        sc.op("dve", lambda e: e.memset(C32, 0.0), writes=["C32"])
        sc.op("dve", lambda e: e.memset(Cbf, 0.0), writes=["Cbf"])
        def m_head(c):
            s = c % 2
            cs = slice(c * 128, (c + 1) * 128)
            bS = s
            sc.op("pe", lambda e, cs=cs, bS=bS: e.matmul(ps[bS][:, 0:128], kT[:, cs], qT[:, cs], start=True, stop=True),
                  reads=qkkeys, writes=[P(bS)])
            sc.op("dve", lambda e, s=s, cs=cs: e.tensor_tensor(out=argd[s], in0=negGb[:, cs], in1=maskneg, op=ALU.add),
                  reads=["negGb"], writes=[("argd", s)])
            sc.op("act", lambda e, s=s, c=c: e.activation(out=argd[s], in_=argd[s], func=AF.Exp, bias=gcol[:, c:c + 1], scale=1.0),
                  reads=[("argd", s), "gcol"], writes=[("argd", s)])
            sc.op("dve", lambda e, s=s, bS=bS: e.tensor_tensor(out=wTt[s], in0=ps[bS][:, 0:128], in1=argd[s], op=ALU.mult),
                  reads=[P(bS), ("argd", s)], writes=[("wT", s)])
            sc.op("act", lambda e, s=s, c=c, cs=cs: e.activation(out=dect[s], in_=negGb[:, cs], func=AF.Exp,
                                                                 bias=Mb[:, NCH + c:NCH + c + 1], scale=1.0),
                  reads=["negGb", "Mb"], writes=[("dec", s)])
            sc.op("dve", lambda e, s=s, cs=cs: e.tensor_tensor(out=qd[s], in0=qT[:, cs], in1=dect[s], op=ALU.mult),
                  reads=[("dec", s)] + qkkeys, writes=[("qd", s)])
            sc.op("pe", lambda e, cs=cs: e.transpose(psb[6][:, 0:128], kT[:, cs], c_id), reads=qkkeys, writes=[P(6)])
            sc.op("dve", lambda e, s=s, c=c: e.tensor_scalar(out=ksc[s], in0=psb[6][:, 0:128], scalar1=srccol[:, c:c + 1],
                                                             scalar2=None, op0=ALU.mult),
                  reads=[P(6), "srccol"], writes=[("ksc", s)])

        def m_mid(c):
            s = c % 2
            bO = 2 + s

            def mmO(e, s=s, c=c, bO=bO):
                e.matmul(ps[bO][:, 0:257], qd[s], Cbf[:, 0:257], start=True, stop=False)
                return e.matmul(ps[bO][:, 0:257], wTt[s], vext[:, c, 0:257], start=False, stop=True)
            sc.op("pe", mmO, reads=[("qd", s), ("wT", s), "Cbf"] + vkeys, writes=[P(bO)])
            sc.op("pe", lambda e, s=s, c=c: e.matmul(ps[7][:, 0:257], ksc[s], vext[:, c, 0:257], start=True, stop=True),
                  reads=[("ksc", s)] + vkeys, writes=[P(7)])
            sc.op("dve", lambda e, c=c: e.scalar_tensor_tensor(out=C32[:, 0:257], in0=C32[:, 0:257], scalar=csb[:, c:c + 1],
                                                               in1=ps[7][:, 0:257], op0=ALU.mult, op1=ALU.add),
                  reads=[P(7), "C32", "csb"], writes=["C32"])
            sc.op("act", lambda e: e.activation(out=Cbf[:, 0:257], in_=C32[:, 0:257], func=AF.Copy), reads=["C32"], writes=["Cbf"])

        def m_tail(c):
            s = c % 2
            bO, bT = 2 + s, 4 + s
            sc.op("dve", lambda e, s=s, bO=bO: e.tensor_scalar(
                out=sm[s][:, 3:4], in0=ps[bO][:, 256:257], scalar1=-1.0, scalar2=None, op0=ALU.mult),
                reads=[P(bO)], writes=[("nden", s)])
            sc.op("dve", lambda e, s=s, c=c, bO=bO: e.scalar_tensor_tensor(
                out=sm[s][:, 0:1], in0=ps[bO][:, 256:257], scalar=ecol[:, c:c + 1], in1=sm[s][:, 3:4],
                op0=ALU.max, op1=ALU.max), reads=[P(bO), "ecol", ("nden", s)], writes=[("dd", s)])
            sc.op("dve", lambda e, s=s: e.reciprocal(out=sm[s][:, 0:1], in_=sm[s][:, 0:1]), reads=[("dd", s)], writes=[("dd", s)])
            sc.op("act", lambda e, s=s, bO=bO: e.activation(out=hh[s], in_=ps[bO][:, 0:256], func=AF.Copy, scale=sm[s][:, 0:1]),
                  reads=[P(bO), ("dd", s)], writes=[("hh", s)])
            sc.op("act", lambda e, s=s: e.activation(out=junk, in_=hh[s], func=AF.Square, accum_out=sm[s][:, 1:2]),
                  reads=[("hh", s)], writes=[("ss", s), "junk"])
            sc.op("act", lambda e, s=s: e.activation(out=sm[s][:, 2:3], in_=sm[s][:, 1:2], func=AF.Ln, bias=c_eps[:, 0:1], scale=1.0 / 256),
                  reads=[("ss", s)], writes=[("rs", s)])
            sc.op("act", lambda e, s=s: e.activation(out=sm[s][:, 2:3], in_=sm[s][:, 2:3], func=AF.Exp, scale=-0.5),
                  reads=[("rs", s)], writes=[("rs", s)])
            sc.op("dve", lambda e, s=s: e.scalar_tensor_tensor(out=ya[s], in0=hh[s], scalar=sm[s][:, 2:3], in1=gml_b,
                                                               op0=ALU.mult, op1=ALU.mult),
                  reads=[("hh", s), ("rs", s)], writes=[("ya", s)])

            def trY(e, s=s, bT=bT):
                e.transpose(psb[bT][:, 0:128], ya[s][:, 0:128], c_id)
                return e.transpose(psb[bT][:, 128:256], ya[s][:, 128:256], c_id)
            sc.op("pe", trY, reads=[("ya", s)], writes=[P(bT)])
            sc.op("act", lambda e, s=s, bT=bT: e.activation(out=yaT[s].rearrange("p a b -> p (a b)"), in_=psb[bT][:, 0:256], func=AF.Copy),
                  reads=[P(bT)], writes=[("yaT", s)])
            for j in range(2):
                piece, dst = yloc_ap(j, c * 128, 128)
                key = ("yloc", piece, c)
                ywrites[piece].append(key)
                sc.dma("sp", ("yaT", s), lambda e, s=s, j=j, dst=dst: e.dma_start(out=dst, in_=yaT[s][:, j, :]),
                       reads=[("yaT", s)], writes=[key])

        c_eps = mC.alloc([128, 1], F32)
        sc.op("dve", lambda e: e.memset(c_eps, EPS), writes=["c_eps"])
        m_head(0)
        for c in range(NCH + 1):
            if c + 1 < NCH:
                m_head(c + 1)
            if c < NCH:
                m_mid(c)
            if c >= 1:
                m_tail(c - 1)
        sc.barrier()
        mB.reset()
        mC.reset()
        if stop_after == "mlstm":
            return finish(xres)

        def exch2_piece(p, keys):
            sc.dma("pool", ("cc", p % 16), lambda e, p=p: e.collective_compute(
                "AllGather", ALU.bypass, replica_groups=[[0, 1, 2, 3], [4, 5, 6, 7]],
                ins=[yloc[p]], outs=[ex2mid[p]]), reads=keys, writes=[("ex2mid", p)], inc=1)
            sc.dma("pool", ("cc", p % 16), lambda e, p=p: e.collective_compute(
                "AllGather", ALU.bypass, replica_groups=[[0, 4], [1, 5], [2, 6], [3, 7]],
                ins=[ex2mid[p]], outs=[ex2out[p * 8 * cfg.PR2:(p + 1) * 8 * cfg.PR2, :]]),
                reads=[("ex2mid", p)], writes=[("ex2out", p)], inc=1)

        for p in range(2 * NPF):
            exch2_piece(p, [])
        sqT = [mB.alloc([128, S], BF16), None]
        skT = [mB.alloc([128, S], BF16), None]
        sqT[1] = mC.alloc([128, S], BF16)
        skT[1] = mC.alloc([128, S], BF16)
        vsh = [mC.alloc([128, NCH, 128], BF16) for _ in range(2)]
        for h in range(2):
            sc.dma("sp", ("sbl", 0), lambda e, h=h: e.dma_start(out=sqT[h], in_=sqk[2 * h]), writes=[("sq", h)])
            sc.dma("sp", ("sbl", 1), lambda e, h=h: e.dma_start(out=skT[h], in_=sqk[2 * h + 1]), writes=[("sk", h)])
            for c0 in range(0, NCH, 16):
                n = min(16, NCH - c0)
                sc.dma("sp", ("sbl", 2), lambda e, h=h, c0=c0, n=n: e.dma_start(
                    out=vsh[h][:, c0:c0 + n, :], in_=vs_d[h][c0:c0 + n].rearrange("c p d -> p c d")),
                    writes=[("vsh", h, c0)])
        vshkeys = [[("vsh", h, c0) for c0 in range(0, NCH, 16)] for h in range(2)]
        NSL = 4
        spt_ = [mC.alloc([128, 512], F32) for _ in range(NSL)]
        sptb = [mC.alloc([128, 512], BF16) for _ in range(NSL)]
        tt_ = [mC.alloc([128, 512], F32) for _ in range(NSL)]
        Wt_ = [mC.alloc([128, 512], BF16) for _ in range(NSL)]
        ost = [mC.alloc([128, 512], BF16) for _ in range(2)]
        SCL = 128.0 ** -0.5
        units = []
        for qt in range(NQT):
            nkb = 4 * qt + 4
            for kb in range(nkb - 1, -1, -1):
                for h in range(2):
                    units.append((qt, h, kb, kb == nkb - 1, kb == 0, (kb - 4 * qt) if kb >= 4 * qt else None))
        NU = len(units)
        sb_done = {}

        def S1(u):
            qt, h, kb, first, last, dj = units[u]
            zb = u % 3
            sc.op("pe", lambda e, h=h, kb=kb, qt=qt, zb=zb: e.matmul(
                ps[zb][:, :], skT[h][:, kb * 128:(kb + 1) * 128], sqT[h][:, qt * 512:(qt + 1) * 512], start=True, stop=True),
                reads=[("sq", h), ("sk", h)], writes=[P(zb)])

        def S2(u):
            qt, h, kb, first, last, dj = units[u]
            zb, sl = u % 3, u % NSL
            sc.op("act", lambda e, zb=zb, sl=sl: e.activation(out=spt_[sl], in_=ps[zb][:, :], func=AF.Exp, scale=SCL),
                  reads=[P(zb)], writes=[("spt", sl)])
            sc.op("act", lambda e, sl=sl: e.activation(out=sptb[sl], in_=spt_[sl], func=AF.Ln, bias=1.0, scale=1.0),
                  reads=[("spt", sl)], writes=[("sptb", sl)])
            if dj is not None:
                sc.op("dve", lambda e, sl=sl, dj=dj: e.tensor_tensor(out=sptb[sl], in0=sptb[sl], in1=dmask[dj], op=ALU.mult),
                      reads=[("sptb", sl)], writes=[("sptb", sl)])

        def S3(u):
            qt, h, kb, first, last, dj = units[u]
            zb, sl = u % 3, u % NSL
            sc.op("dve", lambda e, zb=zb, sl=sl: e.scalar_tensor_tensor(
                out=tt_[sl], in0=ps[zb][:, :], scalar=SCL, in1=sptb[sl], op0=ALU.mult, op1=ALU.subtract),
                reads=[P(zb), ("sptb", sl)], writes=[("tt", sl)])
            sc.op("pe", lambda e, h=h, sl=sl, first=first: e.matmul(ps[4 + h][:, :], triR, sptb[sl], start=first, stop=False),
                  reads=[("sptb", sl)], writes=[P(4 + h)])

        def S4(u):
            qt, h, kb, first, last, dj = units[u]
            sl = u % NSL
            sc.op("dve", lambda e, h=h, sl=sl: e.tensor_tensor(out=tt_[sl], in0=ps[4 + h][:, :], in1=tt_[sl], op=ALU.add),
                  reads=[P(4 + h), ("tt", sl)], writes=[("tt", sl)])
            sc.op("act", lambda e, sl=sl: e.activation(out=Wt_[sl], in_=tt_[sl], func=AF.Exp), reads=[("tt", sl)], writes=[("W", sl)])
            if dj is not None:
                sc.op("dve", lambda e, sl=sl, dj=dj: e.tensor_tensor(out=Wt_[sl], in0=Wt_[sl], in1=dmask[dj], op=ALU.mult),
                      reads=[("W", sl)], writes=[("W", sl)])

        def S5(u):
            qt, h, kb, first, last, dj = units[u]
            sl = u % NSL

            def mm(e, h=h, sl=sl, kb=kb, first=first, last=last):
                e.matmul(ps[4 + h][:, :], restR, sptb[sl], start=False, stop=last)
                return e.matmul(ps[6 + h][:, :], vsh[h][:, kb, :], Wt_[sl], start=first, stop=last)
            sc.op("pe", mm, reads=[("sptb", sl), ("W", sl), ("tt", sl)] + vshkeys[h], writes=[P(4 + h), P(6 + h)])
            if last:
                sc.op("act", lambda e, h=h: e.activation(out=ost[h], in_=ps[6 + h][:, :], func=AF.Copy),
                      reads=[P(6 + h)], writes=[("ost", h)])
                piece, dst = yloc_ap(2 + h, qt * 512, 512)
                key = ("yloc", piece, qt)
                ywrites[piece].append(key)
                sc.dma("sp", ("ost", h), lambda e, h=h, dst=dst: e.dma_start(out=dst, in_=ost[h]),
                       reads=[("ost", h)], writes=[key])
                tok_end = (qt + 1) * 512
                if tok_end % (ND2 * T) == 0:
                    exch2_piece(piece, list(ywrites[piece]))

        for i in range(-2, NU):
            if 0 <= i + 2 < NU:
                S1(i + 2)
                S2(i + 2)
            if 0 <= i + 1 < NU:
                S3(i + 1)
                S4(i + 1)
            if 0 <= i < NU:
                S5(i)
        sc.barrier()
        mB.reset()
        mC.reset()
        if stop_after == "sb":
            return finish(xres)

        pid_cache = {}
        yaT_ = mB.alloc([128, 16, T], BF16)
        ybT_ = mB.alloc([128, 16, T], BF16) if 2 * 16 * T * 2 <= HB else mC.alloc([128, 16, T], BF16)
        mergedT = mC.alloc([128, KC, T], BF16)
        FCR = NPF * 8 * cfg.PR2
        for fc in range(4):
            def ld(e, fc=fc):
                if "E" not in pid_cache:
                    pid_ = nc.partition_id()
                    if ND2 == 1:
                        pid_cache["E"] = pid_ * (8 * cfg.PR2)
                    else:
                        pid_cache["E"] = (pid_ // ND2) * (8 * cfg.PR2) + (pid_ % ND2) * 128
                E = pid_cache["E"]
                A = ex2out[bass.ds(E, 4 * FCR), :].rearrange("(fc x r f) t -> fc x r f t", fc=4, x=NPF, r=8, f=cfg.PR2)
                return e.dma_start(out=ex2stage[fc], in_=A[fc, 0, :, 0:128, :])
            sc.dma("pool", ("yst", fc), ld, writes=[("ystage", fc)])
        for r in range(NCORES):
            for j in range(2):
                for fc, dstt in ((j, yaT_), (2 + j, ybT_)):
                    sc.dma("sp", ("yld", (2 * r + j) % 4), lambda e, fc=fc, dstt=dstt, r=r, j=j: e.dma_start(
                        out=dstt[:, 2 * r + j, :], in_=ex2stage[fc][r]),
                        reads=[("ystage", fc)], writes=[("yin", fc // 2, 2 * r + j)])
        gtile = [mC.alloc([128, T], BF16) for _ in range(4)]
        for kc in range(16):
            s = kc % 2
            sc.dma("sp", ("gt", s), lambda e, kc=kc, s=s: e.dma_start(out=gtile[s], in_=gts[kc]), writes=[("gt", s)])
            sc.op("dve", lambda e, kc=kc, s=s: e.tensor_tensor(out=yaT_[:, kc, :], in0=yaT_[:, kc, :], in1=gtile[s], op=ALU.mult),
                  reads=[("gt", s), ("yin", 0, kc)], writes=[("yin", 0, kc)])
        wpt = [[mC.alloc([128, 16, 128], BF16) for _ in range(2)] for _ in range(2)]
        m1 = [mC.alloc([128, 512], F32) for _ in range(2)]
        m2 = [mC.alloc([128, 512], F32) for _ in range(2)]
        yakeys = [("yin", 0, kc) for kc in range(16)]
        ybkeys = [("yin", 1, kc) for kc in range(16)]
        cnt = 0
        for o in range(KC):
            s = o % 2
            sc.dma("pool", ("wpa", s), lambda e, o=o, s=s: e.dma_start(
                out=wpt[0][s].rearrange("p a b -> p (a b)"), in_=wpa[o], max_dma_last_dim=8192), writes=[("wpa", s)])
            sc.dma("pool", ("wpb", s), lambda e, o=o, s=s: e.dma_start(
                out=wpt[1][s].rearrange("p a b -> p (a b)"), in_=wpb[o], max_dma_last_dim=8192), writes=[("wpb", s)])
            sc.dma("sp", ("gt", 2), lambda e, o=o: e.dma_start(out=gtile[2], in_=gts[16 + o]), writes=[("gt", 2)])
            sc.dma("sp", ("gt", 3), lambda e, o=o: e.dma_start(out=gtile[3], in_=gts[16 + KC + o]), writes=[("gt", 3)])
            for th in range(TH):
                sl = slice(th * 512, (th + 1) * 512)
                b = (cnt % 2) * 2
                q = cnt % 2
                cnt += 1

                def mm(e, s=s, sl=sl, b=b):
                    for kc in range(16):
                        e.matmul(ps[b][:, :], wpt[0][s][:, kc, :], yaT_[:, kc, sl], start=(kc == 0), stop=(kc == 15))
                    for kc in range(16):
                        r_ = e.matmul(ps[b + 1][:, :], wpt[1][s][:, kc, :], ybT_[:, kc, sl], start=(kc == 0), stop=(kc == 15))
                    return r_
                sc.op("pe", mm, reads=[("wpa", s), ("wpb", s)] + yakeys + ybkeys, writes=[P(b), P(b + 1)])
                sc.op("dve", lambda e, b=b, q=q, sl=sl: e.tensor_tensor(out=m1[q], in0=ps[b][:, :], in1=gtile[2][:, sl], op=ALU.mult),
                      reads=[P(b), ("gt", 2)], writes=[("m1", q)])
                sc.op("dve", lambda e, b=b, q=q, sl=sl: e.tensor_tensor(out=m2[q], in0=ps[b + 1][:, :], in1=gtile[3][:, sl], op=ALU.mult),
                      reads=[P(b + 1), ("gt", 3)], writes=[("m2", q)])
                sc.op("dve", lambda e, q=q, o=o, sl=sl: e.tensor_tensor(out=mergedT[:, o, sl], in0=m1[q], in1=m2[q], op=ALU.add),
                      reads=[("m1", q), ("m2", q)], writes=[("mg", o, th)])
        sc.barrier()
        mB.reset()
        wot = [mB.alloc([128, KC, 128], BF16) for _ in range(2)]
        xo = [mB.alloc([128, 512], F32) for _ in range(4)]
        cnt = 0
        for o in range(KC):
            s = o % 2
            sc.dma("pool", ("wot", s), lambda e, o=o, s=s: e.dma_start(
                out=wot[s].rearrange("p a b -> p (a b)"), in_=wo[o], max_dma_last_dim=8192), writes=[("wot", s)])
            for th in range(TH):
                sl = slice(th * 512, (th + 1) * 512)
                b = 4 + cnt % 4
                xs_ = cnt % 4
                cnt += 1

                def mm(e, s=s, sl=sl, b=b):
                    for kc in range(KC):
                        r_ = e.matmul(ps[b][:, :], wot[s][:, kc, :], mergedT[:, kc, sl], start=(kc == 0), stop=(kc == KC - 1))
                    return r_
                sc.op("pe", mm, reads=[("wot", s)], writes=[P(b)])
                sc.dma("sp", ("xo", xs_), lambda e, o=o, sl=sl, xs_=xs_: e.dma_start(out=xo[xs_], in_=xres[o][:, sl]),
                       writes=[("xo", xs_)])
                sc.op("dve", lambda e, b=b, xs_=xs_: e.tensor_tensor(out=xo[xs_], in0=ps[b][:, :], in1=xo[xs_], op=ALU.add),
                      reads=[P(b), ("xo", xs_)], writes=[("xo", xs_)])
                sc.dma("sp", ("xo", xs_), lambda e, o=o, sl=sl, xs_=xs_: e.dma_start(out=xres[o][:, sl], in_=xo[xs_]),
                       reads=[("xo", xs_)])
        sc.barrier()
        mB.reset()
        mC.reset()
        if stop_after == "merge":
            return finish(xres)

        hT = mB.alloc([128, KC, T], BF16)
        rmsnorm(xres, 2, hT)
        ffn(1, xres, hT)
        rmsnorm(xres, 3, None, outT)
        return finish(None)


def make_consts():
    j = np.arange(128)[:, None]
    s = np.arange(128)[None, :]
    ones = np.ones((128, 128), np.float32)
    triN = np.where(j > s, -1.0, 0.0).astype(np.float32)
    restN = np.where(j <= s, -1.0, 0.0).astype(np.float32)
    maskneg = np.where(j <= s, 0.0, NEG).astype(np.float32)
    U = np.where(j < s, 1.0, 0.0).astype(np.float32)
    identf = np.eye(128, dtype=np.float32)
    t = np.arange(512)[None, :]
    masks = [np.where(128 * dj + j < t, 1.0, 0.0).astype(np.float32) for dj in range(4)]
    cf = np.concatenate([ones, triN, restN, maskneg, U, identf] + masks, axis=1)
    return np.ascontiguousarray(cf), np.eye(128, dtype=np.float32).astype(ml_dtypes.bfloat16)


def _lhsT_layout(w, kc, oc):
    K, O = w.shape
    return np.ascontiguousarray(w.reshape(kc, 128, oc, 128).transpose(2, 1, 0, 3).reshape(oc, 128, kc * 128))


def prep_inputs(cfg, inp):
    D, T, KC, FC = cfg.D, cfg.T, cfg.KC, cfg.FC
    f32 = np.float32
    x = np.asarray(inp["x"], f32)[0]
    shared = {}
    for l, sfx in ((0, "a"), (1, "b")):
        n = "ffn1" if l == 0 else "ffn2"
        shared["w1" + sfx] = _lhsT_layout(np.asarray(inp["w1_" + n], f32)[0], KC, FC)
        shared["w3" + sfx] = _lhsT_layout(np.asarray(inp["w3_" + n], f32)[0], KC, FC)
        shared["w2" + sfx] = _lhsT_layout(np.asarray(inp["w2_" + n], f32)[0], FC, KC)
    gs = [np.asarray(inp[k], f32).reshape(-1) for k in ("g_ffn1", "g_mix", "g_ffn2", "g_final")]
    shared["gv"] = np.ascontiguousarray(np.concatenate([g.reshape(KC, 128).T for g in gs], axis=1))
    w_in = np.asarray(inp["w_in"], f32)[0]
    o_mq, o_mk, o_mv, o_mo, o_mi, o_mf = 0, 1024, 2048, 4096, 6144, 6152
    o_sq, o_sk, o_sv = 6160, 8208, 10256
    o_ga = 12304
    o_gb = o_ga + D
    gate_cols = np.concatenate([np.arange(o_mo, o_mo + 2048), np.arange(o_ga, o_ga + D), np.arange(o_gb, o_gb + D)])
    wgfull = w_in[:, gate_cols]
    shared["wG"] = _lhsT_layout(wgfull, KC, 16 + 2 * KC)
    shared["wpa"] = _lhsT_layout(np.asarray(inp["w_proj_a"], f32)[0], 16, KC)
    shared["wpb"] = _lhsT_layout(np.asarray(inp["w_proj_b"], f32)[0], 16, KC)
    shared["wo"] = _lhsT_layout(np.asarray(inp["w_out"], f32)[0], KC, KC)
    cf, idb = make_consts()
    shared["cf32"] = cf
    shared["identb"] = idb
    conv = np.asarray(inp["conv_qk"], f32)[0]
    b_i = np.asarray(inp["b_igate"], f32)[0]
    b_f = np.asarray(inp["b_fgate"], f32)[0]
    gml = np.asarray(inp["g_mlstm_out"], f32)[0]

    def pk(wcols):
        n = wcols.shape[1]
        return np.ascontiguousarray(wcols.reshape(KC, 128, n).transpose(1, 0, 2).reshape(128, KC * n))

    maps = []
    for c in range(NCORES):
        m = dict(shared)
        m["xT"] = np.ascontiguousarray(x[c * T:(c + 1) * T, :].T.reshape(KC, 128, T))
        colsA = np.concatenate([
            np.arange(o_mq + c * 128, o_mq + (c + 1) * 128), np.arange(o_mk + c * 128, o_mk + (c + 1) * 128),
            np.arange(o_sq + (2 * c) * 128, o_sq + (2 * c + 1) * 128), np.arange(o_sk + (2 * c) * 128, o_sk + (2 * c + 1) * 128),
            np.arange(o_sq + (2 * c + 1) * 128, o_sq + (2 * c + 2) * 128), np.arange(o_sk + (2 * c + 1) * 128, o_sk + (2 * c + 2) * 128)])
        m["wA"] = pk(w_in[:, colsA])
        wg = np.zeros((D, 64), f32)
        wg[:, 0] = w_in[:, o_mi + c]
        wg[:, 32] = w_in[:, o_mf + c]
        m["wGt"] = pk(wg)
        colsV = np.concatenate([np.arange(o_mv + c * 256, o_mv + (c + 1) * 256),
                                np.arange(o_sv + (2 * c) * 128, o_sv + (2 * c + 2) * 128)])
        m["wV"] = pk(w_in[:, colsV])
        sm = np.zeros((128, 16 + 256), f32)
        sm[:, 0:4] = conv[:, c * 128:(c + 1) * 128].T
        sm[:, 4:8] = conv[:, 1024 + c * 128:1024 + (c + 1) * 128].T
        sm[:, 8] = b_i[c]
        sm[:, 9] = b_f[c]
        sm[:, 16:272] = gml[c * 256:(c + 1) * 256][None, :]
        m["small"] = sm
        maps.append(m)
    return maps


_PROG_CACHE = {}


def run(cfg, inp, stop_after=None, trace=False):
    key = (cfg.D, cfg.T, cfg.DFF, stop_after)
    if key not in _PROG_CACHE:
        _PROG_CACHE[key] = build_program(cfg, stop_after)
    nc = _PROG_CACHE[key]
    maps = prep_inputs(cfg, inp)
    res = run_bass_kernel_spmd(nc, maps, core_ids=list(range(NCORES)), trace=trace)
    name = "dbg" if stop_after else "outT"
    outs = [r[name].reshape(cfg.D, cfg.T).T for r in res.results]
    full = np.concatenate(outs, axis=0)[None].astype(np.float32)
    if stop_after:
        ya = np.zeros((cfg.S, 2048), np.float32); yb = np.zeros((cfg.S, 2048), np.float32)
        for c, r in enumerate(res.results):
            y = r["dbg2"].astype(np.float32)
            for fc in range(4):
                for dc in range(NCORES):
                    blk = y[fc * cfg.NPF + dc // cfg.ND2][(dc % cfg.ND2) * 128:(dc % cfg.ND2) * 128 + 128, :]
                    if fc < 2:
                        ya[dc * cfg.T:(dc + 1) * cfg.T, c * 256 + fc * 128:c * 256 + (fc + 1) * 128] = blk.T
                    else:
                        yb[dc * cfg.T:(dc + 1) * cfg.T, (2 * c + fc - 2) * 128:(2 * c + fc - 1) * 128] = blk.T
        res.ya, res.yb = ya, yb
    return full, res


def kernel(**inputs):
    out, _ = run(Cfg(), inputs)
    return out
```

```python
from contextlib import ExitStack
import numpy as np
import ml_dtypes
import concourse.bass as bass
import concourse.mybir as mybir
from concourse.bass_utils import run_bass_kernel_spmd

F32 = mybir.dt.float32
F32R = mybir.dt.float32r
SB_CUMSUM_DT = F32R
BF16 = mybir.dt.bfloat16
AF = mybir.ActivationFunctionType
ALU = mybir.AluOpType
NCORES = 8
EPS = 1e-6
NEG = -30000.0
NCONST = 6 * 128 + 4 * 512


class Cfg:
    def __init__(self, D=4096, T=1024, DFF=11008, G=4, PIECE=512 * 1024):
        self.D, self.T, self.DFF, self.G = D, T, DFF, G
        self.S = NCORES * T
        self.KC = D // 128
        self.FC = DFF // 128
        self.TH = T // 512
        self.NCH = self.S // 128
        self.NQT = self.S // 512
        base, rem = divmod(self.FC, G)
        self.groups = []
        f0 = 0
        for g in range(G):
            n = base + (1 if g < rem else 0)
            self.groups.append((f0, n))
            f0 += n
        self.PR1 = PIECE // (2 * T)
        self.NP1 = D // self.PR1
        self.KPP = self.PR1 // 128
        self.ND2 = PIECE // (128 * T * 2)
        self.NPF = NCORES // self.ND2
        self.NP2 = 4 * self.NPF
        self.PR2 = self.ND2 * 128
        assert self.PR1 % 128 == 0 and D % self.PR1 == 0 and self.ND2 >= 1


class Sched:
    def __init__(self, nc, enter):
        self.nc = nc
        self.enter = enter
        self.streams = {k: [] for k in ("pe", "act", "dve", "pool", "sp")}
        self.sems = {}
        self.count = {}
        for k in ("pe", "act", "dve", "pool"):
            self.sems[k] = enter(nc.semaphore("s_" + k))
            self.count[k] = 0
        self.waited = {k: {} for k in self.streams}
        self.lastw = {}
        self.readers = {}

    def _need(self, e, tok, waits):
        if tok is None:
            return
        k, v = tok
        if e == "pe" and k == "pe":
            return
        if k not in ("pe", "act", "dve", "pool"):
            v = self.count[k]
        if self.waited[e].get(k, 0) >= v:
            return
        self.waited[e][k] = v
        waits[k] = max(waits.get(k, 0), v)

    def _deps(self, e, reads, writes):
        waits = {}
        for r in reads:
            self._need(e, self.lastw.get(r), waits)
        for r in writes:
            self._need(e, self.lastw.get(r), waits)
            for k, v in self.readers.get(r, {}).items():
                self._need(e, (k, v), waits)
        return waits

    def _commit(self, tok, reads, writes):
        for r in reads:
            d = self.readers.setdefault(r, {})
            d[tok[0]] = max(d.get(tok[0], 0), tok[1])
        for r in writes:
            self.lastw[r] = tok
            self.readers[r] = {}

    def op(self, e, fn, reads=(), writes=()):
        waits = self._deps(e, reads, writes)
        self.count[e] += 1
        tok = (e, self.count[e])
        self.streams[e].append((waits, fn, e, 1))
        self._commit(tok, reads, writes)

    def dma(self, q, key, fn, reads=(), writes=(), inc=16):
        if key not in self.sems:
            self.sems[key] = self.enter(self.nc.semaphore("d%d" % len(self.sems)))
            self.count[key] = 0
        waits = self._deps(q, reads, writes)
        self.count[key] += inc
        tok = (key, self.count[key])
        self.streams[q].append((waits, fn, key, inc))
        self._commit(tok, reads, writes)

    def barrier(self):
        for e in self.streams:
            waits = {}
            for k, v in self.count.items():
                if v > 0:
                    self._need(e, (k, v), waits)
            if waits:
                self.streams[e].append((waits, None, None, 0))
        self.lastw = {}
        self.readers = {}

    def emit(self, block):
        def make(name):
            def body(engine):
                for waits, fn, semkey, inc in self.streams[name]:
                    for k, v in waits.items():
                        engine.wait_ge(self.sems[k], v)
                    if fn is None:
                        continue
                    inst = fn(engine)
                    inst.then_inc(self.sems[semkey], inc)
            return body
        block.tensor(make("pe"))
        block.scalar(make("act"))
        block.vector(make("dve"))
        block.gpsimd(make("pool"))
        block.sync(make("sp"))


class Mem:
    def __init__(self, big, start, nbytes):
        self.big, self.start, self.cap, self.off = big, start, nbytes, 0

    def alloc(self, shape, dtype):
        esz = 4 if dtype == F32 else 2
        n = int(np.prod(shape[1:]))
        nb = (n * esz + 63) // 64 * 64
        assert self.off + nb <= self.cap, ("SBUF overflow", self.off, nb, self.cap)
        a = (self.start + self.off) // 2
        ap = self.big[0:shape[0], a:a + n * esz // 2]
        self.off += nb
        if dtype == F32:
            ap = ap.bitcast(F32)
        if len(shape) == 3:
            ap = ap.rearrange("p (a b) -> p a b", a=shape[1])
        return ap

    def reset(self):
        self.off = 0


def build_program(cfg, stop_after=None):
    nc = bass.Bass("TRN2", target_bir_lowering=False)
    D, T, S, KC, FC, TH, NCH, NQT = cfg.D, cfg.T, cfg.S, cfg.KC, cfg.FC, cfg.TH, cfg.NCH, cfg.NQT
    NPF, ND2 = cfg.NPF, cfg.ND2

    def din(name, shape, dt=F32):
        return nc.dram_tensor(name, list(shape), dt, kind="ExternalInput").ap()

    def dscr(name, shape, dt):
        return nc.dram_tensor(name, list(shape), dt).ap()

    xT = din("xT", [KC, 128, T])
    gv = din("gv", [128, 4 * KC])
    w1 = [din("w1a", [FC, 128, KC * 128]), din("w1b", [FC, 128, KC * 128])]
    w3 = [din("w3a", [FC, 128, KC * 128]), din("w3b", [FC, 128, KC * 128])]
    w2 = [din("w2a", [KC, 128, FC * 128]), din("w2b", [KC, 128, FC * 128])]
    wA = din("wA", [128, KC * 768])
    wGt = din("wGt", [128, KC * 64])
    wV = din("wV", [128, KC * 512])
    NGC = 16 + 2 * KC
    wG = din("wG", [NGC, 128, KC * 128])
    wpa = din("wpa", [KC, 128, 16 * 128])
    wpb = din("wpb", [KC, 128, 16 * 128])
    wo = din("wo", [KC, 128, KC * 128])
    small = din("small", [128, 16 + 256])
    cf32 = din("cf32", [128, NCONST])
    identb = din("identb", [128, 128], BF16)
    outT = nc.dram_tensor("outT", [KC, 128, T], F32, kind="ExternalOutput").ap()
    dbg = nc.dram_tensor("dbg", [KC, 128, T], F32, kind="ExternalOutput").ap() if stop_after else None
    dbg2 = nc.dram_tensor("dbg2", [cfg.NP2, cfg.PR2, T], BF16, kind="ExternalOutput").ap() if stop_after else None

    xres = dscr("xres", [KC, 128, T], F32)
    gts = dscr("gts", [NGC, 128, T], BF16)
    h2loc = dscr("h2loc", [cfg.NP1, cfg.PR1, T], BF16)
    ex1mid = dscr("ex1mid", [cfg.NP1, 4 * cfg.PR1, T], BF16)
    ex1out = dscr("ex1out", [cfg.NP1, 8 * cfg.PR1, T], BF16)
    qkpre = dscr("qkpre", [2, 128, S], F32)
    sqk = dscr("sqk", [4, 128, S], BF16)
    gates_d = dscr("gates_d", [2, S], F32)
    vm_d = dscr("vm_d", [NCH, 128, 256], BF16)
    vs_d = dscr("vs_d", [2, NCH, 128, 128], BF16)
    rows_d = dscr("rows_d", [2, S], F32)
    yloc = dscr("yloc", [cfg.NP2, cfg.PR2, T], BF16)
    ex2mid = dscr("ex2mid", [cfg.NP2, 4 * cfg.PR2, T], BF16)
    ex2out = dscr("ex2out", [5 * cfg.NPF * 8 * cfg.PR2, T], BF16)
    ex2stage = dscr("ex2stage", [4, 8, 128, T], BF16)

    with ExitStack() as es:
        enter = es.enter_context
        CB = 16 * 1024
        HB = max(KC * T * 2, 64 * 1024)
        TOTAL = 200 * 1024
        big = enter(nc.sbuf_tensor("big", [128, TOTAL // 2], BF16))
        psall = enter(nc.psum_tensor("psall", [128, 8 * 512], F32))
        ps = [psall[:, i * 512:(i + 1) * 512] for i in range(8)]
        psb = [p.bitcast(BF16) for p in ps]
        sc = Sched(nc, enter)
        mA = Mem(big, 0, CB)
        mB = Mem(big, CB, HB)
        mC = Mem(big, CB + HB, TOTAL - CB - HB)
        P = lambda i: ("ps", i)

        c_gv = mA.alloc([128, 4 * KC], F32)
        c_small = mA.alloc([128, 16 + 256], F32)
        c_f32 = mA.alloc([128, NCONST], F32)
        c_id = mA.alloc([128, 128], BF16)
        sc.dma("sp", "c", lambda e: e.dma_start(out=c_gv, in_=gv), writes=["c_gv"])
        sc.dma("sp", "c", lambda e: e.dma_start(out=c_small, in_=small), writes=["c_small"])
        sc.dma("sp", "c", lambda e: e.dma_start(out=c_f32, in_=cf32), writes=["c_f32"])
        sc.dma("sp", "c", lambda e: e.dma_start(out=c_id, in_=identb), writes=["c_id"])
        sc.barrier()
        ones_f = c_f32[:, 0:128]
        triN = c_f32[:, 128:256]
        restN = c_f32[:, 256:384]
        maskneg = c_f32[:, 384:512]
        Umat = c_f32[:, 512:640]
        identf = c_f32[:, 640:768]
        dmask = [c_f32[:, 768 + 512 * j:768 + 512 * (j + 1)] for j in range(4)]
        gml_b = c_small[:, 16:272]
        c_tr = mA.alloc([128, 256], BF16)
        sc.op("dve", lambda e: e.tensor_copy(out=c_tr, in_=c_f32[:, 128:384]), writes=["c_tr"])
        sc.barrier()
        triR = c_tr[:, 0:128]
        restR = c_tr[:, 128:256]
        ones_b = mA.alloc([128, 128], BF16)
        sc.op("dve", lambda e: e.tensor_copy(out=ones_b, in_=c_f32[:, 0:128]), writes=["ones_b"])
        sc.barrier()

        def finish(dump=None):
            if dump is not None:
                for c in range(KC):
                    sc.dma("sp", "dump", lambda e, c=c: e.dma_start(out=dbg[c], in_=dump[c]))
                for p in range(cfg.NP2):
                    sc.dma("sp", "dump", lambda e, p=p: e.dma_start(out=dbg2[p], in_=yloc[p]))
            sc.barrier()
            with nc.Block() as block:
                sc.emit(block)
            return nc

        def rmsnorm(src, gidx, hT, dst_dram=None):
            xs = [mC.alloc([128, T], F32) for _ in range(6)]
            sq = [mC.alloc([128, T], BF16) for _ in range(2)]
            rstd = mC.alloc([128, T], F32)
            outs = [mC.alloc([128, T], F32) for _ in range(2)] if hT is None else None
            for c in range(KC):
                s = c % 6
                sc.dma("sp", ("xs", s), lambda e, c=c, s=s: e.dma_start(out=xs[s], in_=src[c]),
                       writes=[("xs", s)])
                q = c % 2
                sc.op("act", lambda e, s=s, q=q: e.activation(out=sq[q], in_=xs[s], func=AF.Square),
                      reads=[("xs", s)], writes=[("sq", q)])
                for th in range(TH):
                    sc.op("pe", lambda e, q=q, th=th, c=c: e.matmul(
                        ps[4 + th][:, :], ones_b, sq[q][:, th * 512:(th + 1) * 512],
                        start=(c == 0), stop=(c == KC - 1)),
                        reads=[("sq", q)], writes=[P(4 + th)])
            for th in range(TH):
                sl = slice(th * 512, (th + 1) * 512)
                sc.op("act", lambda e, th=th, sl=sl: e.activation(
                    out=rstd[:, sl], in_=ps[4 + th][:, :], func=AF.Sqrt, bias=EPS, scale=1.0 / D),
                    reads=[P(4 + th)], writes=[("rstd", th)])
                sc.op("dve", lambda e, sl=sl: e.reciprocal(out=rstd[:, sl], in_=rstd[:, sl]),
                      reads=[("rstd", th)], writes=[("rstd", th)])
            for c in range(KC):
                s = c % 6
                sc.dma("sp", ("xs", s), lambda e, c=c, s=s: e.dma_start(out=xs[s], in_=src[c]),
                       writes=[("xs", s)])
                gcol = c_gv[:, gidx * KC + c:gidx * KC + c + 1]
                rd = [("xs", s)] + [("rstd", th) for th in range(TH)]
                if hT is not None:
                    sc.op("dve", lambda e, c=c, s=s, gcol=gcol: e.scalar_tensor_tensor(
                        out=hT[:, c, :], in0=xs[s], scalar=gcol, in1=rstd, op0=ALU.mult, op1=ALU.mult),
                        reads=rd, writes=[("hT", c)])
                else:
                    o = c % 2
                    sc.op("dve", lambda e, o=o, s=s, gcol=gcol: e.scalar_tensor_tensor(
                        out=outs[o], in0=xs[s], scalar=gcol, in1=rstd, op0=ALU.mult, op1=ALU.mult),
                        reads=rd, writes=[("nout", o)])
                    sc.dma("sp", ("nout", o), lambda e, o=o, c=c: e.dma_start(out=dst_dram[c], in_=outs[o]),
                           reads=[("nout", o)])
            sc.barrier()
            mC.reset()

        def ffn(l, src, hT):
            ngmax = max(n for _, n in cfg.groups)
            aT = mC.alloc([128, ngmax, T], BF16)
            w1t = [mC.alloc([128, KC, 128], BF16) for _ in range(2)]
            w3t = [mC.alloc([128, KC, 128], BF16) for _ in range(2)]
            w2t = [mC.alloc([128, ngmax, 128], BF16) for _ in range(2)]
            sil = [mC.alloc([128, 512], F32) for _ in range(2)]
            xo = [mC.alloc([128, 512], F32) for _ in range(4)]
            cnt_u = 0
            cnt_d = 0
            issued = set()

            def wload(f):
                if f in issued or f >= FC:
                    return
                issued.add(f)
                s = f % 2
                sc.dma("pool", ("w1t", s), lambda e, f=f, s=s: e.dma_start(
                    out=w1t[s].rearrange("p a b -> p (a b)"), in_=w1[l][f], max_dma_last_dim=8192),
                    writes=[("w1t", s)])
                sc.dma("pool", ("w3t", s), lambda e, f=f, s=s: e.dma_start(
                    out=w3t[s].rearrange("p a b -> p (a b)"), in_=w3[l][f], max_dma_last_dim=8192),
                    writes=[("w3t", s)])
            for g, (f0, ng) in enumerate(cfg.groups):
                for fi in range(ng):
                    f = f0 + fi
                    s = f % 2
                    wload(f)
                    for th in range(TH):
                        sl = slice(th * 512, (th + 1) * 512)
                        bu = (cnt_u % 2) * 2
                        q = cnt_u % 2
                        cnt_u += 1

                        def mm(e, s=s, sl=sl, bu=bu):
                            for kc in range(KC):
                                e.matmul(ps[bu][:, :], w1t[s][:, kc, :], hT[:, kc, sl],
                                         start=(kc == 0), stop=(kc == KC - 1))
                            for kc in range(KC):
                                r = e.matmul(ps[bu + 1][:, :], w3t[s][:, kc, :], hT[:, kc, sl],
                                             start=(kc == 0), stop=(kc == KC - 1))
                            return r
                        sc.op("pe", mm, reads=[("w1t", s), ("w3t", s)], writes=[P(bu), P(bu + 1)])
                        sc.op("act", lambda e, q=q, bu=bu: e.activation(out=sil[q], in_=ps[bu][:, :], func=AF.Silu),
                              reads=[P(bu)], writes=[("sil", q)])
                        sc.op("dve", lambda e, q=q, bu=bu, fi=fi, sl=sl: e.tensor_tensor(
                            out=aT[:, fi, sl], in0=ps[bu + 1][:, :], in1=sil[q], op=ALU.mult),
                            reads=[P(bu + 1), ("sil", q)], writes=[("aT", fi, th)])
                wload(f0 + ng)
                wload(f0 + ng + 1)
                for o in range(KC):
                    s = o % 2
                    sc.dma("pool", ("w2t", s), lambda e, o=o, s=s, f0=f0, ng=ng: e.dma_start(
                        out=w2t[s][:, 0:ng, :].rearrange("p a b -> p (a b)"),
                        in_=w2[l][o][:, f0 * 128:(f0 + ng) * 128], max_dma_last_dim=8192),
                        writes=[("w2t", s)])
                    for th in range(TH):
                        sl = slice(th * 512, (th + 1) * 512)
                        bd = 4 + (cnt_d % 4)
                        xs_ = cnt_d % 4
                        cnt_d += 1

                        def mm2(e, s=s, sl=sl, bd=bd, ng=ng):
                            for fi in range(ng):
                                r = e.matmul(ps[bd][:, :], w2t[s][:, fi, :], aT[:, fi, sl],
                                             start=(fi == 0), stop=(fi == ng - 1))
                            return r
                        sc.op("pe", mm2, reads=[("w2t", s)] + [("aT", fi, th) for fi in range(ng)], writes=[P(bd)])
                        srcd = src if g == 0 else xres
                        sc.dma("sp", ("xo", xs_), lambda e, o=o, sl=sl, xs_=xs_, srcd=srcd: e.dma_start(
                            out=xo[xs_], in_=srcd[o][:, sl]), reads=[("xres", o, th)], writes=[("xo", xs_)])
                        sc.op("dve", lambda e, bd=bd, xs_=xs_: e.scalar_tensor_tensor(
                            out=xo[xs_], in0=ps[bd][:, :], scalar=0.5, in1=xo[xs_], op0=ALU.mult, op1=ALU.add),
                            reads=[P(bd), ("xo", xs_)], writes=[("xo", xs_)])
                        sc.dma("sp", ("xo", xs_), lambda e, o=o, sl=sl, xs_=xs_: e.dma_start(
                            out=xres[o][:, sl], in_=xo[xs_]), reads=[("xo", xs_)], writes=[("xres", o, th)])
            sc.barrier()
            mC.reset()

        hT = mB.alloc([128, KC, T], BF16)
        rmsnorm(xT, 0, hT)
        ffn(0, xT, hT)
        if stop_after == "ffn1":
            return finish(xres)

        rmsnorm(xres, 1, hT)
        KPP = cfg.KPP
        for i in range(cfg.NP1):
            sc.dma("sp", "c", lambda e, i=i: e.dma_start(
                out=h2loc[i].rearrange("(k p) t -> p k t", p=128), in_=hT[:, i * KPP:(i + 1) * KPP, :]),
                writes=[("h2loc", i)])
        wA_t = mC.alloc([128, KC, 768], BF16)
        wGt_t = mC.alloc([128, KC, 64], BF16)
        wV_t = mC.alloc([128, KC, 512], BF16)
        head_mark = mC.off
        hw_dmas = []
        for kc0 in range(0, KC, 2):
            hw_dmas.append(lambda e, kc0=kc0: e.dma_start(
                out=wA_t[:, kc0:kc0 + 2, :].rearrange("p a b -> p (a b)"),
                in_=wA[:, kc0 * 768:(kc0 + 2) * 768], max_dma_last_dim=6144))
        hw_dmas.append(lambda e: e.dma_start(out=wGt_t.rearrange("p a b -> p (a b)"), in_=wGt, max_dma_last_dim=8192))
        for kc0 in range(0, KC, 2):
            hw_dmas.append(lambda e, kc0=kc0: e.dma_start(
                out=wV_t[:, kc0:kc0 + 2, :].rearrange("p a b -> p (a b)"),
                in_=wV[:, kc0 * 512:(kc0 + 2) * 512], max_dma_last_dim=8192))
        for i in range(cfg.NP1):
            sc.dma("pool", ("cc", i % 16), lambda e, i=i: e.collective_compute(
                "AllGather", ALU.bypass, replica_groups=[[0, 1, 2, 3], [4, 5, 6, 7]],
                ins=[h2loc[i]], outs=[ex1mid[i]]),
                reads=[("h2loc", i)], writes=[("ex1mid", i)], inc=1)
        NWG = 3
        wgt = [mC.alloc([128, KC, 128], BF16) for _ in range(NWG)]
        gst = [mC.alloc([128, 512], BF16) for _ in range(4)]
        cnt = 0
        def ex1_stage2(i):
            sc.dma("pool", ("cc", i % 16), lambda e, i=i: e.collective_compute(
                "AllGather", ALU.bypass, replica_groups=[[0, 4], [1, 5], [2, 6], [3, 7]],
                ins=[ex1mid[i]], outs=[ex1out[i]]),
                reads=[("ex1mid", i)], writes=[("ex1out", i)], inc=1)
        st2_next = 0
        for f in range(NGC):
            s = f % NWG
            sc.dma("pool", ("wgt", s), lambda e, f=f, s=s: e.dma_start(
                out=wgt[s].rearrange("p a b -> p (a b)"), in_=wG[f], max_dma_last_dim=8192), writes=[("wgt", s)])
            if f >= 3 and f % 2 == 1 and st2_next < cfg.NP1:
                ex1_stage2(st2_next)
                st2_next += 1
            if f >= 3 and hw_dmas:
                sc.dma("pool", "wA", hw_dmas.pop(0))
            for th in range(TH):
                sl = slice(th * 512, (th + 1) * 512)
                b = cnt % 4
                cnt += 1

                def mm(e, s=s, sl=sl, b=b):
                    for kc in range(KC):
                        r = e.matmul(ps[b][:, :], wgt[s][:, kc, :], hT[:, kc, sl], start=(kc == 0), stop=(kc == KC - 1))
                    return r
                sc.op("pe", mm, reads=[("wgt", s)], writes=[P(b)])
                sc.op("act", lambda e, b=b: e.activation(out=gst[b], in_=ps[b][:, :], func=AF.Sigmoid),
                      reads=[P(b)], writes=[("gst", b)])
                sc.dma("sp", ("gst", b), lambda e, b=b, f=f, sl=sl: e.dma_start(out=gts[f][:, sl], in_=gst[b]),
                       reads=[("gst", b)])
        while st2_next < cfg.NP1:
            ex1_stage2(st2_next)
            st2_next += 1
        while hw_dmas:
            sc.dma("pool", "wA", hw_dmas.pop(0))
        sc.barrier()
        if stop_after == "ex1":
            return finish(xres)

        mB.reset()
        mC.off = head_mark
        h2b = [mB.alloc([128, KC, 512], BF16) for _ in range(2)] if 2 * KC * 512 * 2 <= HB else \
            [mC.alloc([128, KC, 512], BF16) for _ in range(2)]
        stf = [mC.alloc([128, 512], F32) for _ in range(2)]
        stb = [mC.alloc([128, 512], BF16) for _ in range(4)]
        stg = [mC.alloc([64, 512], F32) for _ in range(2)]
        vst = [mC.alloc([128, 512], BF16) for _ in range(2)]
        pcnt = 0
        for tb in range(S // 512):
            r = (tb * 512) // T
            toff = (tb * 512) % T
            hs = tb % 2
            cs = slice(tb * 512, (tb + 1) * 512)
            def load_blk(tb_):
                r_ = (tb_ * 512) // T
                toff_ = (tb_ * 512) % T
                hs_ = tb_ % 2
                for i in range(cfg.NP1):
                    sc.dma("sp", ("h2b", hs_, i % 4), lambda e, i=i, r_=r_, toff_=toff_, hs_=hs_: e.dma_start(
                        out=h2b[hs_][:, i * KPP:(i + 1) * KPP, :],
                        in_=ex1out[i][r_ * cfg.PR1:(r_ + 1) * cfg.PR1, toff_:toff_ + 512].rearrange("(k p) t -> p k t", p=128)),
                        writes=[("h2b", hs_, i)])
            if tb == 0:
                load_blk(0)
            if tb + 1 < S // 512:
                load_blk(tb + 1)
            hkeys = [("h2b", hs, i) for i in range(cfg.NP1)]
            for j in range(7):
                b = pcnt % 4
                pcnt += 1
                if j < 6:
                    def mm(e, j=j, hs=hs, b=b):
                        for kc in range(KC):
                            r_ = e.matmul(ps[b][:, :], wA_t[:, kc, j * 128:(j + 1) * 128], h2b[hs][:, kc, :],
                                          start=(kc == 0), stop=(kc == KC - 1))
                        return r_
                else:
                    def mm(e, hs=hs, b=b):
                        for kc in range(KC):
                            r_ = e.matmul(ps[b][0:64, :], wGt_t[:, kc, :], h2b[hs][:, kc, :],
                                          start=(kc == 0), stop=(kc == KC - 1))
                        return r_
                sc.op("pe", mm, reads=hkeys, writes=[P(b)])
                if j < 2:
                    q = j
                    sc.op("act", lambda e, q=q, b=b: e.activation(out=stf[q], in_=ps[b][:, :], func=AF.Copy),
                          reads=[P(b)], writes=[("stf", q)])
                    sc.dma("sp", ("stf", q), lambda e, q=q, j=j, cs=cs: e.dma_start(out=qkpre[j][:, cs], in_=stf[q]),
                           reads=[("stf", q)])
                elif j < 6:
                    q = j - 2
                    sc.op("dve", lambda e, q=q, b=b: e.tensor_copy(out=stb[q], in_=ps[b][:, :]),
                          reads=[P(b)], writes=[("stb", q)])
                    sc.dma("sp", ("stb", q), lambda e, q=q, cs=cs: e.dma_start(out=sqk[q][:, cs], in_=stb[q]),
                           reads=[("stb", q)])
                else:
                    q = tb % 2
                    sc.op("act", lambda e, q=q, b=b: e.activation(out=stg[q], in_=ps[b][0:64, :], func=AF.Copy),
                          reads=[P(b)], writes=[("stg", q)])
                    sc.dma("sp", ("stg", q), lambda e, q=q, cs=cs: e.dma_start(out=gates_d[0:1, cs], in_=stg[q][0:1, :]),
                           reads=[("stg", q)])
                    sc.dma("sp", ("stg", q), lambda e, q=q, cs=cs: e.dma_start(out=gates_d[1:2, cs], in_=stg[q][32:33, :]),
                           reads=[("stg", q)])
            for sub in range(4):
                ch = tb * 4 + sub
                b = 4 + ch % 4
                vq = ch % 2

                def mmv(e, hs=hs, b=b, sub=sub):
                    for kc in range(KC):
                        r_ = e.matmul(ps[b][:, :], h2b[hs][:, kc, sub * 128:(sub + 1) * 128], wV_t[:, kc, :],
                                      start=(kc == 0), stop=(kc == KC - 1))
                    return r_
                sc.op("pe", mmv, reads=hkeys, writes=[P(b)])
                sc.op("dve", lambda e, b=b, vq=vq: e.tensor_copy(out=vst[vq], in_=ps[b][:, :]),
                      reads=[P(b)], writes=[("vst", vq)])
                sc.dma("sp", ("vst", vq), lambda e, vq=vq, ch=ch: e.dma_start(out=vm_d[ch], in_=vst[vq][:, 0:256]),
                       reads=[("vst", vq)])
                for h in range(2):
                    sc.dma("sp", ("vst", vq), lambda e, vq=vq, ch=ch, h=h: e.dma_start(
                        out=vs_d[h][ch], in_=vst[vq][:, 256 + 128 * h:384 + 128 * h]), reads=[("vst", vq)])
        sc.barrier()
        mB.reset()
        mC.reset()
        if stop_after == "heads":
            return finish(xres)

        def yloc_ap(fc, tok0, n):
            dc = tok0 // T
            to = tok0 % T
            piece = fc * NPF + dc // ND2
            row0 = (dc % ND2) * 128
            return piece, yloc[piece][row0:row0 + 128, to:to + n]

        ywrites = {p: [] for p in range(cfg.NP2)}
        qT = mB.alloc([128, S], BF16)
        kT = mB.alloc([128, S], BF16)
        negGb = mC.alloc([128, S], F32)
        vext = mC.alloc([128, NCH, 264], BF16)
        for c0 in range(0, NCH, 8):
            n = min(8, NCH - c0)
            sc.dma("sp", "c", lambda e, c0=c0, n=n: e.dma_start(
                out=vext[:, c0:c0 + n, 0:256], in_=vm_d[c0:c0 + n].rearrange("c p d -> p c d")),
                writes=[("vext", c0)])
        sc.op("dve", lambda e: e.memset(vext[:, :, 256:257], 1.0), writes=["vones"])
        vkeys = [("vext", c0) for c0 in range(0, NCH, 8)] + ["vones"]
        SEG = min(2048, S)
        pre = [mC.alloc([128, SEG + 4], F32) for _ in range(2)]
        acc = [mC.alloc([128, SEG], F32) for _ in range(2)]
        it = 0
        for j in range(2):
            for sg in range(S // SEG):
                s = it % 2
                it += 1
                if sg == 0:
                    sc.op("dve", lambda e, s=s: e.memset(pre[s][:, 0:3], 0.0), writes=[("pre0", s)])
                    sc.dma("sp", ("pre", s), lambda e, s=s, j=j: e.dma_start(out=pre[s][:, 3:3 + SEG], in_=qkpre[j][:, 0:SEG]),
                           reads=[("pre0", s)], writes=[("pre", s)])
                else:
                    sc.dma("sp", ("pre", s), lambda e, s=s, j=j, sg=sg: e.dma_start(
                        out=pre[s][:, 0:3 + SEG], in_=qkpre[j][:, sg * SEG - 3:(sg + 1) * SEG]),
                        reads=[("pre0", s)], writes=[("pre", s)])
                wc = lambda i, j=j: c_small[:, 4 * j + i:4 * j + i + 1]
                sc.op("dve", lambda e, s=s, wc=wc: e.tensor_scalar(
                    out=acc[s], in0=pre[s][:, 3:3 + SEG], scalar1=wc(3), scalar2=None, op0=ALU.mult),
                    reads=[("pre", s), ("pre0", s)], writes=[("acc", s)])
                for i in (2, 1, 0):
                    sc.op("dve", lambda e, s=s, wc=wc, i=i: e.scalar_tensor_tensor(
                        out=acc[s], in0=pre[s][:, i:i + SEG], scalar=wc(i), in1=acc[s], op0=ALU.mult, op1=ALU.add),
                        reads=[("pre", s), ("acc", s)], writes=[("acc", s)])
                seg = slice(sg * SEG, (sg + 1) * SEG)
                if j == 0:
                    sc.op("act", lambda e, s=s, seg=seg: e.activation(out=qT[:, seg], in_=acc[s], func=AF.Silu),
                          reads=[("acc", s)], writes=[("qT", sg), ("pre", s), ("pre0", s)])
                else:
                    sc.op("act", lambda e, s=s: e.activation(out=acc[s], in_=acc[s], func=AF.Silu),
                          reads=[("acc", s)], writes=[("acc", s)])
                    sc.op("dve", lambda e, s=s, seg=seg: e.tensor_scalar(
                        out=kT[:, seg], in0=acc[s], scalar1=128.0 ** -0.5, scalar2=None, op0=ALU.mult),
                        reads=[("acc", s)], writes=[("kT", sg), ("pre", s), ("pre0", s)])
        qkkeys = [("qT", sg) for sg in range(S // SEG)] + [("kT", sg) for sg in range(S // SEG)]
        def cm():
            return mC.alloc([NCH, 128], F32)
        gi, gf, spt, Floc, Fp, gg, Gloc, Gt, tmpc = [cm() for _ in range(9)]
        colt = mC.alloc([128, 8], F32)
        rowt = mC.alloc([1, 4 * NCH], F32)
        Mb = mC.alloc([128, 2 * NCH], F32)
        gcol = mC.alloc([128, NCH], F32)
        ecol = mC.alloc([128, NCH], F32)
        srccol = mC.alloc([128, NCH], F32)
        csb = mC.alloc([128, NCH], F32)
        tcol = mC.alloc([128, NCH], F32)
        sc.dma("sp", "c", lambda e: e.dma_start(out=gi, in_=gates_d[0].rearrange("(c p) -> c p", p=128)), writes=["gi"])
        sc.dma("sp", "c", lambda e: e.dma_start(out=gf, in_=gates_d[1].rearrange("(c p) -> c p", p=128)), writes=["gf"])
        sc.op("dve", lambda e: e.tensor_scalar(out=colt[:, 0:1], in0=c_small[:, 9:10], scalar1=-1.0, scalar2=None, op0=ALU.mult),
              writes=["nbf"])
        sc.op("act", lambda e: e.activation(out=spt, in_=gf, func=AF.Exp, bias=colt[0:NCH, 0:1], scale=-1.0),
              reads=["gf", "nbf"], writes=["spt"])
        sc.op("act", lambda e: e.activation(out=spt, in_=spt, func=AF.Ln, bias=1.0, scale=1.0), reads=["spt"], writes=["spt"])
        sc.op("dve", lambda e: e.tensor_tensor_scan(out=Floc, data0=ones_f[0:NCH, :], data1=spt, initial=0.0,
                                                    op0=ALU.mult, op1=ALU.add), reads=["spt"], writes=["Floc"])
        sc.op("pe", lambda e: e.matmul(ps[0][0:NCH, 0:1], Umat[0:NCH, 0:NCH], Floc[:, 127:128], start=True, stop=True),
              reads=["Floc"], writes=[P(0)])
        sc.op("act", lambda e: e.activation(out=colt[0:NCH, 1:2], in_=ps[0][0:NCH, 0:1], func=AF.Copy), reads=[P(0)], writes=["offs"])
        sc.op("dve", lambda e: e.tensor_scalar(out=Fp, in0=Floc, scalar1=colt[0:NCH, 1:2], scalar2=None, op0=ALU.add),
              reads=["Floc", "offs"], writes=["Fp"])
        sc.op("dve", lambda e: e.scalar_tensor_tensor(out=gg, in0=gi, scalar=c_small[0:NCH, 8:9], in1=Fp, op0=ALU.add, op1=ALU.add),
              reads=["gi", "Fp"], writes=["gg"])
        sc.op("dve", lambda e: e.tensor_tensor_scan(out=Gloc, data0=gg, data1=gg, initial=-1.0e30, op0=ALU.max, op1=ALU.max),
              reads=["gg"], writes=["Gloc"])
        sc.op("pe", lambda e: e.matmul(ps[1][0:1, 0:NCH], Gloc[:, 127:128], identf[0:NCH, 0:NCH], start=True, stop=True),
              reads=["Gloc"], writes=[P(1)])
        sc.op("act", lambda e: e.activation(out=rowt[:, 0:NCH], in_=ps[1][0:1, 0:NCH], func=AF.Copy), reads=[P(1)], writes=["rowmax"])
        sc.op("dve", lambda e: e.tensor_tensor_scan(out=rowt[:, NCH:2 * NCH], data0=rowt[:, 0:NCH], data1=rowt[:, 0:NCH],
                                                    initial=0.0, op0=ALU.max, op1=ALU.max), reads=["rowmax"], writes=["Mrow"])
        sc.op("dve", lambda e: e.memset(rowt[:, 2 * NCH:2 * NCH + 1], 0.0), writes=["Mp0"])
        sc.op("dve", lambda e: e.tensor_copy(out=rowt[:, 2 * NCH + 1:3 * NCH], in_=rowt[:, NCH:2 * NCH - 1]),
              reads=["Mrow", "Mp0"], writes=["Mprow"])
        sc.op("pe", lambda e: e.matmul(ps[2][:, 0:2 * NCH], ones_f[0:1, 0:128], rowt[:, NCH:3 * NCH], start=True, stop=True),
              reads=["Mrow", "Mprow"], writes=[P(2)])
        sc.op("act", lambda e: e.activation(out=Mb, in_=ps[2][:, 0:2 * NCH], func=AF.Copy), reads=[P(2)], writes=["Mb"])
        sc.op("pe", lambda e: e.matmul(ps[3][0:NCH, 0:1], rowt[:, 2 * NCH:3 * NCH], ones_f[0:1, 0:1], start=True, stop=True),
              reads=["Mprow"], writes=[P(3)])
        sc.op("act", lambda e: e.activation(out=colt[0:NCH, 2:3], in_=ps[3][0:NCH, 0:1], func=AF.Copy), reads=[P(3)], writes=["Pcol"])
        sc.op("dve", lambda e: e.tensor_scalar(out=Gt, in0=Gloc, scalar1=colt[0:NCH, 2:3], scalar2=None, op0=ALU.max),
              reads=["Gloc", "Pcol"], writes=["Gt"])
        sc.op("dve", lambda e: e.tensor_scalar(out=tmpc, in0=Gt, scalar1=-1.0, scalar2=None, op0=ALU.mult),
              reads=["Gt"], writes=["tmpc"])
        sc.dma("sp", "rowsd", lambda e: e.dma_start(out=rows_d[0].rearrange("(c p) -> c p", p=128), in_=tmpc),
               reads=["tmpc"], writes=["rows0"])
        sc.dma("sp", "negGb", lambda e: e.dma_start(out=negGb, in_=rows_d[0:1, :].broadcast_to([128, S])),
               reads=["rows0"], writes=["negGb"])
        sc.op("dve", lambda e: e.tensor_tensor(out=Fp, in0=Fp, in1=Gt, op=ALU.subtract), reads=["Fp", "Gt", "gg"], writes=["Fp"])
        sc.op("pe", lambda e: e.matmul(ps[0][:, 0:NCH], gg, identf[0:NCH, 0:NCH], start=True, stop=True),
              reads=["gg"], writes=[P(0)])
        sc.op("act", lambda e: e.activation(out=gcol, in_=ps[0][:, 0:NCH], func=AF.Copy), reads=[P(0)], writes=["gcol"])
        sc.op("pe", lambda e: e.matmul(ps[1][:, 0:NCH], Fp, identf[0:NCH, 0:NCH], start=True, stop=True),
              reads=["Fp"], writes=[P(1)])
        sc.op("act", lambda e: e.activation(out=ecol, in_=ps[1][:, 0:NCH], func=AF.Exp), reads=[P(1)], writes=["ecol"])
        sc.op("dve", lambda e: e.tensor_tensor(out=tcol, in0=gcol, in1=Mb[:, 0:NCH], op=ALU.subtract), reads=["gcol", "Mb"], writes=["tcol"])
        sc.op("act", lambda e: e.activation(out=srccol, in_=tcol, func=AF.Exp), reads=["tcol"], writes=["srccol"])
        sc.op("dve", lambda e: e.tensor_tensor(out=tcol, in0=Mb[:, NCH:2 * NCH], in1=Mb[:, 0:NCH], op=ALU.subtract),
              reads=["Mb", "srccol"], writes=["tcol"])
        sc.op("act", lambda e: e.activation(out=csb, in_=tcol, func=AF.Exp), reads=["tcol"], writes=["csb"])
        argd = [mC.alloc([128, 128], F32) for _ in range(2)]
        dect = [mC.alloc([128, 128], F32) for _ in range(2)]
        wTt = [mC.alloc([128, 128], BF16) for _ in range(2)]
        qd = [mC.alloc([128, 128], BF16) for _ in range(2)]
        ksc = [mC.alloc([128, 128], BF16) for _ in range(2)]
        hh = [mC.alloc([128, 256], F32) for _ in range(2)]
        junk = mC.alloc([128, 256], F32)
        ya = [mC.alloc([128, 256], BF16) for _ in range(2)]
        yaT = [mC.alloc([128, 2, 128], BF16) for _ in range(2)]
        sm = [mC.alloc([128, 4], F32) for _ in range(2)]
        C32 = mC.alloc([128, 264], F32)
        Cbf = mC.alloc([128, 264], BF16)
        sc.op("dve", lambda e: e.memset(C32, 0.0), writes=["C32"])
        sc.op("dve", lambda e: e.memset(Cbf, 0.0), writes=["Cbf"])
        def m_head(c):
            s = c % 2
            cs = slice(c * 128, (c + 1) * 128)
            bS = s
            sc.op("pe", lambda e, cs=cs, bS=bS: e.matmul(ps[bS][:, 0:128], kT[:, cs], qT[:, cs], start=True, stop=True),
                  reads=qkkeys, writes=[P(bS)])
            sc.op("dve", lambda e, s=s, cs=cs: e.tensor_tensor(out=argd[s], in0=negGb[:, cs], in1=maskneg, op=ALU.add),
                  reads=["negGb"], writes=[("argd", s)])
            sc.op("act", lambda e, s=s, c=c: e.activation(out=argd[s], in_=argd[s], func=AF.Exp, bias=gcol[:, c:c + 1], scale=1.0),
                  reads=[("argd", s), "gcol"], writes=[("argd", s)])
            sc.op("dve", lambda e, s=s, bS=bS: e.tensor_tensor(out=wTt[s], in0=ps[bS][:, 0:128], in1=argd[s], op=ALU.mult),
                  reads=[P(bS), ("argd", s)], writes=[("wT", s)])
            sc.op("act", lambda e, s=s, c=c, cs=cs: e.activation(out=dect[s], in_=negGb[:, cs], func=AF.Exp,
                                                                 bias=Mb[:, NCH + c:NCH + c + 1], scale=1.0),
                  reads=["negGb", "Mb"], writes=[("dec", s)])
            sc.op("dve", lambda e, s=s, cs=cs: e.tensor_tensor(out=qd[s], in0=qT[:, cs], in1=dect[s], op=ALU.mult),
                  reads=[("dec", s)] + qkkeys, writes=[("qd", s)])
            sc.op("pe", lambda e, cs=cs: e.transpose(psb[6][:, 0:128], kT[:, cs], c_id), reads=qkkeys, writes=[P(6)])
            sc.op("dve", lambda e, s=s, c=c: e.tensor_scalar(out=ksc[s], in0=psb[6][:, 0:128], scalar1=srccol[:, c:c + 1],
                                                             scalar2=None, op0=ALU.mult),
                  reads=[P(6), "srccol"], writes=[("ksc", s)])

        def m_mid(c):
            s = c % 2
            bO = 2 + s

            def mmO(e, s=s, c=c, bO=bO):
                e.matmul(ps[bO][:, 0:257], qd[s], Cbf[:, 0:257], start=True, stop=False)
                return e.matmul(ps[bO][:, 0:257], wTt[s], vext[:, c, 0:257], start=False, stop=True)
            sc.op("pe", mmO, reads=[("qd", s), ("wT", s), "Cbf"] + vkeys, writes=[P(bO)])
            sc.op("pe", lambda e, s=s, c=c: e.matmul(ps[7][:, 0:257], ksc[s], vext[:, c, 0:257], start=True, stop=True),
                  reads=[("ksc", s)] + vkeys, writes=[P(7)])
            sc.op("dve", lambda e, c=c: e.scalar_tensor_tensor(out=C32[:, 0:257], in0=C32[:, 0:257], scalar=csb[:, c:c + 1],
                                                               in1=ps[7][:, 0:257], op0=ALU.mult, op1=ALU.add),
                  reads=[P(7), "C32", "csb"], writes=["C32"])
            sc.op("act", lambda e: e.activation(out=Cbf[:, 0:257], in_=C32[:, 0:257], func=AF.Copy), reads=["C32"], writes=["Cbf"])

        def m_tail(c):
            s = c % 2
            bO, bT = 2 + s, 4 + s
            sc.op("dve", lambda e, s=s, bO=bO: e.tensor_scalar(
                out=sm[s][:, 3:4], in0=ps[bO][:, 256:257], scalar1=-1.0, scalar2=None, op0=ALU.mult),
                reads=[P(bO)], writes=[("nden", s)])
            sc.op("dve", lambda e, s=s, c=c, bO=bO: e.scalar_tensor_tensor(
                out=sm[s][:, 0:1], in0=ps[bO][:, 256:257], scalar=ecol[:, c:c + 1], in1=sm[s][:, 3:4],
                op0=ALU.max, op1=ALU.max), reads=[P(bO), "ecol", ("nden", s)], writes=[("dd", s)])
            sc.op("dve", lambda e, s=s: e.reciprocal(out=sm[s][:, 0:1], in_=sm[s][:, 0:1]), reads=[("dd", s)], writes=[("dd", s)])
            sc.op("act", lambda e, s=s, bO=bO: e.activation(out=hh[s], in_=ps[bO][:, 0:256], func=AF.Copy, scale=sm[s][:, 0:1]),
                  reads=[P(bO), ("dd", s)], writes=[("hh", s)])
            sc.op("act", lambda e, s=s: e.activation(out=junk, in_=hh[s], func=AF.Square, accum_out=sm[s][:, 1:2]),
                  reads=[("hh", s)], writes=[("ss", s), "junk"])
            sc.op("act", lambda e, s=s: e.activation(out=sm[s][:, 2:3], in_=sm[s][:, 1:2], func=AF.Ln, bias=c_eps[:, 0:1], scale=1.0 / 256),
                  reads=[("ss", s)], writes=[("rs", s)])
            sc.op("act", lambda e, s=s: e.activation(out=sm[s][:, 2:3], in_=sm[s][:, 2:3], func=AF.Exp, scale=-0.5),
                  reads=[("rs", s)], writes=[("rs", s)])
            sc.op("dve", lambda e, s=s: e.scalar_tensor_tensor(out=ya[s], in0=hh[s], scalar=sm[s][:, 2:3], in1=gml_b,
                                                               op0=ALU.mult, op1=ALU.mult),
                  reads=[("hh", s), ("rs", s)], writes=[("ya", s)])

            def trY(e, s=s, bT=bT):
                e.transpose(psb[bT][:, 0:128], ya[s][:, 0:128], c_id)
                return e.transpose(psb[bT][:, 128:256], ya[s][:, 128:256], c_id)
            sc.op("pe", trY, reads=[("ya", s)], writes=[P(bT)])
            sc.op("act", lambda e, s=s, bT=bT: e.activation(out=yaT[s].rearrange("p a b -> p (a b)"), in_=psb[bT][:, 0:256], func=AF.Copy),
                  reads=[P(bT)], writes=[("yaT", s)])
            for j in range(2):
                piece, dst = yloc_ap(j, c * 128, 128)
                key = ("yloc", piece, c)
                ywrites[piece].append(key)
                sc.dma("sp", ("yaT", s), lambda e, s=s, j=j, dst=dst: e.dma_start(out=dst, in_=yaT[s][:, j, :]),
                       reads=[("yaT", s)], writes=[key])

        c_eps = mC.alloc([128, 1], F32)
        sc.op("dve", lambda e: e.memset(c_eps, EPS), writes=["c_eps"])
        m_head(0)
        for c in range(NCH + 1):
            if c + 1 < NCH:
                m_head(c + 1)
            if c < NCH:
                m_mid(c)
            if c >= 1:
                m_tail(c - 1)
        sc.barrier()
        mB.reset()
        mC.reset()
        if stop_after == "mlstm":
            return finish(xres)

        def exch2_piece(p, keys):
            sc.dma("pool", ("cc", p % 16), lambda e, p=p: e.collective_compute(
                "AllGather", ALU.bypass, replica_groups=[[0, 1, 2, 3], [4, 5, 6, 7]],
                ins=[yloc[p]], outs=[ex2mid[p]]), reads=keys, writes=[("ex2mid", p)], inc=1)
            sc.dma("pool", ("cc", p % 16), lambda e, p=p: e.collective_compute(
                "AllGather", ALU.bypass, replica_groups=[[0, 4], [1, 5], [2, 6], [3, 7]],
                ins=[ex2mid[p]], outs=[ex2out[p * 8 * cfg.PR2:(p + 1) * 8 * cfg.PR2, :]]),
                reads=[("ex2mid", p)], writes=[("ex2out", p)], inc=1)

        for p in range(2 * NPF):
            exch2_piece(p, [])
        sqT = [mB.alloc([128, S], BF16), None]
        skT = [mB.alloc([128, S], BF16), None]
        sqT[1] = mC.alloc([128, S], BF16)
        skT[1] = mC.alloc([128, S], BF16)
        vsh = [mC.alloc([128, NCH, 128], BF16) for _ in range(2)]
        for h in range(2):
            sc.dma("sp", "c", lambda e, h=h: e.dma_start(out=sqT[h], in_=sqk[2 * h]), writes=[("sq", h)])
            sc.dma("sp", "c", lambda e, h=h: e.dma_start(out=skT[h], in_=sqk[2 * h + 1]), writes=[("sk", h)])
            for c0 in range(0, NCH, 16):
                n = min(16, NCH - c0)
                sc.dma("sp", "c", lambda e, h=h, c0=c0, n=n: e.dma_start(
                    out=vsh[h][:, c0:c0 + n, :], in_=vs_d[h][c0:c0 + n].rearrange("c p d -> p c d")),
                    writes=[("vsh", h, c0)])
        vshkeys = [[("vsh", h, c0) for c0 in range(0, NCH, 16)] for h in range(2)]
        NSL = 4
        spt_ = [mC.alloc([128, 1024], F32) for _ in range(NSL)]
        sptb = [mC.alloc([128, 1024], BF16) for _ in range(NSL)]
        tt_ = [mC.alloc([128, 1024], F32) for _ in range(NSL)]
        Wt_ = [mC.alloc([128, 1024], BF16) for _ in range(NSL)]
        ost = mC.alloc([128, 1024], BF16)
        SCL = 128.0 ** -0.5
        units = []
        for qt in range(NQT):
            nkb = 4 * qt + 4
            for kb in range(nkb - 1, -1, -1):
                units.append((qt, kb, kb == nkb - 1, kb == 0, (kb - 4 * qt) if kb >= 4 * qt else None))
        NU = len(units)
        zpair = [psall[:, 0:1024], psall[:, 1024:2048]]
        apair = psall[:, 4 * 512:6 * 512]
        opair = psall[:, 6 * 512:8 * 512]
        hs_ = [slice(0, 512), slice(512, 1024)]

        def S1(u):
            qt, kb, first, last, dj = units[u]
            zb = (u % 2) * 2

            def mm(e, kb=kb, qt=qt, zb=zb):
                for h in range(2):
                    r_ = e.matmul(ps[zb + h][:, :], skT[h][:, kb * 128:(kb + 1) * 128], sqT[h][:, qt * 512:(qt + 1) * 512],
                                  start=True, stop=True)
                return r_
            sc.op("pe", mm, reads=[("sq", 0), ("sk", 0), ("sq", 1), ("sk", 1)], writes=[P(zb), P(zb + 1)])

        def S2(u):
            qt, kb, first, last, dj = units[u]
            zb, sl = (u % 2) * 2, u % NSL
            zp = zpair[u % 2]
            sc.op("act", lambda e, zp=zp, sl=sl: e.activation(out=spt_[sl], in_=zp, func=AF.Exp, scale=SCL),
                  reads=[P(zb), P(zb + 1)], writes=[("spt", sl)])
            sc.op("act", lambda e, sl=sl: e.activation(out=sptb[sl], in_=spt_[sl], func=AF.Ln, bias=1.0, scale=1.0),
                  reads=[("spt", sl)], writes=[("sptb", sl)])
            if dj is not None:
                for h in range(2):
                    sc.op("dve", lambda e, sl=sl, dj=dj, h=h: e.tensor_tensor(out=sptb[sl][:, hs_[h]], in0=sptb[sl][:, hs_[h]],
                                                                           in1=dmask[dj], op=ALU.mult),
                          reads=[("sptb", sl)], writes=[("sptb", sl)])

        def S3(u):
            qt, kb, first, last, dj = units[u]
            zb, sl = (u % 2) * 2, u % NSL
            zp = zpair[u % 2]
            sc.op("dve", lambda e, zp=zp, sl=sl: e.scalar_tensor_tensor(
                out=tt_[sl], in0=zp, scalar=SCL, in1=sptb[sl], op0=ALU.mult, op1=ALU.subtract),
                reads=[P(zb), P(zb + 1), ("sptb", sl)], writes=[("tt", sl)])

            def mm(e, sl=sl, first=first):
                for h in range(2):
                    r_ = e.matmul(ps[4 + h][:, :], triR, sptb[sl][:, hs_[h]], start=first, stop=False)
                return r_
            sc.op("pe", mm, reads=[("sptb", sl)], writes=[P(4), P(5)])

        def S4(u):
            qt, kb, first, last, dj = units[u]
            sl = u % NSL
            sc.op("dve", lambda e, sl=sl: e.tensor_tensor(out=tt_[sl], in0=apair, in1=tt_[sl], op=ALU.add),
                  reads=[P(4), P(5), ("tt", sl)], writes=[("tt", sl)])
            sc.op("act", lambda e, sl=sl: e.activation(out=Wt_[sl], in_=tt_[sl], func=AF.Exp), reads=[("tt", sl)], writes=[("W", sl)])
            if dj is not None:
                for h in range(2):
                    sc.op("dve", lambda e, sl=sl, dj=dj, h=h: e.tensor_tensor(out=Wt_[sl][:, hs_[h]], in0=Wt_[sl][:, hs_[h]],
                                                                           in1=dmask[dj], op=ALU.mult),
                          reads=[("W", sl)], writes=[("W", sl)])

        def S5a(u):
            qt, kb, first, last, dj = units[u]
            sl = u % NSL

            def mm(e, sl=sl, last=last):
                for h in range(2):
                    r_ = e.matmul(ps[4 + h][:, :], restR, sptb[sl][:, hs_[h]], start=False, stop=last)
                return r_
            sc.op("pe", mm, reads=[("sptb", sl), ("tt", sl)], writes=[P(4), P(5)])

        def S5(u):
            qt, kb, first, last, dj = units[u]
            sl = u % NSL

            def mm(e, sl=sl, kb=kb, first=first, last=last):
                for h in range(2):
                    r_ = e.matmul(ps[6 + h][:, :], vsh[h][:, kb, :], Wt_[sl][:, hs_[h]], start=first, stop=last)
                return r_
            sc.op("pe", mm, reads=[("W", sl)] + vshkeys[0] + vshkeys[1], writes=[P(6), P(7)])
            if last:
                sc.op("act", lambda e: e.activation(out=ost, in_=opair, func=AF.Copy), reads=[P(6), P(7)], writes=["ost"])
                for h in range(2):
                    piece, dst = yloc_ap(2 + h, qt * 512, 512)
                    key = ("yloc", piece, qt)
                    ywrites[piece].append(key)
                    sc.dma("sp", ("ost", h), lambda e, h=h, dst=dst: e.dma_start(out=dst, in_=ost[:, hs_[h]]),
                           reads=["ost"], writes=[key])
                tok_end = (qt + 1) * 512
                if tok_end % (ND2 * T) == 0:
                    for h in range(2):
                        piece, _ = yloc_ap(2 + h, qt * 512, 512)
                        exch2_piece(piece, list(ywrites[piece]))

        for i in range(-2, NU):
            if 0 <= i + 2 < NU:
                S1(i + 2)
                S2(i + 2)
            if 0 <= i < NU:
                S5a(i)
            if 0 <= i + 1 < NU:
                S3(i + 1)
                S4(i + 1)
            if 0 <= i < NU:
                S5(i)
        sc.barrier()
        mB.reset()
        mC.reset()
        if stop_after == "sb":
            return finish(xres)

        pid_cache = {}
        yaT_ = mB.alloc([128, 16, T], BF16)
        ybT_ = mB.alloc([128, 16, T], BF16) if 2 * 16 * T * 2 <= HB else mC.alloc([128, 16, T], BF16)
        mergedT = mC.alloc([128, KC, T], BF16)
        FCR = NPF * 8 * cfg.PR2
        for fc in range(4):
            def ld(e, fc=fc):
                if "E" not in pid_cache:
                    pid_ = nc.partition_id()
                    if ND2 == 1:
                        pid_cache["E"] = pid_ * (8 * cfg.PR2)
                    else:
                        pid_cache["E"] = (pid_ // ND2) * (8 * cfg.PR2) + (pid_ % ND2) * 128
                E = pid_cache["E"]
                A = ex2out[bass.ds(E, 4 * FCR), :].rearrange("(fc x r f) t -> fc x r f t", fc=4, x=NPF, r=8, f=cfg.PR2)
                return e.dma_start(out=ex2stage[fc], in_=A[fc, 0, :, 0:128, :])
            sc.dma("pool", "yst", ld, writes=[("ystage", fc)])
        for r in range(NCORES):
            for j in range(2):
                for fc, dstt in ((j, yaT_), (2 + j, ybT_)):
                    sc.dma("sp", "yld", lambda e, fc=fc, dstt=dstt, r=r, j=j: e.dma_start(
                        out=dstt[:, 2 * r + j, :], in_=ex2stage[fc][r]),
                        reads=[("ystage", fc)], writes=[("yin", fc // 2, 2 * r + j)])
        gtile = [mC.alloc([128, T], BF16) for _ in range(4)]
        for kc in range(16):
            s = kc % 2
            sc.dma("sp", ("gt", s), lambda e, kc=kc, s=s: e.dma_start(out=gtile[s], in_=gts[kc]), writes=[("gt", s)])
            sc.op("dve", lambda e, kc=kc, s=s: e.tensor_tensor(out=yaT_[:, kc, :], in0=yaT_[:, kc, :], in1=gtile[s], op=ALU.mult),
                  reads=[("gt", s), ("yin", 0, kc)], writes=[("yin", 0, kc)])
        wpt = [[mC.alloc([128, 16, 128], BF16) for _ in range(2)] for _ in range(2)]
        m1 = [mC.alloc([128, 512], F32) for _ in range(2)]
        m2 = [mC.alloc([128, 512], F32) for _ in range(2)]
        yakeys = [("yin", 0, kc) for kc in range(16)]
        ybkeys = [("yin", 1, kc) for kc in range(16)]
        cnt = 0
        for o in range(KC):
            s = o % 2
            sc.dma("pool", ("wpa", s), lambda e, o=o, s=s: e.dma_start(
                out=wpt[0][s].rearrange("p a b -> p (a b)"), in_=wpa[o], max_dma_last_dim=8192), writes=[("wpa", s)])
            sc.dma("pool", ("wpb", s), lambda e, o=o, s=s: e.dma_start(
                out=wpt[1][s].rearrange("p a b -> p (a b)"), in_=wpb[o], max_dma_last_dim=8192), writes=[("wpb", s)])
            sc.dma("sp", ("gt", 2), lambda e, o=o: e.dma_start(out=gtile[2], in_=gts[16 + o]), writes=[("gt", 2)])
            sc.dma("sp", ("gt", 3), lambda e, o=o: e.dma_start(out=gtile[3], in_=gts[16 + KC + o]), writes=[("gt", 3)])
            for th in range(TH):
                sl = slice(th * 512, (th + 1) * 512)
                b = (cnt % 2) * 2
                q = cnt % 2
                cnt += 1

                def mm(e, s=s, sl=sl, b=b):
                    for kc in range(16):
                        e.matmul(ps[b][:, :], wpt[0][s][:, kc, :], yaT_[:, kc, sl], start=(kc == 0), stop=(kc == 15))
                    for kc in range(16):
                        r_ = e.matmul(ps[b + 1][:, :], wpt[1][s][:, kc, :], ybT_[:, kc, sl], start=(kc == 0), stop=(kc == 15))
                    return r_
                sc.op("pe", mm, reads=[("wpa", s), ("wpb", s)] + yakeys + ybkeys, writes=[P(b), P(b + 1)])
                sc.op("dve", lambda e, b=b, q=q, sl=sl: e.tensor_tensor(out=m1[q], in0=ps[b][:, :], in1=gtile[2][:, sl], op=ALU.mult),
                      reads=[P(b), ("gt", 2)], writes=[("m1", q)])
                sc.op("dve", lambda e, b=b, q=q, sl=sl: e.tensor_tensor(out=m2[q], in0=ps[b + 1][:, :], in1=gtile[3][:, sl], op=ALU.mult),
                      reads=[P(b + 1), ("gt", 3)], writes=[("m2", q)])
                sc.op("dve", lambda e, q=q, o=o, sl=sl: e.tensor_tensor(out=mergedT[:, o, sl], in0=m1[q], in1=m2[q], op=ALU.add),
                      reads=[("m1", q), ("m2", q)], writes=[("mg", o, th)])
        sc.barrier()
        mB.reset()
        wot = [mB.alloc([128, KC, 128], BF16) for _ in range(2)]
        xo = [mB.alloc([128, 512], F32) for _ in range(4)]
        cnt = 0
        for o in range(KC):
            s = o % 2
            sc.dma("pool", ("wot", s), lambda e, o=o, s=s: e.dma_start(
                out=wot[s].rearrange("p a b -> p (a b)"), in_=wo[o], max_dma_last_dim=8192), writes=[("wot", s)])
            for th in range(TH):
                sl = slice(th * 512, (th + 1) * 512)
                b = 4 + cnt % 4
                xs_ = cnt % 4
                cnt += 1

                def mm(e, s=s, sl=sl, b=b):
                    for kc in range(KC):
                        r_ = e.matmul(ps[b][:, :], wot[s][:, kc, :], mergedT[:, kc, sl], start=(kc == 0), stop=(kc == KC - 1))
                    return r_
                sc.op("pe", mm, reads=[("wot", s)], writes=[P(b)])
                sc.dma("sp", ("xo", xs_), lambda e, o=o, sl=sl, xs_=xs_: e.dma_start(out=xo[xs_], in_=xres[o][:, sl]),
                       writes=[("xo", xs_)])
                sc.op("dve", lambda e, b=b, xs_=xs_: e.tensor_tensor(out=xo[xs_], in0=ps[b][:, :], in1=xo[xs_], op=ALU.add),
                      reads=[P(b), ("xo", xs_)], writes=[("xo", xs_)])
                sc.dma("sp", ("xo", xs_), lambda e, o=o, sl=sl, xs_=xs_: e.dma_start(out=xres[o][:, sl], in_=xo[xs_]),
                       reads=[("xo", xs_)])
        sc.barrier()
        mB.reset()
        mC.reset()
        if stop_after == "merge":
            return finish(xres)

        hT = mB.alloc([128, KC, T], BF16)
        rmsnorm(xres, 2, hT)
        ffn(1, xres, hT)
        rmsnorm(xres, 3, None, outT)
        return finish(None)


def make_consts():
    j = np.arange(128)[:, None]
    s = np.arange(128)[None, :]
    ones = np.ones((128, 128), np.float32)
    triN = np.where(j > s, -1.0, 0.0).astype(np.float32)
    restN = np.where(j <= s, -1.0, 0.0).astype(np.float32)
    maskneg = np.where(j <= s, 0.0, NEG).astype(np.float32)
    U = np.where(j < s, 1.0, 0.0).astype(np.float32)
    identf = np.eye(128, dtype=np.float32)
    t = np.arange(512)[None, :]
    masks = [np.where(128 * dj + j < t, 1.0, 0.0).astype(np.float32) for dj in range(4)]
    cf = np.concatenate([ones, triN, restN, maskneg, U, identf] + masks, axis=1)
    return np.ascontiguousarray(cf), np.eye(128, dtype=np.float32).astype(ml_dtypes.bfloat16)


def _lhsT_layout(w, kc, oc):
    K, O = w.shape
    return np.ascontiguousarray(w.reshape(kc, 128, oc, 128).transpose(2, 1, 0, 3).reshape(oc, 128, kc * 128))


def prep_inputs(cfg, inp):
    D, T, KC, FC = cfg.D, cfg.T, cfg.KC, cfg.FC
    f32 = np.float32
    x = np.asarray(inp["x"], f32)[0]
    shared = {}
    for l, sfx in ((0, "a"), (1, "b")):
        n = "ffn1" if l == 0 else "ffn2"
        shared["w1" + sfx] = _lhsT_layout(np.asarray(inp["w1_" + n], f32)[0], KC, FC)
        shared["w3" + sfx] = _lhsT_layout(np.asarray(inp["w3_" + n], f32)[0], KC, FC)
        shared["w2" + sfx] = _lhsT_layout(np.asarray(inp["w2_" + n], f32)[0], FC, KC)
    gs = [np.asarray(inp[k], f32).reshape(-1) for k in ("g_ffn1", "g_mix", "g_ffn2", "g_final")]
    shared["gv"] = np.ascontiguousarray(np.concatenate([g.reshape(KC, 128).T for g in gs], axis=1))
    w_in = np.asarray(inp["w_in"], f32)[0]
    o_mq, o_mk, o_mv, o_mo, o_mi, o_mf = 0, 1024, 2048, 4096, 6144, 6152
    o_sq, o_sk, o_sv = 6160, 8208, 10256
    o_ga = 12304
    o_gb = o_ga + D
    gate_cols = np.concatenate([np.arange(o_mo, o_mo + 2048), np.arange(o_ga, o_ga + D), np.arange(o_gb, o_gb + D)])
    wgfull = w_in[:, gate_cols]
    shared["wG"] = _lhsT_layout(wgfull, KC, 16 + 2 * KC)
    shared["wpa"] = _lhsT_layout(np.asarray(inp["w_proj_a"], f32)[0], 16, KC)
    shared["wpb"] = _lhsT_layout(np.asarray(inp["w_proj_b"], f32)[0], 16, KC)
    shared["wo"] = _lhsT_layout(np.asarray(inp["w_out"], f32)[0], KC, KC)
    cf, idb = make_consts()
    shared["cf32"] = cf
    shared["identb"] = idb
    conv = np.asarray(inp["conv_qk"], f32)[0]
    b_i = np.asarray(inp["b_igate"], f32)[0]
    b_f = np.asarray(inp["b_fgate"], f32)[0]
    gml = np.asarray(inp["g_mlstm_out"], f32)[0]

    def pk(wcols):
        n = wcols.shape[1]
        return np.ascontiguousarray(wcols.reshape(KC, 128, n).transpose(1, 0, 2).reshape(128, KC * n))

    maps = []
    for c in range(NCORES):
        m = dict(shared)
        m["xT"] = np.ascontiguousarray(x[c * T:(c + 1) * T, :].T.reshape(KC, 128, T))
        colsA = np.concatenate([
            np.arange(o_mq + c * 128, o_mq + (c + 1) * 128), np.arange(o_mk + c * 128, o_mk + (c + 1) * 128),
            np.arange(o_sq + (2 * c) * 128, o_sq + (2 * c + 1) * 128), np.arange(o_sk + (2 * c) * 128, o_sk + (2 * c + 1) * 128),
            np.arange(o_sq + (2 * c + 1) * 128, o_sq + (2 * c + 2) * 128), np.arange(o_sk + (2 * c + 1) * 128, o_sk + (2 * c + 2) * 128)])
        m["wA"] = pk(w_in[:, colsA])
        wg = np.zeros((D, 64), f32)
        wg[:, 0] = w_in[:, o_mi + c]
        wg[:, 32] = w_in[:, o_mf + c]
        m["wGt"] = pk(wg)
        colsV = np.concatenate([np.arange(o_mv + c * 256, o_mv + (c + 1) * 256),
                                np.arange(o_sv + (2 * c) * 128, o_sv + (2 * c + 2) * 128)])
        m["wV"] = pk(w_in[:, colsV])
        sm = np.zeros((128, 16 + 256), f32)
        sm[:, 0:4] = conv[:, c * 128:(c + 1) * 128].T
        sm[:, 4:8] = conv[:, 1024 + c * 128:1024 + (c + 1) * 128].T
        sm[:, 8] = b_i[c]
        sm[:, 9] = b_f[c]
        sm[:, 16:272] = gml[c * 256:(c + 1) * 256][None, :]
        m["small"] = sm
        maps.append(m)
    return maps


_PROG_CACHE = {}


def run(cfg, inp, stop_after=None, trace=False):
    key = (cfg.D, cfg.T, cfg.DFF, stop_after)
    if key not in _PROG_CACHE:
        _PROG_CACHE[key] = build_program(cfg, stop_after)
    nc = _PROG_CACHE[key]
    maps = prep_inputs(cfg, inp)
    res = run_bass_kernel_spmd(nc, maps, core_ids=list(range(NCORES)), trace=trace)
    name = "dbg" if stop_after else "outT"
    outs = [r[name].reshape(cfg.D, cfg.T).T for r in res.results]
    full = np.concatenate(outs, axis=0)[None].astype(np.float32)
    if stop_after:
        ya = np.zeros((cfg.S, 2048), np.float32); yb = np.zeros((cfg.S, 2048), np.float32)
        for c, r in enumerate(res.results):
            y = r["dbg2"].astype(np.float32)
            for fc in range(4):
                for dc in range(NCORES):
                    blk = y[fc * cfg.NPF + dc // cfg.ND2][(dc % cfg.ND2) * 128:(dc % cfg.ND2) * 128 + 128, :]
                    if fc < 2:
                        ya[dc * cfg.T:(dc + 1) * cfg.T, c * 256 + fc * 128:c * 256 + (fc + 1) * 128] = blk.T
                    else:
                        yb[dc * cfg.T:(dc + 1) * cfg.T, (2 * c + fc - 2) * 128:(2 * c + fc - 1) * 128] = blk.T
        res.ya, res.yb = ya, yb
    return full, res


def kernel(**inputs):
    out, _ = run(Cfg(), inputs)
    return out
```

```python
from contextlib import ExitStack
import numpy as np
import ml_dtypes
import concourse.bass as bass
import concourse.mybir as mybir
from concourse.bass_utils import run_bass_kernel_spmd

F32 = mybir.dt.float32
F32R = mybir.dt.float32r
SB_CUMSUM_DT = F32R
BF16 = mybir.dt.bfloat16
AF = mybir.ActivationFunctionType
ALU = mybir.AluOpType
NCORES = 8
EPS = 1e-6
NEG = -30000.0
NCONST = 6 * 128 + 4 * 512


class Cfg:
    def __init__(self, D=4096, T=1024, DFF=11008, G=4, PIECE=512 * 1024):
        self.D, self.T, self.DFF, self.G = D, T, DFF, G
        self.S = NCORES * T
        self.KC = D // 128
        self.FC = DFF // 128
        self.TH = T // 512
        self.NCH = self.S // 128
        self.NQT = self.S // 512
        base, rem = divmod(self.FC, G)
        self.groups = []
        f0 = 0
        for g in range(G):
            n = base + (1 if g < rem else 0)
            self.groups.append((f0, n))
            f0 += n
        self.PR1 = PIECE // (2 * T)
        self.NP1 = D // self.PR1
        self.KPP = self.PR1 // 128
        self.ND2 = PIECE // (128 * T * 2)
        self.NPF = NCORES // self.ND2
        self.NP2 = 4 * self.NPF
        self.PR2 = self.ND2 * 128
        assert self.PR1 % 128 == 0 and D % self.PR1 == 0 and self.ND2 >= 1


class Sched:
    def __init__(self, nc, enter):
        self.nc = nc
        self.enter = enter
        self.streams = {k: [] for k in ("pe", "act", "dve", "pool", "sp")}
        self.sems = {}
        self.count = {}
        for k in ("pe", "act", "dve", "pool"):
            self.sems[k] = enter(nc.semaphore("s_" + k))
            self.count[k] = 0
        self.waited = {k: {} for k in self.streams}
        self.lastw = {}
        self.readers = {}

    def _need(self, e, tok, waits):
        if tok is None:
            return
        k, v = tok
        if e == "pe" and k == "pe":
            return
        if k not in ("pe", "act", "dve", "pool"):
            v = self.count[k]
        if self.waited[e].get(k, 0) >= v:
            return
        self.waited[e][k] = v
        waits[k] = max(waits.get(k, 0), v)

    def _deps(self, e, reads, writes):
        waits = {}
        for r in reads:
            self._need(e, self.lastw.get(r), waits)
        for r in writes:
            self._need(e, self.lastw.get(r), waits)
            for k, v in self.readers.get(r, {}).items():
                self._need(e, (k, v), waits)
        return waits

    def _commit(self, tok, reads, writes):
        for r in reads:
            d = self.readers.setdefault(r, {})
            d[tok[0]] = max(d.get(tok[0], 0), tok[1])
        for r in writes:
            self.lastw[r] = tok
            self.readers[r] = {}

    def op(self, e, fn, reads=(), writes=()):
        waits = self._deps(e, reads, writes)
        self.count[e] += 1
        tok = (e, self.count[e])
        self.streams[e].append((waits, fn, e, 1))
        self._commit(tok, reads, writes)

    def dma(self, q, key, fn, reads=(), writes=(), inc=16):
        if key not in self.sems:
            self.sems[key] = self.enter(self.nc.semaphore("d%d" % len(self.sems)))
            self.count[key] = 0
        waits = self._deps(q, reads, writes)
        self.count[key] += inc
        tok = (key, self.count[key])
        self.streams[q].append((waits, fn, key, inc))
        self._commit(tok, reads, writes)

    def barrier(self):
        for e in self.streams:
            waits = {}
            for k, v in self.count.items():
                if v > 0:
                    self._need(e, (k, v), waits)
            if waits:
                self.streams[e].append((waits, None, None, 0))
        self.lastw = {}
        self.readers = {}

    def emit(self, block):
        def make(name):
            def body(engine):
                for waits, fn, semkey, inc in self.streams[name]:
                    for k, v in waits.items():
                        engine.wait_ge(self.sems[k], v)
                    if fn is None:
                        continue
                    inst = fn(engine)
                    inst.then_inc(self.sems[semkey], inc)
            return body
        block.tensor(make("pe"))
        block.scalar(make("act"))
        block.vector(make("dve"))
        block.gpsimd(make("pool"))
        block.sync(make("sp"))


class Mem:
    def __init__(self, big, start, nbytes):
        self.big, self.start, self.cap, self.off = big, start, nbytes, 0

    def alloc(self, shape, dtype):
        esz = 4 if dtype == F32 else 2
        n = int(np.prod(shape[1:]))
        nb = (n * esz + 63) // 64 * 64
        assert self.off + nb <= self.cap, ("SBUF overflow", self.off, nb, self.cap)
        a = (self.start + self.off) // 2
        ap = self.big[0:shape[0], a:a + n * esz // 2]
        self.off += nb
        if dtype == F32:
            ap = ap.bitcast(F32)
        if len(shape) == 3:
            ap = ap.rearrange("p (a b) -> p a b", a=shape[1])
        return ap

    def reset(self):
        self.off = 0


def build_program(cfg, stop_after=None):
    nc = bass.Bass("TRN2", target_bir_lowering=False)
    D, T, S, KC, FC, TH, NCH, NQT = cfg.D, cfg.T, cfg.S, cfg.KC, cfg.FC, cfg.TH, cfg.NCH, cfg.NQT
    NPF, ND2 = cfg.NPF, cfg.ND2

    def din(name, shape, dt=F32):
        return nc.dram_tensor(name, list(shape), dt, kind="ExternalInput").ap()

    def dscr(name, shape, dt):
        return nc.dram_tensor(name, list(shape), dt).ap()

    xT = din("xT", [KC, 128, T])
    gv = din("gv", [128, 4 * KC])
    w1 = [din("w1a", [FC, 128, KC * 128]), din("w1b", [FC, 128, KC * 128])]
    w3 = [din("w3a", [FC, 128, KC * 128]), din("w3b", [FC, 128, KC * 128])]
    w2 = [din("w2a", [KC, 128, FC * 128]), din("w2b", [KC, 128, FC * 128])]
    wA = din("wA", [128, KC * 768])
    wGt = din("wGt", [128, KC * 64])
    wV = din("wV", [128, KC * 512])
    NGC = 16 + 2 * KC
    wG = din("wG", [NGC, 128, KC * 128])
    wpa = din("wpa", [KC, 128, 16 * 128])
    wpb = din("wpb", [KC, 128, 16 * 128])
    wo = din("wo", [KC, 128, KC * 128])
    small = din("small", [128, 16 + 256])
    cf32 = din("cf32", [128, NCONST])
    identb = din("identb", [128, 128], BF16)
    outT = nc.dram_tensor("outT", [KC, 128, T], F32, kind="ExternalOutput").ap()
    dbg = nc.dram_tensor("dbg", [KC, 128, T], F32, kind="ExternalOutput").ap() if stop_after else None
    dbg2 = nc.dram_tensor("dbg2", [cfg.NP2, cfg.PR2, T], BF16, kind="ExternalOutput").ap() if stop_after else None

    xres = dscr("xres", [KC, 128, T], F32)
    gts = dscr("gts", [NGC, 128, T], BF16)
    h2loc = dscr("h2loc", [cfg.NP1, cfg.PR1, T], BF16)
    ex1mid = dscr("ex1mid", [cfg.NP1, 4 * cfg.PR1, T], BF16)
    ex1out = dscr("ex1out", [cfg.NP1, 8 * cfg.PR1, T], BF16)
    qkpre = dscr("qkpre", [2, 128, S], F32)
    sqk = dscr("sqk", [4, 128, S], BF16)
    gates_d = dscr("gates_d", [2, S], F32)
    vm_d = dscr("vm_d", [NCH, 128, 256], BF16)
    vs_d = dscr("vs_d", [2, NCH, 128, 128], BF16)
    rows_d = dscr("rows_d", [2, S], F32)
    yloc = dscr("yloc", [cfg.NP2, cfg.PR2, T], BF16)
    ex2mid = dscr("ex2mid", [cfg.NP2, 4 * cfg.PR2, T], BF16)
    ex2out = dscr("ex2out", [5 * cfg.NPF * 8 * cfg.PR2, T], BF16)
    ex2stage = dscr("ex2stage", [4, 8, 128, T], BF16)

    with ExitStack() as es:
        enter = es.enter_context
        CB = 16 * 1024
        HB = max(KC * T * 2, 64 * 1024)
        TOTAL = 200 * 1024
        big = enter(nc.sbuf_tensor("big", [128, TOTAL // 2], BF16))
        psall = enter(nc.psum_tensor("psall", [128, 8 * 512], F32))
        ps = [psall[:, i * 512:(i + 1) * 512] for i in range(8)]
        psb = [p.bitcast(BF16) for p in ps]
        sc = Sched(nc, enter)
        mA = Mem(big, 0, CB)
        mB = Mem(big, CB, HB)
        mC = Mem(big, CB + HB, TOTAL - CB - HB)
        P = lambda i: ("ps", i)

        c_gv = mA.alloc([128, 4 * KC], F32)
        c_small = mA.alloc([128, 16 + 256], F32)
        c_f32 = mA.alloc([128, NCONST], F32)
        c_id = mA.alloc([128, 128], BF16)
        sc.dma("sp", "c", lambda e: e.dma_start(out=c_gv, in_=gv), writes=["c_gv"])
        sc.dma("sp", "c", lambda e: e.dma_start(out=c_small, in_=small), writes=["c_small"])
        sc.dma("sp", "c", lambda e: e.dma_start(out=c_f32, in_=cf32), writes=["c_f32"])
        sc.dma("sp", "c", lambda e: e.dma_start(out=c_id, in_=identb), writes=["c_id"])
        sc.barrier()
        ones_f = c_f32[:, 0:128]
        triN = c_f32[:, 128:256]
        restN = c_f32[:, 256:384]
        maskneg = c_f32[:, 384:512]
        Umat = c_f32[:, 512:640]
        identf = c_f32[:, 640:768]
        dmask = [c_f32[:, 768 + 512 * j:768 + 512 * (j + 1)] for j in range(4)]
        gml_b = c_small[:, 16:272]
        c_tr = mA.alloc([128, 256], BF16)
        sc.op("dve", lambda e: e.tensor_copy(out=c_tr, in_=c_f32[:, 128:384]), writes=["c_tr"])
        sc.barrier()
        triR = c_tr[:, 0:128]
        restR = c_tr[:, 128:256]
        ones_b = mA.alloc([128, 128], BF16)
        sc.op("dve", lambda e: e.tensor_copy(out=ones_b, in_=c_f32[:, 0:128]), writes=["ones_b"])
        sc.barrier()

        def finish(dump=None):
            if dump is not None:
                for c in range(KC):
                    sc.dma("sp", "dump", lambda e, c=c: e.dma_start(out=dbg[c], in_=dump[c]))
                for p in range(cfg.NP2):
                    sc.dma("sp", "dump", lambda e, p=p: e.dma_start(out=dbg2[p], in_=yloc[p]))
            sc.barrier()
            with nc.Block() as block:
                sc.emit(block)
            return nc

        def rmsnorm(src, gidx, hT, dst_dram=None):
            xs = [mC.alloc([128, T], F32) for _ in range(6)]
            sq = [mC.alloc([128, T], BF16) for _ in range(2)]
            rstd = mC.alloc([128, T], F32)
            outs = [mC.alloc([128, T], F32) for _ in range(2)] if hT is None else None
            for c in range(KC):
                s = c % 6
                sc.dma("sp", ("xs", s), lambda e, c=c, s=s: e.dma_start(out=xs[s], in_=src[c]),
                       writes=[("xs", s)])
                q = c % 2
                sc.op("act", lambda e, s=s, q=q: e.activation(out=sq[q], in_=xs[s], func=AF.Square),
                      reads=[("xs", s)], writes=[("sq", q)])
                for th in range(TH):
                    sc.op("pe", lambda e, q=q, th=th, c=c: e.matmul(
                        ps[4 + th][:, :], ones_b, sq[q][:, th * 512:(th + 1) * 512],
                        start=(c == 0), stop=(c == KC - 1)),
                        reads=[("sq", q)], writes=[P(4 + th)])
            for th in range(TH):
                sl = slice(th * 512, (th + 1) * 512)
                sc.op("act", lambda e, th=th, sl=sl: e.activation(
                    out=rstd[:, sl], in_=ps[4 + th][:, :], func=AF.Sqrt, bias=EPS, scale=1.0 / D),
                    reads=[P(4 + th)], writes=[("rstd", th)])
                sc.op("dve", lambda e, sl=sl: e.reciprocal(out=rstd[:, sl], in_=rstd[:, sl]),
                      reads=[("rstd", th)], writes=[("rstd", th)])
            for c in range(KC):
                s = c % 6
                sc.dma("sp", ("xs", s), lambda e, c=c, s=s: e.dma_start(out=xs[s], in_=src[c]),
                       writes=[("xs", s)])
                gcol = c_gv[:, gidx * KC + c:gidx * KC + c + 1]
                rd = [("xs", s)] + [("rstd", th) for th in range(TH)]
                if hT is not None:
                    sc.op("dve", lambda e, c=c, s=s, gcol=gcol: e.scalar_tensor_tensor(
                        out=hT[:, c, :], in0=xs[s], scalar=gcol, in1=rstd, op0=ALU.mult, op1=ALU.mult),
                        reads=rd, writes=[("hT", c)])
                else:
                    o = c % 2
                    sc.op("dve", lambda e, o=o, s=s, gcol=gcol: e.scalar_tensor_tensor(
                        out=outs[o], in0=xs[s], scalar=gcol, in1=rstd, op0=ALU.mult, op1=ALU.mult),
                        reads=rd, writes=[("nout", o)])
                    sc.dma("sp", ("nout", o), lambda e, o=o, c=c: e.dma_start(out=dst_dram[c], in_=outs[o]),
                           reads=[("nout", o)])
            sc.barrier()
            mC.reset()

        def ffn(l, src, hT):
            ngmax = max(n for _, n in cfg.groups)
            aT = mC.alloc([128, ngmax, T], BF16)
            w1t = [mC.alloc([128, KC, 128], BF16) for _ in range(2)]
            w3t = [mC.alloc([128, KC, 128], BF16) for _ in range(2)]
            w2t = [mC.alloc([128, ngmax, 128], BF16) for _ in range(2)]
            sil = [mC.alloc([128, 512], F32) for _ in range(2)]
            xo = [mC.alloc([128, 512], F32) for _ in range(4)]
            cnt_u = 0
            cnt_d = 0
            issued = set()

            def wload(f):
                if f in issued or f >= FC:
                    return
                issued.add(f)
                s = f % 2
                sc.dma("pool", ("w1t", s), lambda e, f=f, s=s: e.dma_start(
                    out=w1t[s].rearrange("p a b -> p (a b)"), in_=w1[l][f], max_dma_last_dim=8192),
                    writes=[("w1t", s)])
                sc.dma("pool", ("w3t", s), lambda e, f=f, s=s: e.dma_start(
                    out=w3t[s].rearrange("p a b -> p (a b)"), in_=w3[l][f], max_dma_last_dim=8192),
                    writes=[("w3t", s)])
            for g, (f0, ng) in enumerate(cfg.groups):
                for fi in range(ng):
                    f = f0 + fi
                    s = f % 2
                    wload(f)
                    for th in range(TH):
                        sl = slice(th * 512, (th + 1) * 512)
                        bu = (cnt_u % 2) * 2
                        q = cnt_u % 2
                        cnt_u += 1

                        def mm(e, s=s, sl=sl, bu=bu):
                            for kc in range(KC):
                                e.matmul(ps[bu][:, :], w1t[s][:, kc, :], hT[:, kc, sl],
                                         start=(kc == 0), stop=(kc == KC - 1))
                            for kc in range(KC):
                                r = e.matmul(ps[bu + 1][:, :], w3t[s][:, kc, :], hT[:, kc, sl],
                                             start=(kc == 0), stop=(kc == KC - 1))
                            return r
                        sc.op("pe", mm, reads=[("w1t", s), ("w3t", s)], writes=[P(bu), P(bu + 1)])
                        sc.op("act", lambda e, q=q, bu=bu: e.activation(out=sil[q], in_=ps[bu][:, :], func=AF.Silu),
                              reads=[P(bu)], writes=[("sil", q)])
                        sc.op("dve", lambda e, q=q, bu=bu, fi=fi, sl=sl: e.tensor_tensor(
                            out=aT[:, fi, sl], in0=ps[bu + 1][:, :], in1=sil[q], op=ALU.mult),
                            reads=[P(bu + 1), ("sil", q)], writes=[("aT", fi, th)])
                wload(f0 + ng)
                wload(f0 + ng + 1)
                for o in range(KC):
                    s = o % 2
                    sc.dma("pool", ("w2t", s), lambda e, o=o, s=s, f0=f0, ng=ng: e.dma_start(
                        out=w2t[s][:, 0:ng, :].rearrange("p a b -> p (a b)"),
                        in_=w2[l][o][:, f0 * 128:(f0 + ng) * 128], max_dma_last_dim=8192),
                        writes=[("w2t", s)])
                    for th in range(TH):
                        sl = slice(th * 512, (th + 1) * 512)
                        bd = 4 + (cnt_d % 4)
                        xs_ = cnt_d % 4
                        cnt_d += 1

                        def mm2(e, s=s, sl=sl, bd=bd, ng=ng):
                            for fi in range(ng):
                                r = e.matmul(ps[bd][:, :], w2t[s][:, fi, :], aT[:, fi, sl],
                                             start=(fi == 0), stop=(fi == ng - 1))
                            return r
                        sc.op("pe", mm2, reads=[("w2t", s)] + [("aT", fi, th) for fi in range(ng)], writes=[P(bd)])
                        srcd = src if g == 0 else xres
                        sc.dma("sp", ("xo", xs_), lambda e, o=o, sl=sl, xs_=xs_, srcd=srcd: e.dma_start(
                            out=xo[xs_], in_=srcd[o][:, sl]), reads=[("xres", o, th)], writes=[("xo", xs_)])
                        sc.op("dve", lambda e, bd=bd, xs_=xs_: e.scalar_tensor_tensor(
                            out=xo[xs_], in0=ps[bd][:, :], scalar=0.5, in1=xo[xs_], op0=ALU.mult, op1=ALU.add),
                            reads=[P(bd), ("xo", xs_)], writes=[("xo", xs_)])
                        sc.dma("sp", ("xo", xs_), lambda e, o=o, sl=sl, xs_=xs_: e.dma_start(
                            out=xres[o][:, sl], in_=xo[xs_]), reads=[("xo", xs_)], writes=[("xres", o, th)])
            sc.barrier()
            mC.reset()

        hT = mB.alloc([128, KC, T], BF16)
        rmsnorm(xT, 0, hT)
        ffn(0, xT, hT)
        if stop_after == "ffn1":
            return finish(xres)

        rmsnorm(xres, 1, hT)
        KPP = cfg.KPP
        for i in range(cfg.NP1):
            sc.dma("sp", "c", lambda e, i=i: e.dma_start(
                out=h2loc[i].rearrange("(k p) t -> p k t", p=128), in_=hT[:, i * KPP:(i + 1) * KPP, :]),
                writes=[("h2loc", i)])
        wA_t = mC.alloc([128, KC, 768], BF16)
        wGt_t = mC.alloc([128, KC, 64], BF16)
        wV_t = mC.alloc([128, KC, 512], BF16)
        head_mark = mC.off
        hw_dmas = []
        for kc0 in range(0, KC, 2):
            hw_dmas.append(lambda e, kc0=kc0: e.dma_start(
                out=wA_t[:, kc0:kc0 + 2, :].rearrange("p a b -> p (a b)"),
                in_=wA[:, kc0 * 768:(kc0 + 2) * 768], max_dma_last_dim=6144))
        hw_dmas.append(lambda e: e.dma_start(out=wGt_t.rearrange("p a b -> p (a b)"), in_=wGt, max_dma_last_dim=8192))
        for kc0 in range(0, KC, 2):
            hw_dmas.append(lambda e, kc0=kc0: e.dma_start(
                out=wV_t[:, kc0:kc0 + 2, :].rearrange("p a b -> p (a b)"),
                in_=wV[:, kc0 * 512:(kc0 + 2) * 512], max_dma_last_dim=8192))
        for i in range(cfg.NP1):
            sc.dma("pool", ("cc", i % 16), lambda e, i=i: e.collective_compute(
                "AllGather", ALU.bypass, replica_groups=[[0, 1, 2, 3], [4, 5, 6, 7]],
                ins=[h2loc[i]], outs=[ex1mid[i]]),
                reads=[("h2loc", i)], writes=[("ex1mid", i)], inc=1)
        NWG = 4
        wgt = [mC.alloc([128, KC, 128], BF16) for _ in range(NWG)]
        gst = [mC.alloc([128, 512], BF16) for _ in range(4)]
        cnt = 0
        def ex1_stage2(i):
            sc.dma("pool", ("cc", i % 16), lambda e, i=i: e.collective_compute(
                "AllGather", ALU.bypass, replica_groups=[[0, 4], [1, 5], [2, 6], [3, 7]],
                ins=[ex1mid[i]], outs=[ex1out[i]]),
                reads=[("ex1mid", i)], writes=[("ex1out", i)], inc=1)
        st2_next = 0
        for f in range(NGC):
            s = f % NWG
            sc.dma("pool", ("wgt", s), lambda e, f=f, s=s: e.dma_start(
                out=wgt[s].rearrange("p a b -> p (a b)"), in_=wG[f], max_dma_last_dim=8192), writes=[("wgt", s)])
            if f >= 8 and f % 2 == 0 and st2_next < cfg.NP1:
                ex1_stage2(st2_next)
                st2_next += 1
            if f >= 3 and hw_dmas:
                sc.dma("pool", "wA", hw_dmas.pop(0))
            for th in range(TH):
                sl = slice(th * 512, (th + 1) * 512)
                b = cnt % 4
                cnt += 1

                def mm(e, s=s, sl=sl, b=b):
                    for kc in range(KC):
                        r = e.matmul(ps[b][:, :], wgt[s][:, kc, :], hT[:, kc, sl], start=(kc == 0), stop=(kc == KC - 1))
                    return r
                sc.op("pe", mm, reads=[("wgt", s)], writes=[P(b)])
                sc.op("act", lambda e, b=b: e.activation(out=gst[b], in_=ps[b][:, :], func=AF.Sigmoid),
                      reads=[P(b)], writes=[("gst", b)])
                sc.dma("sp", ("gst", b), lambda e, b=b, f=f, sl=sl: e.dma_start(out=gts[f][:, sl], in_=gst[b]),
                       reads=[("gst", b)])
        while st2_next < cfg.NP1:
            ex1_stage2(st2_next)
            st2_next += 1
        while hw_dmas:
            sc.dma("pool", "wA", hw_dmas.pop(0))
        sc.barrier()
        if stop_after == "ex1":
            return finish(xres)

        mB.reset()
        mC.off = head_mark
        h2b = [mB.alloc([128, KC, 512], BF16) for _ in range(2)] if 2 * KC * 512 * 2 <= HB else \
            [mC.alloc([128, KC, 512], BF16) for _ in range(2)]
        stf = [mC.alloc([128, 512], F32) for _ in range(2)]
        stb = [mC.alloc([128, 512], BF16) for _ in range(4)]
        stg = [mC.alloc([64, 512], F32) for _ in range(2)]
        vst = [mC.alloc([128, 512], BF16) for _ in range(2)]
        pcnt = 0
        for tb in range(S // 512):
            r = (tb * 512) // T
            toff = (tb * 512) % T
            hs = tb % 2
            cs = slice(tb * 512, (tb + 1) * 512)
            def load_blk(tb_):
                r_ = (tb_ * 512) // T
                toff_ = (tb_ * 512) % T
                hs_ = tb_ % 2
                for i in range(cfg.NP1):
                    sc.dma("sp", ("h2b", hs_, i % 4), lambda e, i=i, r_=r_, toff_=toff_, hs_=hs_: e.dma_start(
                        out=h2b[hs_][:, i * KPP:(i + 1) * KPP, :],
                        in_=ex1out[i][r_ * cfg.PR1:(r_ + 1) * cfg.PR1, toff_:toff_ + 512].rearrange("(k p) t -> p k t", p=128)),
                        writes=[("h2b", hs_, i)])
            if tb == 0:
                load_blk(0)
            if tb + 1 < S // 512:
                load_blk(tb + 1)
            hkeys = [("h2b", hs, i) for i in range(cfg.NP1)]
            for j in range(7):
                b = pcnt % 4
                pcnt += 1
                if j < 6:
                    def mm(e, j=j, hs=hs, b=b):
                        for kc in range(KC):
                            r_ = e.matmul(ps[b][:, :], wA_t[:, kc, j * 128:(j + 1) * 128], h2b[hs][:, kc, :],
                                          start=(kc == 0), stop=(kc == KC - 1))
                        return r_
                else:
                    def mm(e, hs=hs, b=b):
                        for kc in range(KC):
                            r_ = e.matmul(ps[b][0:64, :], wGt_t[:, kc, :], h2b[hs][:, kc, :],
                                          start=(kc == 0), stop=(kc == KC - 1))
                        return r_
                sc.op("pe", mm, reads=hkeys, writes=[P(b)])
                if j < 2:
                    q = j
                    sc.op("act", lambda e, q=q, b=b: e.activation(out=stf[q], in_=ps[b][:, :], func=AF.Copy),
                          reads=[P(b)], writes=[("stf", q)])
                    sc.dma("sp", ("stf", q), lambda e, q=q, j=j, cs=cs: e.dma_start(out=qkpre[j][:, cs], in_=stf[q]),
                           reads=[("stf", q)])
                elif j < 6:
                    q = j - 2
                    sc.op("dve", lambda e, q=q, b=b: e.tensor_copy(out=stb[q], in_=ps[b][:, :]),
                          reads=[P(b)], writes=[("stb", q)])
                    sc.dma("sp", ("stb", q), lambda e, q=q, cs=cs: e.dma_start(out=sqk[q][:, cs], in_=stb[q]),
                           reads=[("stb", q)])
                else:
                    q = tb % 2
                    sc.op("act", lambda e, q=q, b=b: e.activation(out=stg[q], in_=ps[b][0:64, :], func=AF.Copy),
                          reads=[P(b)], writes=[("stg", q)])
                    sc.dma("sp", ("stg", q), lambda e, q=q, cs=cs: e.dma_start(out=gates_d[0:1, cs], in_=stg[q][0:1, :]),
                           reads=[("stg", q)])
                    sc.dma("sp", ("stg", q), lambda e, q=q, cs=cs: e.dma_start(out=gates_d[1:2, cs], in_=stg[q][32:33, :]),
                           reads=[("stg", q)])
            for sub in range(4):
                ch = tb * 4 + sub
                b = 4 + ch % 4
                vq = ch % 2

                def mmv(e, hs=hs, b=b, sub=sub):
                    for kc in range(KC):
                        r_ = e.matmul(ps[b][:, :], h2b[hs][:, kc, sub * 128:(sub + 1) * 128], wV_t[:, kc, :],
                                      start=(kc == 0), stop=(kc == KC - 1))
                    return r_
                sc.op("pe", mmv, reads=hkeys, writes=[P(b)])
                sc.op("dve", lambda e, b=b, vq=vq: e.tensor_copy(out=vst[vq], in_=ps[b][:, :]),
                      reads=[P(b)], writes=[("vst", vq)])
                sc.dma("sp", ("vst", vq), lambda e, vq=vq, ch=ch: e.dma_start(out=vm_d[ch], in_=vst[vq][:, 0:256]),
                       reads=[("vst", vq)])
                for h in range(2):
                    sc.dma("sp", ("vst", vq), lambda e, vq=vq, ch=ch, h=h: e.dma_start(
                        out=vs_d[h][ch], in_=vst[vq][:, 256 + 128 * h:384 + 128 * h]), reads=[("vst", vq)])
        sc.barrier()
        mB.reset()
        mC.reset()
        if stop_after == "heads":
            return finish(xres)

        def yloc_ap(fc, tok0, n):
            dc = tok0 // T
            to = tok0 % T
            piece = fc * NPF + dc // ND2
            row0 = (dc % ND2) * 128
            return piece, yloc[piece][row0:row0 + 128, to:to + n]

        ywrites = {p: [] for p in range(cfg.NP2)}
        qT = mB.alloc([128, S], BF16)
        kT = mB.alloc([128, S], BF16)
        negGb = mC.alloc([128, S], F32)
        vext = mC.alloc([128, NCH, 264], BF16)
        for c0 in range(0, NCH, 8):
            n = min(8, NCH - c0)
            sc.dma("sp", "c", lambda e, c0=c0, n=n: e.dma_start(
                out=vext[:, c0:c0 + n, 0:256], in_=vm_d[c0:c0 + n].rearrange("c p d -> p c d")),
                writes=[("vext", c0)])
        sc.op("dve", lambda e: e.memset(vext[:, :, 256:257], 1.0), writes=["vones"])
        vkeys = [("vext", c0) for c0 in range(0, NCH, 8)] + ["vones"]
        SEG = min(2048, S)
        pre = [mC.alloc([128, SEG + 4], F32) for _ in range(2)]
        acc = [mC.alloc([128, SEG], F32) for _ in range(2)]
        it = 0
        for j in range(2):
            for sg in range(S // SEG):
                s = it % 2
                it += 1
                if sg == 0:
                    sc.op("dve", lambda e, s=s: e.memset(pre[s][:, 0:3], 0.0), writes=[("pre0", s)])
                    sc.dma("sp", ("pre", s), lambda e, s=s, j=j: e.dma_start(out=pre[s][:, 3:3 + SEG], in_=qkpre[j][:, 0:SEG]),
                           reads=[("pre0", s)], writes=[("pre", s)])
                else:
                    sc.dma("sp", ("pre", s), lambda e, s=s, j=j, sg=sg: e.dma_start(
                        out=pre[s][:, 0:3 + SEG], in_=qkpre[j][:, sg * SEG - 3:(sg + 1) * SEG]),
                        reads=[("pre0", s)], writes=[("pre", s)])
                wc = lambda i, j=j: c_small[:, 4 * j + i:4 * j + i + 1]
                sc.op("dve", lambda e, s=s, wc=wc: e.tensor_scalar(
                    out=acc[s], in0=pre[s][:, 3:3 + SEG], scalar1=wc(3), scalar2=None, op0=ALU.mult),
                    reads=[("pre", s), ("pre0", s)], writes=[("acc", s)])
                for i in (2, 1, 0):
                    sc.op("dve", lambda e, s=s, wc=wc, i=i: e.scalar_tensor_tensor(
                        out=acc[s], in0=pre[s][:, i:i + SEG], scalar=wc(i), in1=acc[s], op0=ALU.mult, op1=ALU.add),
                        reads=[("pre", s), ("acc", s)], writes=[("acc", s)])
                seg = slice(sg * SEG, (sg + 1) * SEG)
                if j == 0:
                    sc.op("act", lambda e, s=s, seg=seg: e.activation(out=qT[:, seg], in_=acc[s], func=AF.Silu),
                          reads=[("acc", s)], writes=[("qT", sg), ("pre", s), ("pre0", s)])
                else:
                    sc.op("act", lambda e, s=s: e.activation(out=acc[s], in_=acc[s], func=AF.Silu),
                          reads=[("acc", s)], writes=[("acc", s)])
                    sc.op("dve", lambda e, s=s, seg=seg: e.tensor_scalar(
                        out=kT[:, seg], in0=acc[s], scalar1=128.0 ** -0.5, scalar2=None, op0=ALU.mult),
                        reads=[("acc", s)], writes=[("kT", sg), ("pre", s), ("pre0", s)])
        qkkeys = [("qT", sg) for sg in range(S // SEG)] + [("kT", sg) for sg in range(S // SEG)]
        def cm():
            return mC.alloc([NCH, 128], F32)
        gi, gf, spt, Floc, Fp, gg, Gloc, Gt, tmpc = [cm() for _ in range(9)]
        colt = mC.alloc([128, 8], F32)
        rowt = mC.alloc([1, 4 * NCH], F32)
        Mb = mC.alloc([128, 2 * NCH], F32)
        gcol = mC.alloc([128, NCH], F32)
        ecol = mC.alloc([128, NCH], F32)
        srccol = mC.alloc([128, NCH], F32)
        csb = mC.alloc([128, NCH], F32)
        tcol = mC.alloc([128, NCH], F32)
        sc.dma("sp", "c", lambda e: e.dma_start(out=gi, in_=gates_d[0].rearrange("(c p) -> c p", p=128)), writes=["gi"])
        sc.dma("sp", "c", lambda e: e.dma_start(out=gf, in_=gates_d[1].rearrange("(c p) -> c p", p=128)), writes=["gf"])
        sc.op("dve", lambda e: e.tensor_scalar(out=colt[:, 0:1], in0=c_small[:, 9:10], scalar1=-1.0, scalar2=None, op0=ALU.mult),
              writes=["nbf"])
        sc.op("act", lambda e: e.activation(out=spt, in_=gf, func=AF.Exp, bias=colt[0:NCH, 0:1], scale=-1.0),
              reads=["gf", "nbf"], writes=["spt"])
        sc.op("act", lambda e: e.activation(out=spt, in_=spt, func=AF.Ln, bias=1.0, scale=1.0), reads=["spt"], writes=["spt"])
        sc.op("dve", lambda e: e.tensor_tensor_scan(out=Floc, data0=ones_f[0:NCH, :], data1=spt, initial=0.0,
                                                    op0=ALU.mult, op1=ALU.add), reads=["spt"], writes=["Floc"])
        sc.op("pe", lambda e: e.matmul(ps[0][0:NCH, 0:1], Umat[0:NCH, 0:NCH], Floc[:, 127:128], start=True, stop=True),
              reads=["Floc"], writes=[P(0)])
        sc.op("act", lambda e: e.activation(out=colt[0:NCH, 1:2], in_=ps[0][0:NCH, 0:1], func=AF.Copy), reads=[P(0)], writes=["offs"])
        sc.op("dve", lambda e: e.tensor_scalar(out=Fp, in0=Floc, scalar1=colt[0:NCH, 1:2], scalar2=None, op0=ALU.add),
              reads=["Floc", "offs"], writes=["Fp"])
        sc.op("dve", lambda e: e.scalar_tensor_tensor(out=gg, in0=gi, scalar=c_small[0:NCH, 8:9], in1=Fp, op0=ALU.add, op1=ALU.add),
              reads=["gi", "Fp"], writes=["gg"])
        sc.op("dve", lambda e: e.tensor_tensor_scan(out=Gloc, data0=gg, data1=gg, initial=-1.0e30, op0=ALU.max, op1=ALU.max),
              reads=["gg"], writes=["Gloc"])
        sc.op("pe", lambda e: e.matmul(ps[1][0:1, 0:NCH], Gloc[:, 127:128], identf[0:NCH, 0:NCH], start=True, stop=True),
              reads=["Gloc"], writes=[P(1)])
        sc.op("act", lambda e: e.activation(out=rowt[:, 0:NCH], in_=ps[1][0:1, 0:NCH], func=AF.Copy), reads=[P(1)], writes=["rowmax"])
        sc.op("dve", lambda e: e.tensor_tensor_scan(out=rowt[:, NCH:2 * NCH], data0=rowt[:, 0:NCH], data1=rowt[:, 0:NCH],
                                                    initial=0.0, op0=ALU.max, op1=ALU.max), reads=["rowmax"], writes=["Mrow"])
        sc.op("dve", lambda e: e.memset(rowt[:, 2 * NCH:2 * NCH + 1], 0.0), writes=["Mp0"])
        sc.op("dve", lambda e: e.tensor_copy(out=rowt[:, 2 * NCH + 1:3 * NCH], in_=rowt[:, NCH:2 * NCH - 1]),
              reads=["Mrow", "Mp0"], writes=["Mprow"])
        sc.op("pe", lambda e: e.matmul(ps[2][:, 0:2 * NCH], ones_f[0:1, 0:128], rowt[:, NCH:3 * NCH], start=True, stop=True),
              reads=["Mrow", "Mprow"], writes=[P(2)])
        sc.op("act", lambda e: e.activation(out=Mb, in_=ps[2][:, 0:2 * NCH], func=AF.Copy), reads=[P(2)], writes=["Mb"])
        sc.op("pe", lambda e: e.matmul(ps[3][0:NCH, 0:1], rowt[:, 2 * NCH:3 * NCH], ones_f[0:1, 0:1], start=True, stop=True),
              reads=["Mprow"], writes=[P(3)])
        sc.op("act", lambda e: e.activation(out=colt[0:NCH, 2:3], in_=ps[3][0:NCH, 0:1], func=AF.Copy), reads=[P(3)], writes=["Pcol"])
        sc.op("dve", lambda e: e.tensor_scalar(out=Gt, in0=Gloc, scalar1=colt[0:NCH, 2:3], scalar2=None, op0=ALU.max),
              reads=["Gloc", "Pcol"], writes=["Gt"])
        sc.op("dve", lambda e: e.tensor_scalar(out=tmpc, in0=Gt, scalar1=-1.0, scalar2=None, op0=ALU.mult),
              reads=["Gt"], writes=["tmpc"])
        sc.dma("sp", "rowsd", lambda e: e.dma_start(out=rows_d[0].rearrange("(c p) -> c p", p=128), in_=tmpc),
               reads=["tmpc"], writes=["rows0"])
        sc.dma("sp", "negGb", lambda e: e.dma_start(out=negGb, in_=rows_d[0:1, :].broadcast_to([128, S])),
               reads=["rows0"], writes=["negGb"])
        sc.op("dve", lambda e: e.tensor_tensor(out=Fp, in0=Fp, in1=Gt, op=ALU.subtract), reads=["Fp", "Gt", "gg"], writes=["Fp"])
        sc.op("pe", lambda e: e.matmul(ps[0][:, 0:NCH], gg, identf[0:NCH, 0:NCH], start=True, stop=True),
              reads=["gg"], writes=[P(0)])
        sc.op("act", lambda e: e.activation(out=gcol, in_=ps[0][:, 0:NCH], func=AF.Copy), reads=[P(0)], writes=["gcol"])
        sc.op("pe", lambda e: e.matmul(ps[1][:, 0:NCH], Fp, identf[0:NCH, 0:NCH], start=True, stop=True),
              reads=["Fp"], writes=[P(1)])
        sc.op("act", lambda e: e.activation(out=ecol, in_=ps[1][:, 0:NCH], func=AF.Exp), reads=[P(1)], writes=["ecol"])
        sc.op("dve", lambda e: e.tensor_tensor(out=tcol, in0=gcol, in1=Mb[:, 0:NCH], op=ALU.subtract), reads=["gcol", "Mb"], writes=["tcol"])
        sc.op("act", lambda e: e.activation(out=srccol, in_=tcol, func=AF.Exp), reads=["tcol"], writes=["srccol"])
        sc.op("dve", lambda e: e.tensor_tensor(out=tcol, in0=Mb[:, NCH:2 * NCH], in1=Mb[:, 0:NCH], op=ALU.subtract),
              reads=["Mb", "srccol"], writes=["tcol"])
        sc.op("act", lambda e: e.activation(out=csb, in_=tcol, func=AF.Exp), reads=["tcol"], writes=["csb"])
        argd = [mC.alloc([128, 128], F32) for _ in range(2)]
        dect = [mC.alloc([128, 128], F32) for _ in range(2)]
        wTt = [mC.alloc([128, 128], BF16) for _ in range(2)]
        qd = [mC.alloc([128, 128], BF16) for _ in range(2)]
        ksc = [mC.alloc([128, 128], BF16) for _ in range(2)]
        hh = [mC.alloc([128, 256], F32) for _ in range(2)]
        junk = mC.alloc([128, 256], F32)
        ya = [mC.alloc([128, 256], BF16) for _ in range(2)]
        yaT = [mC.alloc([128, 2, 128], BF16) for _ in range(2)]
        sm = [mC.alloc([128, 4], F32) for _ in range(2)]
        C32 = mC.alloc([128, 264], F32)
        Cbf = mC.alloc([128, 264], BF16)
        sc.op("dve", lambda e: e.memset(C32, 0.0), writes=["C32"])
        sc.op("dve", lambda e: e.memset(Cbf, 0.0), writes=["Cbf"])
        def m_head(c):
            s = c % 2
            cs = slice(c * 128, (c + 1) * 128)
            bS = s
            sc.op("pe", lambda e, cs=cs, bS=bS: e.matmul(ps[bS][:, 0:128], kT[:, cs], qT[:, cs], start=True, stop=True),
                  reads=qkkeys, writes=[P(bS)])
            sc.op("dve", lambda e, s=s, cs=cs: e.tensor_tensor(out=argd[s], in0=negGb[:, cs], in1=maskneg, op=ALU.add),
                  reads=["negGb"], writes=[("argd", s)])
            sc.op("act", lambda e, s=s, c=c: e.activation(out=argd[s], in_=argd[s], func=AF.Exp, bias=gcol[:, c:c + 1], scale=1.0),
                  reads=[("argd", s), "gcol"], writes=[("argd", s)])
            sc.op("dve", lambda e, s=s, bS=bS: e.tensor_tensor(out=wTt[s], in0=ps[bS][:, 0:128], in1=argd[s], op=ALU.mult),
                  reads=[P(bS), ("argd", s)], writes=[("wT", s)])
            sc.op("act", lambda e, s=s, c=c, cs=cs: e.activation(out=dect[s], in_=negGb[:, cs], func=AF.Exp,
                                                                 bias=Mb[:, NCH + c:NCH + c + 1], scale=1.0),
                  reads=["negGb", "Mb"], writes=[("dec", s)])
            sc.op("dve", lambda e, s=s, cs=cs: e.tensor_tensor(out=qd[s], in0=qT[:, cs], in1=dect[s], op=ALU.mult),
                  reads=[("dec", s)] + qkkeys, writes=[("qd", s)])
            sc.op("pe", lambda e, cs=cs: e.transpose(psb[6][:, 0:128], kT[:, cs], c_id), reads=qkkeys, writes=[P(6)])
            sc.op("dve", lambda e, s=s, c=c: e.tensor_scalar(out=ksc[s], in0=psb[6][:, 0:128], scalar1=srccol[:, c:c + 1],
                                                             scalar2=None, op0=ALU.mult),
                  reads=[P(6), "srccol"], writes=[("ksc", s)])

        def m_mid(c):
            s = c % 2
            bO = 2 + s

            def mmO(e, s=s, c=c, bO=bO):
                e.matmul(ps[bO][:, 0:257], qd[s], Cbf[:, 0:257], start=True, stop=False)
                return e.matmul(ps[bO][:, 0:257], wTt[s], vext[:, c, 0:257], start=False, stop=True)
            sc.op("pe", mmO, reads=[("qd", s), ("wT", s), "Cbf"] + vkeys, writes=[P(bO)])
            sc.op("pe", lambda e, s=s, c=c: e.matmul(ps[7][:, 0:257], ksc[s], vext[:, c, 0:257], start=True, stop=True),
                  reads=[("ksc", s)] + vkeys, writes=[P(7)])
            sc.op("dve", lambda e, c=c: e.scalar_tensor_tensor(out=C32[:, 0:257], in0=C32[:, 0:257], scalar=csb[:, c:c + 1],
                                                               in1=ps[7][:, 0:257], op0=ALU.mult, op1=ALU.add),
                  reads=[P(7), "C32", "csb"], writes=["C32"])
            sc.op("act", lambda e: e.activation(out=Cbf[:, 0:257], in_=C32[:, 0:257], func=AF.Copy), reads=["C32"], writes=["Cbf"])

        def m_tail(c):
            s = c % 2
            bO, bT = 2 + s, 4 + s
            sc.op("dve", lambda e, s=s, bO=bO: e.tensor_scalar(
                out=sm[s][:, 3:4], in0=ps[bO][:, 256:257], scalar1=-1.0, scalar2=None, op0=ALU.mult),
                reads=[P(bO)], writes=[("nden", s)])
            sc.op("dve", lambda e, s=s, c=c, bO=bO: e.scalar_tensor_tensor(
                out=sm[s][:, 0:1], in0=ps[bO][:, 256:257], scalar=ecol[:, c:c + 1], in1=sm[s][:, 3:4],
                op0=ALU.max, op1=ALU.max), reads=[P(bO), "ecol", ("nden", s)], writes=[("dd", s)])
            sc.op("dve", lambda e, s=s: e.reciprocal(out=sm[s][:, 0:1], in_=sm[s][:, 0:1]), reads=[("dd", s)], writes=[("dd", s)])
            sc.op("act", lambda e, s=s, bO=bO: e.activation(out=hh[s], in_=ps[bO][:, 0:256], func=AF.Copy, scale=sm[s][:, 0:1]),
                  reads=[P(bO), ("dd", s)], writes=[("hh", s)])
            sc.op("act", lambda e, s=s: e.activation(out=junk, in_=hh[s], func=AF.Square, accum_out=sm[s][:, 1:2]),
                  reads=[("hh", s)], writes=[("ss", s), "junk"])
            sc.op("act", lambda e, s=s: e.activation(out=sm[s][:, 2:3], in_=sm[s][:, 1:2], func=AF.Ln, bias=c_eps[:, 0:1], scale=1.0 / 256),
                  reads=[("ss", s)], writes=[("rs", s)])
            sc.op("act", lambda e, s=s: e.activation(out=sm[s][:, 2:3], in_=sm[s][:, 2:3], func=AF.Exp, scale=-0.5),
                  reads=[("rs", s)], writes=[("rs", s)])
            sc.op("dve", lambda e, s=s: e.scalar_tensor_tensor(out=ya[s], in0=hh[s], scalar=sm[s][:, 2:3], in1=gml_b,
                                                               op0=ALU.mult, op1=ALU.mult),
                  reads=[("hh", s), ("rs", s)], writes=[("ya", s)])

            def trY(e, s=s, bT=bT):
                e.transpose(psb[bT][:, 0:128], ya[s][:, 0:128], c_id)
                return e.transpose(psb[bT][:, 128:256], ya[s][:, 128:256], c_id)
            sc.op("pe", trY, reads=[("ya", s)], writes=[P(bT)])
            sc.op("act", lambda e, s=s, bT=bT: e.activation(out=yaT[s].rearrange("p a b -> p (a b)"), in_=psb[bT][:, 0:256], func=AF.Copy),
                  reads=[P(bT)], writes=[("yaT", s)])
            for j in range(2):
                piece, dst = yloc_ap(j, c * 128, 128)
                key = ("yloc", piece, c)
                ywrites[piece].append(key)
                sc.dma("sp", ("yaT", s), lambda e, s=s, j=j, dst=dst: e.dma_start(out=dst, in_=yaT[s][:, j, :]),
                       reads=[("yaT", s)], writes=[key])

        c_eps = mC.alloc([128, 1], F32)
        sc.op("dve", lambda e: e.memset(c_eps, EPS), writes=["c_eps"])
        m_head(0)
        for c in range(NCH + 1):
            if c + 1 < NCH:
                m_head(c + 1)
            if c < NCH:
                m_mid(c)
            if c >= 1:
                m_tail(c - 1)
        sc.barrier()
        mB.reset()
        mC.reset()
        if stop_after == "mlstm":
            return finish(xres)

        def exch2_piece(p, keys):
            sc.dma("pool", ("cc", p % 16), lambda e, p=p: e.collective_compute(
                "AllGather", ALU.bypass, replica_groups=[[0, 1, 2, 3], [4, 5, 6, 7]],
                ins=[yloc[p]], outs=[ex2mid[p]]), reads=keys, writes=[("ex2mid", p)], inc=1)
            sc.dma("pool", ("cc", p % 16), lambda e, p=p: e.collective_compute(
                "AllGather", ALU.bypass, replica_groups=[[0, 4], [1, 5], [2, 6], [3, 7]],
                ins=[ex2mid[p]], outs=[ex2out[p * 8 * cfg.PR2:(p + 1) * 8 * cfg.PR2, :]]),
                reads=[("ex2mid", p)], writes=[("ex2out", p)], inc=1)

        for p in range(2 * NPF):
            exch2_piece(p, [])
        sqT = [mB.alloc([128, S], BF16), None]
        skT = [mB.alloc([128, S], BF16), None]
        sqT[1] = mC.alloc([128, S], BF16)
        skT[1] = mC.alloc([128, S], BF16)
        vsh = [mC.alloc([128, NCH, 128], BF16) for _ in range(2)]
        for h in range(2):
            sc.dma("sp", "c", lambda e, h=h: e.dma_start(out=sqT[h], in_=sqk[2 * h]), writes=[("sq", h)])
            sc.dma("sp", "c", lambda e, h=h: e.dma_start(out=skT[h], in_=sqk[2 * h + 1]), writes=[("sk", h)])
            for c0 in range(0, NCH, 16):
                n = min(16, NCH - c0)
                sc.dma("sp", "c", lambda e, h=h, c0=c0, n=n: e.dma_start(
                    out=vsh[h][:, c0:c0 + n, :], in_=vs_d[h][c0:c0 + n].rearrange("c p d -> p c d")),
                    writes=[("vsh", h, c0)])
        vshkeys = [[("vsh", h, c0) for c0 in range(0, NCH, 16)] for h in range(2)]
        NSL = 4
        spt_ = [mC.alloc([128, 1024], F32) for _ in range(NSL)]
        sptb = [mC.alloc([128, 1024], BF16) for _ in range(NSL)]
        tt_ = [mC.alloc([128, 1024], F32) for _ in range(NSL)]
        Wt_ = [mC.alloc([128, 1024], BF16) for _ in range(NSL)]
        ost = mC.alloc([128, 1024], BF16)
        SCL = 128.0 ** -0.5
        units = []
        for qt in range(NQT):
            nkb = 4 * qt + 4
            for kb in range(nkb - 1, -1, -1):
                units.append((qt, kb, kb == nkb - 1, kb == 0, (kb - 4 * qt) if kb >= 4 * qt else None))
        NU = len(units)
        zpair = [psall[:, 0:1024], psall[:, 1024:2048]]
        apair = psall[:, 4 * 512:6 * 512]
        opair = psall[:, 6 * 512:8 * 512]
        hs_ = [slice(0, 512), slice(512, 1024)]

        def S1(u):
            qt, kb, first, last, dj = units[u]
            zb = (u % 2) * 2

            def mm(e, kb=kb, qt=qt, zb=zb):
                for h in range(2):
                    r_ = e.matmul(ps[zb + h][:, :], skT[h][:, kb * 128:(kb + 1) * 128], sqT[h][:, qt * 512:(qt + 1) * 512],
                                  start=True, stop=True)
                return r_
            sc.op("pe", mm, reads=[("sq", 0), ("sk", 0), ("sq", 1), ("sk", 1)], writes=[P(zb), P(zb + 1)])

        def S2(u):
            qt, kb, first, last, dj = units[u]
            zb, sl = (u % 2) * 2, u % NSL
            zp = zpair[u % 2]
            sc.op("act", lambda e, zp=zp, sl=sl: e.activation(out=spt_[sl], in_=zp, func=AF.Exp, scale=SCL),
                  reads=[P(zb), P(zb + 1)], writes=[("spt", sl)])
            sc.op("act", lambda e, sl=sl: e.activation(out=sptb[sl], in_=spt_[sl], func=AF.Ln, bias=1.0, scale=1.0),
                  reads=[("spt", sl)], writes=[("sptb", sl)])
            if dj is not None:
                for h in range(2):
                    sc.op("dve", lambda e, sl=sl, dj=dj, h=h: e.tensor_tensor(out=sptb[sl][:, hs_[h]], in0=sptb[sl][:, hs_[h]],
                                                                           in1=dmask[dj], op=ALU.mult),
                          reads=[("sptb", sl)], writes=[("sptb", sl)])

        def S3(u):
            qt, kb, first, last, dj = units[u]
            zb, sl = (u % 2) * 2, u % NSL
            zp = zpair[u % 2]
            sc.op("dve", lambda e, zp=zp, sl=sl: e.scalar_tensor_tensor(
                out=tt_[sl], in0=zp, scalar=SCL, in1=sptb[sl], op0=ALU.mult, op1=ALU.subtract),
                reads=[P(zb), P(zb + 1), ("sptb", sl)], writes=[("tt", sl)])

            def mm(e, sl=sl, first=first):
                for h in range(2):
                    r_ = e.matmul(ps[4 + h][:, :], triR, sptb[sl][:, hs_[h]], start=first, stop=False)
                return r_
            sc.op("pe", mm, reads=[("sptb", sl)], writes=[P(4), P(5)])

        def S4(u):
            qt, kb, first, last, dj = units[u]
            sl = u % NSL
            sc.op("dve", lambda e, sl=sl: e.tensor_tensor(out=tt_[sl], in0=apair, in1=tt_[sl], op=ALU.add),
                  reads=[P(4), P(5), ("tt", sl)], writes=[("tt", sl)])
            sc.op("act", lambda e, sl=sl: e.activation(out=Wt_[sl], in_=tt_[sl], func=AF.Exp), reads=[("tt", sl)], writes=[("W", sl)])
            if dj is not None:
                for h in range(2):
                    sc.op("dve", lambda e, sl=sl, dj=dj, h=h: e.tensor_tensor(out=Wt_[sl][:, hs_[h]], in0=Wt_[sl][:, hs_[h]],
                                                                           in1=dmask[dj], op=ALU.mult),
                          reads=[("W", sl)], writes=[("W", sl)])

        def S5a(u):
            qt, kb, first, last, dj = units[u]
            sl = u % NSL

            def mm(e, sl=sl, last=last):
                for h in range(2):
                    r_ = e.matmul(ps[4 + h][:, :], restR, sptb[sl][:, hs_[h]], start=False, stop=last)
                return r_
            sc.op("pe", mm, reads=[("sptb", sl), ("tt", sl)], writes=[P(4), P(5)])

        def S5(u):
            qt, kb, first, last, dj = units[u]
            sl = u % NSL

            def mm(e, sl=sl, kb=kb, first=first, last=last):
                for h in range(2):
                    r_ = e.matmul(ps[6 + h][:, :], vsh[h][:, kb, :], Wt_[sl][:, hs_[h]], start=first, stop=last)
                return r_
            sc.op("pe", mm, reads=[("W", sl)] + vshkeys[0] + vshkeys[1], writes=[P(6), P(7)])
            if last:
                sc.op("act", lambda e: e.activation(out=ost, in_=opair, func=AF.Copy), reads=[P(6), P(7)], writes=["ost"])
                for h in range(2):
                    piece, dst = yloc_ap(2 + h, qt * 512, 512)
                    key = ("yloc", piece, qt)
                    ywrites[piece].append(key)
                    sc.dma("sp", ("ost", h), lambda e, h=h, dst=dst: e.dma_start(out=dst, in_=ost[:, hs_[h]]),
                           reads=["ost"], writes=[key])
                tok_end = (qt + 1) * 512
                if tok_end % (ND2 * T) == 0:
                    for h in range(2):
                        piece, _ = yloc_ap(2 + h, qt * 512, 512)
                        exch2_piece(piece, list(ywrites[piece]))

        for i in range(-2, NU):
            if 0 <= i + 2 < NU:
                S1(i + 2)
                S2(i + 2)
            if 0 <= i < NU:
                S5a(i)
            if 0 <= i + 1 < NU:
                S3(i + 1)
                S4(i + 1)
            if 0 <= i < NU:
                S5(i)
        sc.barrier()
        mB.reset()
        mC.reset()
        if stop_after == "sb":
            return finish(xres)

        pid_cache = {}
        yaT_ = mB.alloc([128, 16, T], BF16)
        ybT_ = mB.alloc([128, 16, T], BF16) if 2 * 16 * T * 2 <= HB else mC.alloc([128, 16, T], BF16)
        mergedT = mC.alloc([128, KC, T], BF16)
        FCR = NPF * 8 * cfg.PR2
        for fc in range(4):
            def ld(e, fc=fc):
                if "E" not in pid_cache:
                    pid_ = nc.partition_id()
                    if ND2 == 1:
                        pid_cache["E"] = pid_ * (8 * cfg.PR2)
                    else:
                        pid_cache["E"] = (pid_ // ND2) * (8 * cfg.PR2) + (pid_ % ND2) * 128
                E = pid_cache["E"]
                A = ex2out[bass.ds(E, 4 * FCR), :].rearrange("(fc x r f) t -> fc x r f t", fc=4, x=NPF, r=8, f=cfg.PR2)
                return e.dma_start(out=ex2stage[fc], in_=A[fc, 0, :, 0:128, :])
            sc.dma("pool", "yst", ld, writes=[("ystage", fc)])
        for r in range(NCORES):
            for j in range(2):
                for fc, dstt in ((j, yaT_), (2 + j, ybT_)):
                    sc.dma("sp", "yld", lambda e, fc=fc, dstt=dstt, r=r, j=j: e.dma_start(
                        out=dstt[:, 2 * r + j, :], in_=ex2stage[fc][r]),
                        reads=[("ystage", fc)], writes=[("yin", fc // 2, 2 * r + j)])
        gtile = [mC.alloc([128, T], BF16) for _ in range(4)]
        for kc in range(16):
            s = kc % 2
            sc.dma("sp", ("gt", s), lambda e, kc=kc, s=s: e.dma_start(out=gtile[s], in_=gts[kc]), writes=[("gt", s)])
            sc.op("dve", lambda e, kc=kc, s=s: e.tensor_tensor(out=yaT_[:, kc, :], in0=yaT_[:, kc, :], in1=gtile[s], op=ALU.mult),
                  reads=[("gt", s), ("yin", 0, kc)], writes=[("yin", 0, kc)])
        wpt = [[mC.alloc([128, 16, 128], BF16) for _ in range(2)] for _ in range(2)]
        m1 = [mC.alloc([128, 512], F32) for _ in range(2)]
        m2 = [mC.alloc([128, 512], F32) for _ in range(2)]
        yakeys = [("yin", 0, kc) for kc in range(16)]
        ybkeys = [("yin", 1, kc) for kc in range(16)]
        cnt = 0
        for o in range(KC):
            s = o % 2
            sc.dma("pool", ("wpa", s), lambda e, o=o, s=s: e.dma_start(
                out=wpt[0][s].rearrange("p a b -> p (a b)"), in_=wpa[o], max_dma_last_dim=8192), writes=[("wpa", s)])
            sc.dma("pool", ("wpb", s), lambda e, o=o, s=s: e.dma_start(
                out=wpt[1][s].rearrange("p a b -> p (a b)"), in_=wpb[o], max_dma_last_dim=8192), writes=[("wpb", s)])
            sc.dma("sp", ("gt", 2), lambda e, o=o: e.dma_start(out=gtile[2], in_=gts[16 + o]), writes=[("gt", 2)])
            sc.dma("sp", ("gt", 3), lambda e, o=o: e.dma_start(out=gtile[3], in_=gts[16 + KC + o]), writes=[("gt", 3)])
            for th in range(TH):
                sl = slice(th * 512, (th + 1) * 512)
                b = (cnt % 2) * 2
                q = cnt % 2
                cnt += 1

                def mm(e, s=s, sl=sl, b=b):
                    for kc in range(16):
                        e.matmul(ps[b][:, :], wpt[0][s][:, kc, :], yaT_[:, kc, sl], start=(kc == 0), stop=(kc == 15))
                    for kc in range(16):
                        r_ = e.matmul(ps[b + 1][:, :], wpt[1][s][:, kc, :], ybT_[:, kc, sl], start=(kc == 0), stop=(kc == 15))
                    return r_
                sc.op("pe", mm, reads=[("wpa", s), ("wpb", s)] + yakeys + ybkeys, writes=[P(b), P(b + 1)])
                sc.op("dve", lambda e, b=b, q=q, sl=sl: e.tensor_tensor(out=m1[q], in0=ps[b][:, :], in1=gtile[2][:, sl], op=ALU.mult),
                      reads=[P(b), ("gt", 2)], writes=[("m1", q)])
                sc.op("dve", lambda e, b=b, q=q, sl=sl: e.tensor_tensor(out=m2[q], in0=ps[b + 1][:, :], in1=gtile[3][:, sl], op=ALU.mult),
                      reads=[P(b + 1), ("gt", 3)], writes=[("m2", q)])
                sc.op("dve", lambda e, q=q, o=o, sl=sl: e.tensor_tensor(out=mergedT[:, o, sl], in0=m1[q], in1=m2[q], op=ALU.add),
                      reads=[("m1", q), ("m2", q)], writes=[("mg", o, th)])
        sc.barrier()
        mB.reset()
        wot = [mB.alloc([128, KC, 128], BF16) for _ in range(2)]
        xo = [mB.alloc([128, 512], F32) for _ in range(4)]
        cnt = 0
        for o in range(KC):
            s = o % 2
            sc.dma("pool", ("wot", s), lambda e, o=o, s=s: e.dma_start(
                out=wot[s].rearrange("p a b -> p (a b)"), in_=wo[o], max_dma_last_dim=8192), writes=[("wot", s)])
            for th in range(TH):
                sl = slice(th * 512, (th + 1) * 512)
                b = 4 + cnt % 4
                xs_ = cnt % 4
                cnt += 1

                def mm(e, s=s, sl=sl, b=b):
                    for kc in range(KC):
                        r_ = e.matmul(ps[b][:, :], wot[s][:, kc, :], mergedT[:, kc, sl], start=(kc == 0), stop=(kc == KC - 1))
                    return r_
                sc.op("pe", mm, reads=[("wot", s)], writes=[P(b)])
                sc.dma("sp", ("xo", xs_), lambda e, o=o, sl=sl, xs_=xs_: e.dma_start(out=xo[xs_], in_=xres[o][:, sl]),
                       writes=[("xo", xs_)])
                sc.op("dve", lambda e, b=b, xs_=xs_: e.tensor_tensor(out=xo[xs_], in0=ps[b][:, :], in1=xo[xs_], op=ALU.add),
                      reads=[P(b), ("xo", xs_)], writes=[("xo", xs_)])
                sc.dma("sp", ("xo", xs_), lambda e, o=o, sl=sl, xs_=xs_: e.dma_start(out=xres[o][:, sl], in_=xo[xs_]),
                       reads=[("xo", xs_)])
        sc.barrier()
        mB.reset()
        mC.reset()
        if stop_after == "merge":
            return finish(xres)

        hT = mB.alloc([128, KC, T], BF16)
        rmsnorm(xres, 2, hT)
        ffn(1, xres, hT)
        rmsnorm(xres, 3, None, outT)
        return finish(None)


def make_consts():
    j = np.arange(128)[:, None]
    s = np.arange(128)[None, :]
    ones = np.ones((128, 128), np.float32)
    triN = np.where(j > s, -1.0, 0.0).astype(np.float32)
    restN = np.where(j <= s, -1.0, 0.0).astype(np.float32)
    maskneg = np.where(j <= s, 0.0, NEG).astype(np.float32)
    U = np.where(j < s, 1.0, 0.0).astype(np.float32)
    identf = np.eye(128, dtype=np.float32)
    t = np.arange(512)[None, :]
    masks = [np.where(128 * dj + j < t, 1.0, 0.0).astype(np.float32) for dj in range(4)]
    cf = np.concatenate([ones, triN, restN, maskneg, U, identf] + masks, axis=1)
    return np.ascontiguousarray(cf), np.eye(128, dtype=np.float32).astype(ml_dtypes.bfloat16)


def _lhsT_layout(w, kc, oc):
    K, O = w.shape
    return np.ascontiguousarray(w.reshape(kc, 128, oc, 128).transpose(2, 1, 0, 3).reshape(oc, 128, kc * 128))


def prep_inputs(cfg, inp):
    D, T, KC, FC = cfg.D, cfg.T, cfg.KC, cfg.FC
    f32 = np.float32
    x = np.asarray(inp["x"], f32)[0]
    shared = {}
    for l, sfx in ((0, "a"), (1, "b")):
        n = "ffn1" if l == 0 else "ffn2"
        shared["w1" + sfx] = _lhsT_layout(np.asarray(inp["w1_" + n], f32)[0], KC, FC)
        shared["w3" + sfx] = _lhsT_layout(np.asarray(inp["w3_" + n], f32)[0], KC, FC)
        shared["w2" + sfx] = _lhsT_layout(np.asarray(inp["w2_" + n], f32)[0], FC, KC)
    gs = [np.asarray(inp[k], f32).reshape(-1) for k in ("g_ffn1", "g_mix", "g_ffn2", "g_final")]
    shared["gv"] = np.ascontiguousarray(np.concatenate([g.reshape(KC, 128).T for g in gs], axis=1))
    w_in = np.asarray(inp["w_in"], f32)[0]
    o_mq, o_mk, o_mv, o_mo, o_mi, o_mf = 0, 1024, 2048, 4096, 6144, 6152
    o_sq, o_sk, o_sv = 6160, 8208, 10256
    o_ga = 12304
    o_gb = o_ga + D
    gate_cols = np.concatenate([np.arange(o_mo, o_mo + 2048), np.arange(o_ga, o_ga + D), np.arange(o_gb, o_gb + D)])
    wgfull = w_in[:, gate_cols]
    shared["wG"] = _lhsT_layout(wgfull, KC, 16 + 2 * KC)
    shared["wpa"] = _lhsT_layout(np.asarray(inp["w_proj_a"], f32)[0], 16, KC)
    shared["wpb"] = _lhsT_layout(np.asarray(inp["w_proj_b"], f32)[0], 16, KC)
    shared["wo"] = _lhsT_layout(np.asarray(inp["w_out"], f32)[0], KC, KC)
    cf, idb = make_consts()
    shared["cf32"] = cf
    shared["identb"] = idb
    conv = np.asarray(inp["conv_qk"], f32)[0]
    b_i = np.asarray(inp["b_igate"], f32)[0]
    b_f = np.asarray(inp["b_fgate"], f32)[0]
    gml = np.asarray(inp["g_mlstm_out"], f32)[0]

    def pk(wcols):
        n = wcols.shape[1]
        return np.ascontiguousarray(wcols.reshape(KC, 128, n).transpose(1, 0, 2).reshape(128, KC * n))

    maps = []
    for c in range(NCORES):
        m = dict(shared)
        m["xT"] = np.ascontiguousarray(x[c * T:(c + 1) * T, :].T.reshape(KC, 128, T))
        colsA = np.concatenate([
            np.arange(o_mq + c * 128, o_mq + (c + 1) * 128), np.arange(o_mk + c * 128, o_mk + (c + 1) * 128),
            np.arange(o_sq + (2 * c) * 128, o_sq + (2 * c + 1) * 128), np.arange(o_sk + (2 * c) * 128, o_sk + (2 * c + 1) * 128),
            np.arange(o_sq + (2 * c + 1) * 128, o_sq + (2 * c + 2) * 128), np.arange(o_sk + (2 * c + 1) * 128, o_sk + (2 * c + 2) * 128)])
        m["wA"] = pk(w_in[:, colsA])
        wg = np.zeros((D, 64), f32)
        wg[:, 0] = w_in[:, o_mi + c]
        wg[:, 32] = w_in[:, o_mf + c]
        m["wGt"] = pk(wg)
        colsV = np.concatenate([np.arange(o_mv + c * 256, o_mv + (c + 1) * 256),
                                np.arange(o_sv + (2 * c) * 128, o_sv + (2 * c + 2) * 128)])
        m["wV"] = pk(w_in[:, colsV])
        sm = np.zeros((128, 16 + 256), f32)
        sm[:, 0:4] = conv[:, c * 128:(c + 1) * 128].T
        sm[:, 4:8] = conv[:, 1024 + c * 128:1024 + (c + 1) * 128].T
        sm[:, 8] = b_i[c]
        sm[:, 9] = b_f[c]
        sm[:, 16:272] = gml[c * 256:(c + 1) * 256][None, :]
        m["small"] = sm
        maps.append(m)
    return maps


_PROG_CACHE = {}


def run(cfg, inp, stop_after=None, trace=False):
    key = (cfg.D, cfg.T, cfg.DFF, stop_after)
    if key not in _PROG_CACHE:
        _PROG_CACHE[key] = build_program(cfg, stop_after)
    nc = _PROG_CACHE[key]
    maps = prep_inputs(cfg, inp)
    res = run_bass_kernel_spmd(nc, maps, core_ids=list(range(NCORES)), trace=trace)
    name = "dbg" if stop_after else "outT"
    outs = [r[name].reshape(cfg.D, cfg.T).T for r in res.results]
    full = np.concatenate(outs, axis=0)[None].astype(np.float32)
    if stop_after:
        ya = np.zeros((cfg.S, 2048), np.float32); yb = np.zeros((cfg.S, 2048), np.float32)
        for c, r in enumerate(res.results):
            y = r["dbg2"].astype(np.float32)
            for fc in range(4):
                for dc in range(NCORES):
                    blk = y[fc * cfg.NPF + dc // cfg.ND2][(dc % cfg.ND2) * 128:(dc % cfg.ND2) * 128 + 128, :]
                    if fc < 2:
                        ya[dc * cfg.T:(dc + 1) * cfg.T, c * 256 + fc * 128:c * 256 + (fc + 1) * 128] = blk.T
                    else:
                        yb[dc * cfg.T:(dc + 1) * cfg.T, (2 * c + fc - 2) * 128:(2 * c + fc - 1) * 128] = blk.T
        res.ya, res.yb = ya, yb
    return full, res


def kernel(**inputs):
    out, _ = run(Cfg(), inputs)
    return out
```

```python
from contextlib import ExitStack
import numpy as np
import ml_dtypes
import concourse.bass as bass
import concourse.mybir as mybir
from concourse.bass_utils import run_bass_kernel_spmd

F32 = mybir.dt.float32
F32R = mybir.dt.float32r
SB_CUMSUM_DT = F32R
BF16 = mybir.dt.bfloat16
AF = mybir.ActivationFunctionType
ALU = mybir.AluOpType
NCORES = 8
EPS = 1e-6
NEG = -30000.0
NCONST = 6 * 128 + 4 * 512


class Cfg:
    def __init__(self, D=4096, T=1024, DFF=11008, G=3, PIECE=512 * 1024):
        self.D, self.T, self.DFF, self.G = D, T, DFF, G
        self.S = NCORES * T
        self.KC = D // 128
        self.FC = DFF // 128
        self.TH = T // 512
        self.NCH = self.S // 128
        self.NQT = self.S // 512
        base, rem = divmod(self.FC, G)
        self.groups = []
        f0 = 0
        for g in range(G):
            n = base + (1 if g < rem else 0)
            self.groups.append((f0, n))
            f0 += n
        self.PR1 = PIECE // (2 * T)
        self.NP1 = D // self.PR1
        self.KPP = self.PR1 // 128
        self.ND2 = PIECE // (128 * T * 2)
        self.NPF = NCORES // self.ND2
        self.NP2 = 4 * self.NPF
        self.PR2 = self.ND2 * 128
        assert self.PR1 % 128 == 0 and D % self.PR1 == 0 and self.ND2 >= 1


class Sched:
    def __init__(self, nc, enter):
        self.nc = nc
        self.enter = enter
        self.streams = {k: [] for k in ("pe", "act", "dve", "pool", "sp")}
        self.sems = {}
        self.count = {}
        for k in ("pe", "act", "dve", "pool"):
            self.sems[k] = enter(nc.semaphore("s_" + k))
            self.count[k] = 0
        self.waited = {k: {} for k in self.streams}
        self.lastw = {}
        self.readers = {}

    def _need(self, e, tok, waits):
        if tok is None:
            return
        k, v = tok
        if e == "pe" and k == "pe":
            return
        if k not in ("pe", "act", "dve", "pool"):
            v = self.count[k]
        if self.waited[e].get(k, 0) >= v:
            return
        self.waited[e][k] = v
        waits[k] = max(waits.get(k, 0), v)

    def _deps(self, e, reads, writes):
        waits = {}
        for r in reads:
            self._need(e, self.lastw.get(r), waits)
        for r in writes:
            self._need(e, self.lastw.get(r), waits)
            for k, v in self.readers.get(r, {}).items():
                self._need(e, (k, v), waits)
        return waits

    def _commit(self, tok, reads, writes):
        for r in reads:
            d = self.readers.setdefault(r, {})
            d[tok[0]] = max(d.get(tok[0], 0), tok[1])
        for r in writes:
            self.lastw[r] = tok
            self.readers[r] = {}

    def op(self, e, fn, reads=(), writes=()):
        waits = self._deps(e, reads, writes)
        self.count[e] += 1
        tok = (e, self.count[e])
        self.streams[e].append((waits, fn, e, 1))
        self._commit(tok, reads, writes)

    def dma(self, q, key, fn, reads=(), writes=(), inc=16):
        if key not in self.sems:
            self.sems[key] = self.enter(self.nc.semaphore("d%d" % len(self.sems)))
            self.count[key] = 0
        waits = self._deps(q, reads, writes)
        self.count[key] += inc
        tok = (key, self.count[key])
        self.streams[q].append((waits, fn, key, inc))
        self._commit(tok, reads, writes)

    def barrier(self):
        for e in self.streams:
            waits = {}
            for k, v in self.count.items():
                if v > 0:
                    self._need(e, (k, v), waits)
            if waits:
                self.streams[e].append((waits, None, None, 0))
        self.lastw = {}
        self.readers = {}

    def emit(self, block):
        def make(name):
            def body(engine):
                for waits, fn, semkey, inc in self.streams[name]:
                    for k, v in waits.items():
                        engine.wait_ge(self.sems[k], v)
                    if fn is None:
                        continue
                    inst = fn(engine)
                    inst.then_inc(self.sems[semkey], inc)
            return body
        block.tensor(make("pe"))
        block.scalar(make("act"))
        block.vector(make("dve"))
        block.gpsimd(make("pool"))
        block.sync(make("sp"))


class Mem:
    def __init__(self, big, start, nbytes):
        self.big, self.start, self.cap, self.off = big, start, nbytes, 0

    def alloc(self, shape, dtype):
        esz = 4 if dtype == F32 else 2
        n = int(np.prod(shape[1:]))
        nb = (n * esz + 63) // 64 * 64
        assert self.off + nb <= self.cap, ("SBUF overflow", self.off, nb, self.cap)
        a = (self.start + self.off) // 2
        ap = self.big[0:shape[0], a:a + n * esz // 2]
        self.off += nb
        if dtype == F32:
            ap = ap.bitcast(F32)
        if len(shape) == 3:
            ap = ap.rearrange("p (a b) -> p a b", a=shape[1])
        return ap

    def reset(self):
        self.off = 0


def build_program(cfg, stop_after=None):
    nc = bass.Bass("TRN2", target_bir_lowering=False)
    D, T, S, KC, FC, TH, NCH, NQT = cfg.D, cfg.T, cfg.S, cfg.KC, cfg.FC, cfg.TH, cfg.NCH, cfg.NQT
    NPF, ND2 = cfg.NPF, cfg.ND2

    def din(name, shape, dt=F32):
        return nc.dram_tensor(name, list(shape), dt, kind="ExternalInput").ap()

    def dscr(name, shape, dt):
        return nc.dram_tensor(name, list(shape), dt).ap()

    xT = din("xT", [KC, 128, T])
    gv = din("gv", [128, 4 * KC])
    w1 = [din("w1a", [FC, 128, KC * 128]), din("w1b", [FC, 128, KC * 128])]
    w3 = [din("w3a", [FC, 128, KC * 128]), din("w3b", [FC, 128, KC * 128])]
    w2 = [din("w2a", [KC, 128, FC * 128]), din("w2b", [KC, 128, FC * 128])]
    wA = din("wA", [128, KC * 768])
    wGt = din("wGt", [128, KC * 64])
    wV = din("wV", [128, KC * 512])
    NGC = 16 + 2 * KC
    wG = din("wG", [NGC, 128, KC * 128])
    wpa = din("wpa", [KC, 128, 16 * 128])
    wpb = din("wpb", [KC, 128, 16 * 128])
    wo = din("wo", [KC, 128, KC * 128])
    small = din("small", [128, 16 + 256])
    cf32 = din("cf32", [128, NCONST])
    identb = din("identb", [128, 128], BF16)
    outT = nc.dram_tensor("outT", [KC, 128, T], F32, kind="ExternalOutput").ap()
    dbg = nc.dram_tensor("dbg", [KC, 128, T], F32, kind="ExternalOutput").ap() if stop_after else None
    dbg2 = nc.dram_tensor("dbg2", [cfg.NP2, cfg.PR2, T], BF16, kind="ExternalOutput").ap() if stop_after else None

    xres = dscr("xres", [KC, 128, T], F32)
    gts = dscr("gts", [NGC, 128, T], BF16)
    h2loc = dscr("h2loc", [cfg.NP1, cfg.PR1, T], BF16)
    ex1mid = dscr("ex1mid", [cfg.NP1, 4 * cfg.PR1, T], BF16)
    ex1out = dscr("ex1out", [cfg.NP1, 8 * cfg.PR1, T], BF16)
    qkpre = dscr("qkpre", [2, 128, S], F32)
    sqk = dscr("sqk", [4, 128, S], BF16)
    gates_d = dscr("gates_d", [2, S], F32)
    vm_d = dscr("vm_d", [NCH, 128, 256], BF16)
    vs_d = dscr("vs_d", [2, NCH, 128, 128], BF16)
    rows_d = dscr("rows_d", [2, S], F32)
    yloc = dscr("yloc", [cfg.NP2, cfg.PR2, T], BF16)
    ex2mid = dscr("ex2mid", [cfg.NP2, 4 * cfg.PR2, T], BF16)
    ex2out = dscr("ex2out", [5 * cfg.NPF * 8 * cfg.PR2, T], BF16)
    ex2stage = dscr("ex2stage", [4, 8, 128, T], BF16)

    with ExitStack() as es:
        enter = es.enter_context
        CB = 16 * 1024
        HB = max(KC * T * 2, 64 * 1024)
        TOTAL = 200 * 1024
        big = enter(nc.sbuf_tensor("big", [128, TOTAL // 2], BF16))
        psall = enter(nc.psum_tensor("psall", [128, 8 * 512], F32))
        ps = [psall[:, i * 512:(i + 1) * 512] for i in range(8)]
        psb = [p.bitcast(BF16) for p in ps]
        sc = Sched(nc, enter)
        mA = Mem(big, 0, CB)
        mB = Mem(big, CB, HB)
        mC = Mem(big, CB + HB, TOTAL - CB - HB)
        P = lambda i: ("ps", i)

        c_gv = mA.alloc([128, 4 * KC], F32)
        c_small = mA.alloc([128, 16 + 256], F32)
        c_f32 = mA.alloc([128, NCONST], F32)
        c_id = mA.alloc([128, 128], BF16)
        sc.dma("sp", "c", lambda e: e.dma_start(out=c_gv, in_=gv), writes=["c_gv"])
        sc.dma("sp", "c", lambda e: e.dma_start(out=c_small, in_=small), writes=["c_small"])
        sc.dma("sp", "c", lambda e: e.dma_start(out=c_f32, in_=cf32), writes=["c_f32"])
        sc.dma("sp", "c", lambda e: e.dma_start(out=c_id, in_=identb), writes=["c_id"])
        sc.barrier()
        ones_f = c_f32[:, 0:128]
        triN = c_f32[:, 128:256]
        restN = c_f32[:, 256:384]
        maskneg = c_f32[:, 384:512]
        Umat = c_f32[:, 512:640]
        identf = c_f32[:, 640:768]
        dmask = [c_f32[:, 768 + 512 * j:768 + 512 * (j + 1)] for j in range(4)]
        gml_b = c_small[:, 16:272]
        c_tr = mA.alloc([128, 256], BF16)
        sc.op("dve", lambda e: e.tensor_copy(out=c_tr, in_=c_f32[:, 128:384]), writes=["c_tr"])
        sc.barrier()
        triR = c_tr[:, 0:128]
        restR = c_tr[:, 128:256]
        ones_b = mA.alloc([128, 128], BF16)
        sc.op("dve", lambda e: e.tensor_copy(out=ones_b, in_=c_f32[:, 0:128]), writes=["ones_b"])
        sc.barrier()

        def finish(dump=None):
            if dump is not None:
                for c in range(KC):
                    sc.dma("sp", "dump", lambda e, c=c: e.dma_start(out=dbg[c], in_=dump[c]))
                for p in range(cfg.NP2):
                    sc.dma("sp", "dump", lambda e, p=p: e.dma_start(out=dbg2[p], in_=yloc[p]))
            sc.barrier()
            with nc.Block() as block:
                sc.emit(block)
            return nc

        def rmsnorm(src, gidx, hT, dst_dram=None):
            xs = [mC.alloc([128, T], F32) for _ in range(6)]
            sq = [mC.alloc([128, T], BF16) for _ in range(2)]
            rstd = mC.alloc([128, T], F32)
            outs = [mC.alloc([128, T], F32) for _ in range(2)] if hT is None else None
            for c in range(KC):
                s = c % 6
                sc.dma("sp", ("xs", s), lambda e, c=c, s=s: e.dma_start(out=xs[s], in_=src[c]),
                       writes=[("xs", s)])
                q = c % 2
                sc.op("act", lambda e, s=s, q=q: e.activation(out=sq[q], in_=xs[s], func=AF.Square),
                      reads=[("xs", s)], writes=[("sq", q)])
                for th in range(TH):
                    sc.op("pe", lambda e, q=q, th=th, c=c: e.matmul(
                        ps[4 + th][:, :], ones_b, sq[q][:, th * 512:(th + 1) * 512],
                        start=(c == 0), stop=(c == KC - 1)),
                        reads=[("sq", q)], writes=[P(4 + th)])
            for th in range(TH):
                sl = slice(th * 512, (th + 1) * 512)
                sc.op("act", lambda e, th=th, sl=sl: e.activation(
                    out=rstd[:, sl], in_=ps[4 + th][:, :], func=AF.Sqrt, bias=EPS, scale=1.0 / D),
                    reads=[P(4 + th)], writes=[("rstd", th)])
                sc.op("dve", lambda e, sl=sl: e.reciprocal(out=rstd[:, sl], in_=rstd[:, sl]),
                      reads=[("rstd", th)], writes=[("rstd", th)])
            for c in range(KC):
                s = c % 6
                sc.dma("sp", ("xs", s), lambda e, c=c, s=s: e.dma_start(out=xs[s], in_=src[c]),
                       writes=[("xs", s)])
                gcol = c_gv[:, gidx * KC + c:gidx * KC + c + 1]
                rd = [("xs", s)] + [("rstd", th) for th in range(TH)]
                if hT is not None:
                    sc.op("dve", lambda e, c=c, s=s, gcol=gcol: e.scalar_tensor_tensor(
                        out=hT[:, c, :], in0=xs[s], scalar=gcol, in1=rstd, op0=ALU.mult, op1=ALU.mult),
                        reads=rd, writes=[("hT", c)])
                else:
                    o = c % 2
                    sc.op("dve", lambda e, o=o, s=s, gcol=gcol: e.scalar_tensor_tensor(
                        out=outs[o], in0=xs[s], scalar=gcol, in1=rstd, op0=ALU.mult, op1=ALU.mult),
                        reads=rd, writes=[("nout", o)])
                    sc.dma("sp", ("nout", o), lambda e, o=o, c=c: e.dma_start(out=dst_dram[c], in_=outs[o]),
                           reads=[("nout", o)])
            sc.barrier()
            mC.reset()

        def ffn(l, src, hT):
            ngmax = max(n for _, n in cfg.groups)
            aT = mC.alloc([128, ngmax, T], BF16)
            w1t = [mC.alloc([128, KC, 128], BF16) for _ in range(2)]
            w3t = [mC.alloc([128, KC, 128], BF16) for _ in range(2)]
            w2t = [mC.alloc([128, ngmax, 128], BF16) for _ in range(2)]
            sil = [mC.alloc([128, 512], F32) for _ in range(2)]
            xo = [mC.alloc([128, 512], F32) for _ in range(4)]
            cnt_u = 0
            cnt_d = 0
            issued = set()

            def wload(f):
                if f in issued or f >= FC:
                    return
                issued.add(f)
                s = f % 2
                sc.dma("pool", ("w1t", s), lambda e, f=f, s=s: e.dma_start(
                    out=w1t[s].rearrange("p a b -> p (a b)"), in_=w1[l][f], max_dma_last_dim=8192),
                    writes=[("w1t", s)])
                sc.dma("pool", ("w3t", s), lambda e, f=f, s=s: e.dma_start(
                    out=w3t[s].rearrange("p a b -> p (a b)"), in_=w3[l][f], max_dma_last_dim=8192),
                    writes=[("w3t", s)])
            for g, (f0, ng) in enumerate(cfg.groups):
                for fi in range(ng):
                    f = f0 + fi
                    s = f % 2
                    wload(f)
                    for th in range(TH):
                        sl = slice(th * 512, (th + 1) * 512)
                        bu = (cnt_u % 2) * 2
                        q = cnt_u % 2
                        cnt_u += 1

                        def mm(e, s=s, sl=sl, bu=bu):
                            for kc in range(KC):
                                e.matmul(ps[bu][:, :], w1t[s][:, kc, :], hT[:, kc, sl],
                                         start=(kc == 0), stop=(kc == KC - 1))
                            for kc in range(KC):
                                r = e.matmul(ps[bu + 1][:, :], w3t[s][:, kc, :], hT[:, kc, sl],
                                             start=(kc == 0), stop=(kc == KC - 1))
                            return r
                        sc.op("pe", mm, reads=[("w1t", s), ("w3t", s)], writes=[P(bu), P(bu + 1)])
                        sc.op("act", lambda e, q=q, bu=bu: e.activation(out=sil[q], in_=ps[bu][:, :], func=AF.Silu),
                              reads=[P(bu)], writes=[("sil", q)])
                        sc.op("dve", lambda e, q=q, bu=bu, fi=fi, sl=sl: e.tensor_tensor(
                            out=aT[:, fi, sl], in0=ps[bu + 1][:, :], in1=sil[q], op=ALU.mult),
                            reads=[P(bu + 1), ("sil", q)], writes=[("aT", fi, th)])
                wload(f0 + ng)
                wload(f0 + ng + 1)
                for o in range(KC):
                    s = o % 2
                    sc.dma("pool", ("w2t", s), lambda e, o=o, s=s, f0=f0, ng=ng: e.dma_start(
                        out=w2t[s][:, 0:ng, :].rearrange("p a b -> p (a b)"),
                        in_=w2[l][o][:, f0 * 128:(f0 + ng) * 128], max_dma_last_dim=8192),
                        writes=[("w2t", s)])
                    for th in range(TH):
                        sl = slice(th * 512, (th + 1) * 512)
                        bd = 4 + (cnt_d % 4)
                        xs_ = cnt_d % 4
                        cnt_d += 1

                        def mm2(e, s=s, sl=sl, bd=bd, ng=ng):
                            for fi in range(ng):
                                r = e.matmul(ps[bd][:, :], w2t[s][:, fi, :], aT[:, fi, sl],
                                             start=(fi == 0), stop=(fi == ng - 1))
                            return r
                        sc.op("pe", mm2, reads=[("w2t", s)] + [("aT", fi, th) for fi in range(ng)], writes=[P(bd)])
                        srcd = src if g == 0 else xres
                        sc.dma("sp", ("xo", xs_), lambda e, o=o, sl=sl, xs_=xs_, srcd=srcd: e.dma_start(
                            out=xo[xs_], in_=srcd[o][:, sl]), reads=[("xres", o, th)], writes=[("xo", xs_)])
                        sc.op("dve", lambda e, bd=bd, xs_=xs_: e.scalar_tensor_tensor(
                            out=xo[xs_], in0=ps[bd][:, :], scalar=0.5, in1=xo[xs_], op0=ALU.mult, op1=ALU.add),
                            reads=[P(bd), ("xo", xs_)], writes=[("xo", xs_)])
                        sc.dma("sp", ("xo", xs_), lambda e, o=o, sl=sl, xs_=xs_: e.dma_start(
                            out=xres[o][:, sl], in_=xo[xs_]), reads=[("xo", xs_)], writes=[("xres", o, th)])
            sc.barrier()
            mC.reset()

        hT = mB.alloc([128, KC, T], BF16)
        rmsnorm(xT, 0, hT)
        ffn(0, xT, hT)
        if stop_after == "ffn1":
            return finish(xres)

        rmsnorm(xres, 1, hT)
        KPP = cfg.KPP
        for i in range(cfg.NP1):
            sc.dma("sp", "c", lambda e, i=i: e.dma_start(
                out=h2loc[i].rearrange("(k p) t -> p k t", p=128), in_=hT[:, i * KPP:(i + 1) * KPP, :]),
                writes=[("h2loc", i)])
        wA_t = mC.alloc([128, KC, 768], BF16)
        wGt_t = mC.alloc([128, KC, 64], BF16)
        wV_t = mC.alloc([128, KC, 512], BF16)
        head_mark = mC.off
        hw_dmas = []
        for kc0 in range(0, KC, 2):
            hw_dmas.append(lambda e, kc0=kc0: e.dma_start(
                out=wA_t[:, kc0:kc0 + 2, :].rearrange("p a b -> p (a b)"),
                in_=wA[:, kc0 * 768:(kc0 + 2) * 768], max_dma_last_dim=6144))
        hw_dmas.append(lambda e: e.dma_start(out=wGt_t.rearrange("p a b -> p (a b)"), in_=wGt, max_dma_last_dim=8192))
        for kc0 in range(0, KC, 2):
            hw_dmas.append(lambda e, kc0=kc0: e.dma_start(
                out=wV_t[:, kc0:kc0 + 2, :].rearrange("p a b -> p (a b)"),
                in_=wV[:, kc0 * 512:(kc0 + 2) * 512], max_dma_last_dim=8192))
        for i in range(cfg.NP1):
            sc.dma("pool", ("cc", i % 16), lambda e, i=i: e.collective_compute(
                "AllGather", ALU.bypass, replica_groups=[[0, 1, 2, 3], [4, 5, 6, 7]],
                ins=[h2loc[i]], outs=[ex1mid[i]]),
                reads=[("h2loc", i)], writes=[("ex1mid", i)], inc=1)
        NWG = 4
        wgt = [mC.alloc([128, KC, 128], BF16) for _ in range(NWG)]
        gst = [mC.alloc([128, 512], BF16) for _ in range(4)]
        cnt = 0
        def ex1_stage2(i):
            sc.dma("pool", ("cc", i % 16), lambda e, i=i: e.collective_compute(
                "AllGather", ALU.bypass, replica_groups=[[0, 4], [1, 5], [2, 6], [3, 7]],
                ins=[ex1mid[i]], outs=[ex1out[i]]),
                reads=[("ex1mid", i)], writes=[("ex1out", i)], inc=1)
        st2_next = 0
        for f in range(NGC):
            s = f % NWG
            sc.dma("pool", ("wgt", s), lambda e, f=f, s=s: e.dma_start(
                out=wgt[s].rearrange("p a b -> p (a b)"), in_=wG[f], max_dma_last_dim=8192), writes=[("wgt", s)])
            if f >= 8 and f % 2 == 0 and st2_next < cfg.NP1:
                ex1_stage2(st2_next)
                st2_next += 1
            if f >= 3 and hw_dmas:
                sc.dma("pool", "wA", hw_dmas.pop(0))
            for th in range(TH):
                sl = slice(th * 512, (th + 1) * 512)
                b = cnt % 4
                cnt += 1

                def mm(e, s=s, sl=sl, b=b):
                    for kc in range(KC):
                        r = e.matmul(ps[b][:, :], wgt[s][:, kc, :], hT[:, kc, sl], start=(kc == 0), stop=(kc == KC - 1))
                    return r
                sc.op("pe", mm, reads=[("wgt", s)], writes=[P(b)])
                sc.op("act", lambda e, b=b: e.activation(out=gst[b], in_=ps[b][:, :], func=AF.Sigmoid),
                      reads=[P(b)], writes=[("gst", b)])
                sc.dma("sp", ("gst", b), lambda e, b=b, f=f, sl=sl: e.dma_start(out=gts[f][:, sl], in_=gst[b]),
                       reads=[("gst", b)])
        while st2_next < cfg.NP1:
            ex1_stage2(st2_next)
            st2_next += 1
        while hw_dmas:
            sc.dma("pool", "wA", hw_dmas.pop(0))
        sc.barrier()
        if stop_after == "ex1":
            return finish(xres)

        mB.reset()
        mC.off = head_mark
        h2b = [mB.alloc([128, KC, 512], BF16) for _ in range(2)] if 2 * KC * 512 * 2 <= HB else \
            [mC.alloc([128, KC, 512], BF16) for _ in range(2)]
        stf = [mC.alloc([128, 512], F32) for _ in range(2)]
        stb = [mC.alloc([128, 512], BF16) for _ in range(4)]
        stg = [mC.alloc([64, 512], F32) for _ in range(2)]
        vst = [mC.alloc([128, 512], BF16) for _ in range(2)]
        pcnt = 0
        for tb in range(S // 512):
            r = (tb * 512) // T
            toff = (tb * 512) % T
            hs = tb % 2
            cs = slice(tb * 512, (tb + 1) * 512)
            def load_blk(tb_):
                r_ = (tb_ * 512) // T
                toff_ = (tb_ * 512) % T
                hs_ = tb_ % 2
                for i in range(cfg.NP1):
                    sc.dma("sp", ("h2b", hs_, i % 4), lambda e, i=i, r_=r_, toff_=toff_, hs_=hs_: e.dma_start(
                        out=h2b[hs_][:, i * KPP:(i + 1) * KPP, :],
                        in_=ex1out[i][r_ * cfg.PR1:(r_ + 1) * cfg.PR1, toff_:toff_ + 512].rearrange("(k p) t -> p k t", p=128)),
                        writes=[("h2b", hs_, i)])
            if tb == 0:
                load_blk(0)
            if tb + 1 < S // 512:
                load_blk(tb + 1)
            hkeys = [("h2b", hs, i) for i in range(cfg.NP1)]
            for j in range(7):
                b = pcnt % 4
                pcnt += 1
                if j < 6:
                    def mm(e, j=j, hs=hs, b=b):
                        for kc in range(KC):
                            r_ = e.matmul(ps[b][:, :], wA_t[:, kc, j * 128:(j + 1) * 128], h2b[hs][:, kc, :],
                                          start=(kc == 0), stop=(kc == KC - 1))
                        return r_
                else:
                    def mm(e, hs=hs, b=b):
                        for kc in range(KC):
                            r_ = e.matmul(ps[b][0:64, :], wGt_t[:, kc, :], h2b[hs][:, kc, :],
                                          start=(kc == 0), stop=(kc == KC - 1))
                        return r_
                sc.op("pe", mm, reads=hkeys, writes=[P(b)])
                if j < 2:
                    q = j
                    sc.op("act", lambda e, q=q, b=b: e.activation(out=stf[q], in_=ps[b][:, :], func=AF.Copy),
                          reads=[P(b)], writes=[("stf", q)])
                    sc.dma("sp", ("stf", q), lambda e, q=q, j=j, cs=cs: e.dma_start(out=qkpre[j][:, cs], in_=stf[q]),
                           reads=[("stf", q)])
                elif j < 6:
                    q = j - 2
                    sc.op("dve", lambda e, q=q, b=b: e.tensor_copy(out=stb[q], in_=ps[b][:, :]),
                          reads=[P(b)], writes=[("stb", q)])
                    sc.dma("sp", ("stb", q), lambda e, q=q, cs=cs: e.dma_start(out=sqk[q][:, cs], in_=stb[q]),
                           reads=[("stb", q)])
                else:
                    q = tb % 2
                    sc.op("act", lambda e, q=q, b=b: e.activation(out=stg[q], in_=ps[b][0:64, :], func=AF.Copy),
                          reads=[P(b)], writes=[("stg", q)])
                    sc.dma("sp", ("stg", q), lambda e, q=q, cs=cs: e.dma_start(out=gates_d[0:1, cs], in_=stg[q][0:1, :]),
                           reads=[("stg", q)])
                    sc.dma("sp", ("stg", q), lambda e, q=q, cs=cs: e.dma_start(out=gates_d[1:2, cs], in_=stg[q][32:33, :]),
                           reads=[("stg", q)])
            for sub in range(4):
                ch = tb * 4 + sub
                b = 4 + ch % 4
                vq = ch % 2

                def mmv(e, hs=hs, b=b, sub=sub):
                    for kc in range(KC):
                        r_ = e.matmul(ps[b][:, :], h2b[hs][:, kc, sub * 128:(sub + 1) * 128], wV_t[:, kc, :],
                                      start=(kc == 0), stop=(kc == KC - 1))
                    return r_
                sc.op("pe", mmv, reads=hkeys, writes=[P(b)])
                sc.op("dve", lambda e, b=b, vq=vq: e.tensor_copy(out=vst[vq], in_=ps[b][:, :]),
                      reads=[P(b)], writes=[("vst", vq)])
                sc.dma("sp", ("vst", vq), lambda e, vq=vq, ch=ch: e.dma_start(out=vm_d[ch], in_=vst[vq][:, 0:256]),
                       reads=[("vst", vq)])
                for h in range(2):
                    sc.dma("sp", ("vst", vq), lambda e, vq=vq, ch=ch, h=h: e.dma_start(
                        out=vs_d[h][ch], in_=vst[vq][:, 256 + 128 * h:384 + 128 * h]), reads=[("vst", vq)])
        sc.barrier()
        mB.reset()
        mC.reset()
        if stop_after == "heads":
            return finish(xres)

        def yloc_ap(fc, tok0, n):
            dc = tok0 // T
            to = tok0 % T
            piece = fc * NPF + dc // ND2
            row0 = (dc % ND2) * 128
            return piece, yloc[piece][row0:row0 + 128, to:to + n]

        ywrites = {p: [] for p in range(cfg.NP2)}
        qT = mB.alloc([128, S], BF16)
        kT = mB.alloc([128, S], BF16)
        negGb = mC.alloc([128, S], F32)
        vext = mC.alloc([128, NCH, 264], BF16)
        for c0 in range(0, NCH, 8):
            n = min(8, NCH - c0)
            sc.dma("sp", "c", lambda e, c0=c0, n=n: e.dma_start(
                out=vext[:, c0:c0 + n, 0:256], in_=vm_d[c0:c0 + n].rearrange("c p d -> p c d")),
                writes=[("vext", c0)])
        sc.op("dve", lambda e: e.memset(vext[:, :, 256:257], 1.0), writes=["vones"])
        vkeys = [("vext", c0) for c0 in range(0, NCH, 8)] + ["vones"]
        SEG = min(2048, S)
        pre = [mC.alloc([128, SEG + 4], F32) for _ in range(2)]
        acc = [mC.alloc([128, SEG], F32) for _ in range(2)]
        it = 0
        for j in range(2):
            for sg in range(S // SEG):
                s = it % 2
                it += 1
                if sg == 0:
                    sc.op("dve", lambda e, s=s: e.memset(pre[s][:, 0:3], 0.0), writes=[("pre0", s)])
                    sc.dma("sp", ("pre", s), lambda e, s=s, j=j: e.dma_start(out=pre[s][:, 3:3 + SEG], in_=qkpre[j][:, 0:SEG]),
                           reads=[("pre0", s)], writes=[("pre", s)])
                else:
                    sc.dma("sp", ("pre", s), lambda e, s=s, j=j, sg=sg: e.dma_start(
                        out=pre[s][:, 0:3 + SEG], in_=qkpre[j][:, sg * SEG - 3:(sg + 1) * SEG]),
                        reads=[("pre0", s)], writes=[("pre", s)])
                wc = lambda i, j=j: c_small[:, 4 * j + i:4 * j + i + 1]
                sc.op("dve", lambda e, s=s, wc=wc: e.tensor_scalar(
                    out=acc[s], in0=pre[s][:, 3:3 + SEG], scalar1=wc(3), scalar2=None, op0=ALU.mult),
                    reads=[("pre", s), ("pre0", s)], writes=[("acc", s)])
                for i in (2, 1, 0):
                    sc.op("dve", lambda e, s=s, wc=wc, i=i: e.scalar_tensor_tensor(
                        out=acc[s], in0=pre[s][:, i:i + SEG], scalar=wc(i), in1=acc[s], op0=ALU.mult, op1=ALU.add),
                        reads=[("pre", s), ("acc", s)], writes=[("acc", s)])
                seg = slice(sg * SEG, (sg + 1) * SEG)
                if j == 0:
                    sc.op("act", lambda e, s=s, seg=seg: e.activation(out=qT[:, seg], in_=acc[s], func=AF.Silu),
                          reads=[("acc", s)], writes=[("qT", sg), ("pre", s), ("pre0", s)])
                else:
                    sc.op("act", lambda e, s=s: e.activation(out=acc[s], in_=acc[s], func=AF.Silu),
                          reads=[("acc", s)], writes=[("acc", s)])
                    sc.op("dve", lambda e, s=s, seg=seg: e.tensor_scalar(
                        out=kT[:, seg], in0=acc[s], scalar1=128.0 ** -0.5, scalar2=None, op0=ALU.mult),
                        reads=[("acc", s)], writes=[("kT", sg), ("pre", s), ("pre0", s)])
        qkkeys = [("qT", sg) for sg in range(S // SEG)] + [("kT", sg) for sg in range(S // SEG)]
        def cm():
            return mC.alloc([NCH, 128], F32)
        gi, gf, spt, Floc, Fp, gg, Gloc, Gt, tmpc = [cm() for _ in range(9)]
        colt = mC.alloc([128, 8], F32)
        rowt = mC.alloc([1, 4 * NCH], F32)
        Mb = mC.alloc([128, 2 * NCH], F32)
        gcol = mC.alloc([128, NCH], F32)
        ecol = mC.alloc([128, NCH], F32)
        srccol = mC.alloc([128, NCH], F32)
        csb = mC.alloc([128, NCH], F32)
        tcol = mC.alloc([128, NCH], F32)
        sc.dma("sp", "c", lambda e: e.dma_start(out=gi, in_=gates_d[0].rearrange("(c p) -> c p", p=128)), writes=["gi"])
        sc.dma("sp", "c", lambda e: e.dma_start(out=gf, in_=gates_d[1].rearrange("(c p) -> c p", p=128)), writes=["gf"])
        sc.op("dve", lambda e: e.tensor_scalar(out=colt[:, 0:1], in0=c_small[:, 9:10], scalar1=-1.0, scalar2=None, op0=ALU.mult),
              writes=["nbf"])
        sc.op("act", lambda e: e.activation(out=spt, in_=gf, func=AF.Exp, bias=colt[0:NCH, 0:1], scale=-1.0),
              reads=["gf", "nbf"], writes=["spt"])
        sc.op("act", lambda e: e.activation(out=spt, in_=spt, func=AF.Ln, bias=1.0, scale=1.0), reads=["spt"], writes=["spt"])
        sc.op("dve", lambda e: e.tensor_tensor_scan(out=Floc, data0=ones_f[0:NCH, :], data1=spt, initial=0.0,
                                                    op0=ALU.mult, op1=ALU.add), reads=["spt"], writes=["Floc"])
        sc.op("pe", lambda e: e.matmul(ps[0][0:NCH, 0:1], Umat[0:NCH, 0:NCH], Floc[:, 127:128], start=True, stop=True),
              reads=["Floc"], writes=[P(0)])
        sc.op("act", lambda e: e.activation(out=colt[0:NCH, 1:2], in_=ps[0][0:NCH, 0:1], func=AF.Copy), reads=[P(0)], writes=["offs"])
        sc.op("dve", lambda e: e.tensor_scalar(out=Fp, in0=Floc, scalar1=colt[0:NCH, 1:2], scalar2=None, op0=ALU.add),
              reads=["Floc", "offs"], writes=["Fp"])
        sc.op("dve", lambda e: e.scalar_tensor_tensor(out=gg, in0=gi, scalar=c_small[0:NCH, 8:9], in1=Fp, op0=ALU.add, op1=ALU.add),
              reads=["gi", "Fp"], writes=["gg"])
        sc.op("dve", lambda e: e.tensor_tensor_scan(out=Gloc, data0=gg, data1=gg, initial=-1.0e30, op0=ALU.max, op1=ALU.max),
              reads=["gg"], writes=["Gloc"])
        sc.op("pe", lambda e: e.matmul(ps[1][0:1, 0:NCH], Gloc[:, 127:128], identf[0:NCH, 0:NCH], start=True, stop=True),
              reads=["Gloc"], writes=[P(1)])
        sc.op("act", lambda e: e.activation(out=rowt[:, 0:NCH], in_=ps[1][0:1, 0:NCH], func=AF.Copy), reads=[P(1)], writes=["rowmax"])
        sc.op("dve", lambda e: e.tensor_tensor_scan(out=rowt[:, NCH:2 * NCH], data0=rowt[:, 0:NCH], data1=rowt[:, 0:NCH],
                                                    initial=0.0, op0=ALU.max, op1=ALU.max), reads=["rowmax"], writes=["Mrow"])
        sc.op("dve", lambda e: e.memset(rowt[:, 2 * NCH:2 * NCH + 1], 0.0), writes=["Mp0"])
        sc.op("dve", lambda e: e.tensor_copy(out=rowt[:, 2 * NCH + 1:3 * NCH], in_=rowt[:, NCH:2 * NCH - 1]),
              reads=["Mrow", "Mp0"], writes=["Mprow"])
        sc.op("pe", lambda e: e.matmul(ps[2][:, 0:2 * NCH], ones_f[0:1, 0:128], rowt[:, NCH:3 * NCH], start=True, stop=True),
              reads=["Mrow", "Mprow"], writes=[P(2)])
        sc.op("act", lambda e: e.activation(out=Mb, in_=ps[2][:, 0:2 * NCH], func=AF.Copy), reads=[P(2)], writes=["Mb"])
        sc.op("pe", lambda e: e.matmul(ps[3][0:NCH, 0:1], rowt[:, 2 * NCH:3 * NCH], ones_f[0:1, 0:1], start=True, stop=True),
              reads=["Mprow"], writes=[P(3)])
        sc.op("act", lambda e: e.activation(out=colt[0:NCH, 2:3], in_=ps[3][0:NCH, 0:1], func=AF.Copy), reads=[P(3)], writes=["Pcol"])
        sc.op("dve", lambda e: e.tensor_scalar(out=Gt, in0=Gloc, scalar1=colt[0:NCH, 2:3], scalar2=None, op0=ALU.max),
              reads=["Gloc", "Pcol"], writes=["Gt"])
        sc.op("dve", lambda e: e.tensor_scalar(out=tmpc, in0=Gt, scalar1=-1.0, scalar2=None, op0=ALU.mult),
              reads=["Gt"], writes=["tmpc"])
        sc.dma("sp", "rowsd", lambda e: e.dma_start(out=rows_d[0].rearrange("(c p) -> c p", p=128), in_=tmpc),
               reads=["tmpc"], writes=["rows0"])
        sc.dma("sp", "negGb", lambda e: e.dma_start(out=negGb, in_=rows_d[0:1, :].broadcast_to([128, S])),
               reads=["rows0"], writes=["negGb"])
        sc.op("dve", lambda e: e.tensor_tensor(out=Fp, in0=Fp, in1=Gt, op=ALU.subtract), reads=["Fp", "Gt", "gg"], writes=["Fp"])
        sc.op("pe", lambda e: e.matmul(ps[0][:, 0:NCH], gg, identf[0:NCH, 0:NCH], start=True, stop=True),
              reads=["gg"], writes=[P(0)])
        sc.op("act", lambda e: e.activation(out=gcol, in_=ps[0][:, 0:NCH], func=AF.Copy), reads=[P(0)], writes=["gcol"])
        sc.op("pe", lambda e: e.matmul(ps[1][:, 0:NCH], Fp, identf[0:NCH, 0:NCH], start=True, stop=True),
              reads=["Fp"], writes=[P(1)])
        sc.op("act", lambda e: e.activation(out=ecol, in_=ps[1][:, 0:NCH], func=AF.Exp), reads=[P(1)], writes=["ecol"])
        sc.op("dve", lambda e: e.tensor_tensor(out=tcol, in0=gcol, in1=Mb[:, 0:NCH], op=ALU.subtract), reads=["gcol", "Mb"], writes=["tcol"])
        sc.op("act", lambda e: e.activation(out=srccol, in_=tcol, func=AF.Exp), reads=["tcol"], writes=["srccol"])
        sc.op("dve", lambda e: e.tensor_tensor(out=tcol, in0=Mb[:, NCH:2 * NCH], in1=Mb[:, 0:NCH], op=ALU.subtract),
              reads=["Mb", "srccol"], writes=["tcol"])
        sc.op("act", lambda e: e.activation(out=csb, in_=tcol, func=AF.Exp), reads=["tcol"], writes=["csb"])
        argd = [mC.alloc([128, 128], F32) for _ in range(2)]
        dect = [mC.alloc([128, 128], F32) for _ in range(2)]
        wTt = [mC.alloc([128, 128], BF16) for _ in range(2)]
        qd = [mC.alloc([128, 128], BF16) for _ in range(2)]
        ksc = [mC.alloc([128, 128], BF16) for _ in range(2)]
        hh = [mC.alloc([128, 256], F32) for _ in range(2)]
        junk = mC.alloc([128, 256], F32)
        ya = [mC.alloc([128, 256], BF16) for _ in range(2)]
        yaT = [mC.alloc([128, 2, 128], BF16) for _ in range(2)]
        sm = [mC.alloc([128, 4], F32) for _ in range(2)]
        C32 = mC.alloc([128, 264], F32)
        Cbf = mC.alloc([128, 264], BF16)
        sc.op("dve", lambda e: e.memset(C32, 0.0), writes=["C32"])
        sc.op("dve", lambda e: e.memset(Cbf, 0.0), writes=["Cbf"])
        def m_head(c):
            s = c % 2
            cs = slice(c * 128, (c + 1) * 128)
            bS = s
            sc.op("pe", lambda e, cs=cs, bS=bS: e.matmul(ps[bS][:, 0:128], kT[:, cs], qT[:, cs], start=True, stop=True),
                  reads=qkkeys, writes=[P(bS)])
            sc.op("dve", lambda e, s=s, cs=cs: e.tensor_tensor(out=argd[s], in0=negGb[:, cs], in1=maskneg, op=ALU.add),
                  reads=["negGb"], writes=[("argd", s)])
            sc.op("act", lambda e, s=s, c=c: e.activation(out=argd[s], in_=argd[s], func=AF.Exp, bias=gcol[:, c:c + 1], scale=1.0),
                  reads=[("argd", s), "gcol"], writes=[("argd", s)])
            sc.op("dve", lambda e, s=s, bS=bS: e.tensor_tensor(out=wTt[s], in0=ps[bS][:, 0:128], in1=argd[s], op=ALU.mult),
                  reads=[P(bS), ("argd", s)], writes=[("wT", s)])
            sc.op("act", lambda e, s=s, c=c, cs=cs: e.activation(out=dect[s], in_=negGb[:, cs], func=AF.Exp,
                                                                 bias=Mb[:, NCH + c:NCH + c + 1], scale=1.0),
                  reads=["negGb", "Mb"], writes=[("dec", s)])
            sc.op("dve", lambda e, s=s, cs=cs: e.tensor_tensor(out=qd[s], in0=qT[:, cs], in1=dect[s], op=ALU.mult),
                  reads=[("dec", s)] + qkkeys, writes=[("qd", s)])
            sc.op("pe", lambda e, cs=cs: e.transpose(psb[6][:, 0:128], kT[:, cs], c_id), reads=qkkeys, writes=[P(6)])
            sc.op("dve", lambda e, s=s, c=c: e.tensor_scalar(out=ksc[s], in0=psb[6][:, 0:128], scalar1=srccol[:, c:c + 1],
                                                             scalar2=None, op0=ALU.mult),
                  reads=[P(6), "srccol"], writes=[("ksc", s)])

        def m_mid(c):
            s = c % 2
            bO = 2 + s

            def mmO(e, s=s, c=c, bO=bO):
                e.matmul(ps[bO][:, 0:257], qd[s], Cbf[:, 0:257], start=True, stop=False)
                return e.matmul(ps[bO][:, 0:257], wTt[s], vext[:, c, 0:257], start=False, stop=True)
            sc.op("pe", mmO, reads=[("qd", s), ("wT", s), "Cbf"] + vkeys, writes=[P(bO)])
            sc.op("pe", lambda e, s=s, c=c: e.matmul(ps[7][:, 0:257], ksc[s], vext[:, c, 0:257], start=True, stop=True),
                  reads=[("ksc", s)] + vkeys, writes=[P(7)])
            sc.op("dve", lambda e, c=c: e.scalar_tensor_tensor(out=C32[:, 0:257], in0=C32[:, 0:257], scalar=csb[:, c:c + 1],
                                                               in1=ps[7][:, 0:257], op0=ALU.mult, op1=ALU.add),
                  reads=[P(7), "C32", "csb"], writes=["C32"])
            sc.op("act", lambda e: e.activation(out=Cbf[:, 0:257], in_=C32[:, 0:257], func=AF.Copy), reads=["C32"], writes=["Cbf"])

        def m_tail(c):
            s = c % 2
            bO, bT = 2 + s, 4 + s
            sc.op("dve", lambda e, s=s, bO=bO: e.tensor_scalar(
                out=sm[s][:, 3:4], in0=ps[bO][:, 256:257], scalar1=-1.0, scalar2=None, op0=ALU.mult),
                reads=[P(bO)], writes=[("nden", s)])
            sc.op("dve", lambda e, s=s, c=c, bO=bO: e.scalar_tensor_tensor(
                out=sm[s][:, 0:1], in0=ps[bO][:, 256:257], scalar=ecol[:, c:c + 1], in1=sm[s][:, 3:4],
                op0=ALU.max, op1=ALU.max), reads=[P(bO), "ecol", ("nden", s)], writes=[("dd", s)])
            sc.op("dve", lambda e, s=s: e.reciprocal(out=sm[s][:, 0:1], in_=sm[s][:, 0:1]), reads=[("dd", s)], writes=[("dd", s)])
            sc.op("act", lambda e, s=s, bO=bO: e.activation(out=hh[s], in_=ps[bO][:, 0:256], func=AF.Copy, scale=sm[s][:, 0:1]),
                  reads=[P(bO), ("dd", s)], writes=[("hh", s)])
            sc.op("act", lambda e, s=s: e.activation(out=junk, in_=hh[s], func=AF.Square, accum_out=sm[s][:, 1:2]),
                  reads=[("hh", s)], writes=[("ss", s), "junk"])
            sc.op("act", lambda e, s=s: e.activation(out=sm[s][:, 2:3], in_=sm[s][:, 1:2], func=AF.Ln, bias=c_eps[:, 0:1], scale=1.0 / 256),
                  reads=[("ss", s)], writes=[("rs", s)])
            sc.op("act", lambda e, s=s: e.activation(out=sm[s][:, 2:3], in_=sm[s][:, 2:3], func=AF.Exp, scale=-0.5),
                  reads=[("rs", s)], writes=[("rs", s)])
            sc.op("dve", lambda e, s=s: e.scalar_tensor_tensor(out=ya[s], in0=hh[s], scalar=sm[s][:, 2:3], in1=gml_b,
                                                               op0=ALU.mult, op1=ALU.mult),
                  reads=[("hh", s), ("rs", s)], writes=[("ya", s)])

            def trY(e, s=s, bT=bT):
                e.transpose(psb[bT][:, 0:128], ya[s][:, 0:128], c_id)
                return e.transpose(psb[bT][:, 128:256], ya[s][:, 128:256], c_id)
            sc.op("pe", trY, reads=[("ya", s)], writes=[P(bT)])
            sc.op("act", lambda e, s=s, bT=bT: e.activation(out=yaT[s].rearrange("p a b -> p (a b)"), in_=psb[bT][:, 0:256], func=AF.Copy),
                  reads=[P(bT)], writes=[("yaT", s)])
            for j in range(2):
                piece, dst = yloc_ap(j, c * 128, 128)
                key = ("yloc", piece, c)
                ywrites[piece].append(key)
                sc.dma("sp", ("yaT", s), lambda e, s=s, j=j, dst=dst: e.dma_start(out=dst, in_=yaT[s][:, j, :]),
                       reads=[("yaT", s)], writes=[key])

        c_eps = mC.alloc([128, 1], F32)
        sc.op("dve", lambda e: e.memset(c_eps, EPS), writes=["c_eps"])
        m_head(0)
        for c in range(NCH + 1):
            if c + 1 < NCH:
                m_head(c + 1)
            if c < NCH:
                m_mid(c)
            if c >= 1:
                m_tail(c - 1)
        sc.barrier()
        mB.reset()
        mC.reset()
        if stop_after == "mlstm":
            return finish(xres)

        def exch2_piece(p, keys):
            sc.dma("pool", ("cc", p % 16), lambda e, p=p: e.collective_compute(
                "AllGather", ALU.bypass, replica_groups=[[0, 1, 2, 3], [4, 5, 6, 7]],
                ins=[yloc[p]], outs=[ex2mid[p]]), reads=keys, writes=[("ex2mid", p)], inc=1)
            sc.dma("pool", ("cc", p % 16), lambda e, p=p: e.collective_compute(
                "AllGather", ALU.bypass, replica_groups=[[0, 4], [1, 5], [2, 6], [3, 7]],
                ins=[ex2mid[p]], outs=[ex2out[p * 8 * cfg.PR2:(p + 1) * 8 * cfg.PR2, :]]),
                reads=[("ex2mid", p)], writes=[("ex2out", p)], inc=1)

        for p in range(2 * NPF):
            exch2_piece(p, [])
        sqT = [mB.alloc([128, S], BF16), None]
        skT = [mB.alloc([128, S], BF16), None]
        sqT[1] = mC.alloc([128, S], BF16)
        skT[1] = mC.alloc([128, S], BF16)
        vsh = [mC.alloc([128, NCH, 128], BF16) for _ in range(2)]
        for h in range(2):
            sc.dma("sp", "c", lambda e, h=h: e.dma_start(out=sqT[h], in_=sqk[2 * h]), writes=[("sq", h)])
            sc.dma("sp", "c", lambda e, h=h: e.dma_start(out=skT[h], in_=sqk[2 * h + 1]), writes=[("sk", h)])
            for c0 in range(0, NCH, 16):
                n = min(16, NCH - c0)
                sc.dma("sp", "c", lambda e, h=h, c0=c0, n=n: e.dma_start(
                    out=vsh[h][:, c0:c0 + n, :], in_=vs_d[h][c0:c0 + n].rearrange("c p d -> p c d")),
                    writes=[("vsh", h, c0)])
        vshkeys = [[("vsh", h, c0) for c0 in range(0, NCH, 16)] for h in range(2)]
        NSL = 4
        spt_ = [mC.alloc([128, 1024], F32) for _ in range(NSL)]
        sptb = [mC.alloc([128, 1024], BF16) for _ in range(NSL)]
        tt_ = [mC.alloc([128, 1024], F32) for _ in range(NSL)]
        Wt_ = [mC.alloc([128, 1024], BF16) for _ in range(NSL)]
        ost = mC.alloc([128, 1024], BF16)
        SCL = 128.0 ** -0.5
        units = []
        for qt in range(NQT):
            nkb = 4 * qt + 4
            for kb in range(nkb - 1, -1, -1):
                units.append((qt, kb, kb == nkb - 1, kb == 0, (kb - 4 * qt) if kb >= 4 * qt else None))
        NU = len(units)
        zpair = [psall[:, 0:1024], psall[:, 1024:2048]]
        apair = psall[:, 4 * 512:6 * 512]
        opair = psall[:, 6 * 512:8 * 512]
        hs_ = [slice(0, 512), slice(512, 1024)]

        def S1(u):
            qt, kb, first, last, dj = units[u]
            zb = (u % 2) * 2

            def mm(e, kb=kb, qt=qt, zb=zb):
                for h in range(2):
                    r_ = e.matmul(ps[zb + h][:, :], skT[h][:, kb * 128:(kb + 1) * 128], sqT[h][:, qt * 512:(qt + 1) * 512],
                                  start=True, stop=True)
                return r_
            sc.op("pe", mm, reads=[("sq", 0), ("sk", 0), ("sq", 1), ("sk", 1)], writes=[P(zb), P(zb + 1)])

        def S2(u):
            qt, kb, first, last, dj = units[u]
            zb, sl = (u % 2) * 2, u % NSL
            zp = zpair[u % 2]
            sc.op("act", lambda e, zp=zp, sl=sl: e.activation(out=spt_[sl], in_=zp, func=AF.Exp, scale=SCL),
                  reads=[P(zb), P(zb + 1)], writes=[("spt", sl)])
            sc.op("act", lambda e, sl=sl: e.activation(out=sptb[sl], in_=spt_[sl], func=AF.Ln, bias=1.0, scale=1.0),
                  reads=[("spt", sl)], writes=[("sptb", sl)])
            if dj is not None:
                for h in range(2):
                    sc.op("dve", lambda e, sl=sl, dj=dj, h=h: e.tensor_tensor(out=sptb[sl][:, hs_[h]], in0=sptb[sl][:, hs_[h]],
                                                                           in1=dmask[dj], op=ALU.mult),
                          reads=[("sptb", sl)], writes=[("sptb", sl)])

        def S3(u):
            qt, kb, first, last, dj = units[u]
            zb, sl = (u % 2) * 2, u % NSL
            zp = zpair[u % 2]
            sc.op("dve", lambda e, zp=zp, sl=sl: e.scalar_tensor_tensor(
                out=tt_[sl], in0=zp, scalar=SCL, in1=sptb[sl], op0=ALU.mult, op1=ALU.subtract),
                reads=[P(zb), P(zb + 1), ("sptb", sl)], writes=[("tt", sl)])

            def mm(e, sl=sl, first=first):
                for h in range(2):
                    r_ = e.matmul(ps[4 + h][:, :], triR, sptb[sl][:, hs_[h]], start=first, stop=False)
                return r_
            sc.op("pe", mm, reads=[("sptb", sl)], writes=[P(4), P(5)])

        def S4(u):
            qt, kb, first, last, dj = units[u]
            sl = u % NSL
            sc.op("dve", lambda e, sl=sl: e.tensor_tensor(out=tt_[sl], in0=apair, in1=tt_[sl], op=ALU.add),
                  reads=[P(4), P(5), ("tt", sl)], writes=[("tt", sl)])
            sc.op("act", lambda e, sl=sl: e.activation(out=Wt_[sl], in_=tt_[sl], func=AF.Exp), reads=[("tt", sl)], writes=[("W", sl)])
            if dj is not None:
                for h in range(2):
                    sc.op("dve", lambda e, sl=sl, dj=dj, h=h: e.tensor_tensor(out=Wt_[sl][:, hs_[h]], in0=Wt_[sl][:, hs_[h]],
                                                                           in1=dmask[dj], op=ALU.mult),
                          reads=[("W", sl)], writes=[("W", sl)])

        def S5a(u):
            qt, kb, first, last, dj = units[u]
            sl = u % NSL

            def mm(e, sl=sl, last=last):
                for h in range(2):
                    r_ = e.matmul(ps[4 + h][:, :], restR, sptb[sl][:, hs_[h]], start=False, stop=last)
                return r_
            sc.op("pe", mm, reads=[("sptb", sl), ("tt", sl)], writes=[P(4), P(5)])

        def S5(u):
            qt, kb, first, last, dj = units[u]
            sl = u % NSL

            def mm(e, sl=sl, kb=kb, first=first, last=last):
                for h in range(2):
                    r_ = e.matmul(ps[6 + h][:, :], vsh[h][:, kb, :], Wt_[sl][:, hs_[h]], start=first, stop=last)
                return r_
            sc.op("pe", mm, reads=[("W", sl)] + vshkeys[0] + vshkeys[1], writes=[P(6), P(7)])
            if last:
                sc.op("act", lambda e: e.activation(out=ost, in_=opair, func=AF.Copy), reads=[P(6), P(7)], writes=["ost"])
                for h in range(2):
                    piece, dst = yloc_ap(2 + h, qt * 512, 512)
                    key = ("yloc", piece, qt)
                    ywrites[piece].append(key)
                    sc.dma("sp", ("ost", h), lambda e, h=h, dst=dst: e.dma_start(out=dst, in_=ost[:, hs_[h]]),
                           reads=["ost"], writes=[key])
                tok_end = (qt + 1) * 512
                if tok_end % (ND2 * T) == 0:
                    for h in range(2):
                        piece, _ = yloc_ap(2 + h, qt * 512, 512)
                        exch2_piece(piece, list(ywrites[piece]))

        for i in range(-2, NU):
            if 0 <= i + 2 < NU:
                S1(i + 2)
                S2(i + 2)
            if 0 <= i < NU:
                S5a(i)
            if 0 <= i + 1 < NU:
                S3(i + 1)
                S4(i + 1)
            if 0 <= i < NU:
                S5(i)
        sc.barrier()
        mB.reset()
        mC.reset()
        if stop_after == "sb":
            return finish(xres)

        pid_cache = {}
        yaT_ = mB.alloc([128, 16, T], BF16)
        ybT_ = mB.alloc([128, 16, T], BF16) if 2 * 16 * T * 2 <= HB else mC.alloc([128, 16, T], BF16)
        mergedT = mC.alloc([128, KC, T], BF16)
        FCR = NPF * 8 * cfg.PR2
        for fc in range(4):
            def ld(e, fc=fc):
                if "E" not in pid_cache:
                    pid_ = nc.partition_id()
                    if ND2 == 1:
                        pid_cache["E"] = pid_ * (8 * cfg.PR2)
                    else:
                        pid_cache["E"] = (pid_ // ND2) * (8 * cfg.PR2) + (pid_ % ND2) * 128
                E = pid_cache["E"]
                A = ex2out[bass.ds(E, 4 * FCR), :].rearrange("(fc x r f) t -> fc x r f t", fc=4, x=NPF, r=8, f=cfg.PR2)
                return e.dma_start(out=ex2stage[fc], in_=A[fc, 0, :, 0:128, :])
            sc.dma("pool", "yst", ld, writes=[("ystage", fc)])
        for r in range(NCORES):
            for j in range(2):
                for fc, dstt in ((j, yaT_), (2 + j, ybT_)):
                    sc.dma("sp", "yld", lambda e, fc=fc, dstt=dstt, r=r, j=j: e.dma_start(
                        out=dstt[:, 2 * r + j, :], in_=ex2stage[fc][r]),
                        reads=[("ystage", fc)], writes=[("yin", fc // 2, 2 * r + j)])
        gtile = [mC.alloc([128, T], BF16) for _ in range(4)]
        for kc in range(16):
            s = kc % 2
            sc.dma("sp", ("gt", s), lambda e, kc=kc, s=s: e.dma_start(out=gtile[s], in_=gts[kc]), writes=[("gt", s)])
            sc.op("dve", lambda e, kc=kc, s=s: e.tensor_tensor(out=yaT_[:, kc, :], in0=yaT_[:, kc, :], in1=gtile[s], op=ALU.mult),
                  reads=[("gt", s), ("yin", 0, kc)], writes=[("yin", 0, kc)])
        wpt = [[mC.alloc([128, 16, 128], BF16) for _ in range(2)] for _ in range(2)]
        m1 = [mC.alloc([128, 512], F32) for _ in range(2)]
        m2 = [mC.alloc([128, 512], F32) for _ in range(2)]
        yakeys = [("yin", 0, kc) for kc in range(16)]
        ybkeys = [("yin", 1, kc) for kc in range(16)]
        cnt = 0
        for o in range(KC):
            s = o % 2
            sc.dma("pool", ("wpa", s), lambda e, o=o, s=s: e.dma_start(
                out=wpt[0][s].rearrange("p a b -> p (a b)"), in_=wpa[o], max_dma_last_dim=8192), writes=[("wpa", s)])
            sc.dma("pool", ("wpb", s), lambda e, o=o, s=s: e.dma_start(
                out=wpt[1][s].rearrange("p a b -> p (a b)"), in_=wpb[o], max_dma_last_dim=8192), writes=[("wpb", s)])
            sc.dma("sp", ("gt", 2), lambda e, o=o: e.dma_start(out=gtile[2], in_=gts[16 + o]), writes=[("gt", 2)])
            sc.dma("sp", ("gt", 3), lambda e, o=o: e.dma_start(out=gtile[3], in_=gts[16 + KC + o]), writes=[("gt", 3)])
            for th in range(TH):
                sl = slice(th * 512, (th + 1) * 512)
                b = (cnt % 2) * 2
                q = cnt % 2
                cnt += 1

                def mm(e, s=s, sl=sl, b=b):
                    for kc in range(16):
                        e.matmul(ps[b][:, :], wpt[0][s][:, kc, :], yaT_[:, kc, sl], start=(kc == 0), stop=(kc == 15))
                    for kc in range(16):
                        r_ = e.matmul(ps[b + 1][:, :], wpt[1][s][:, kc, :], ybT_[:, kc, sl], start=(kc == 0), stop=(kc == 15))
                    return r_
                sc.op("pe", mm, reads=[("wpa", s), ("wpb", s)] + yakeys + ybkeys, writes=[P(b), P(b + 1)])
                sc.op("dve", lambda e, b=b, q=q, sl=sl: e.tensor_tensor(out=m1[q], in0=ps[b][:, :], in1=gtile[2][:, sl], op=ALU.mult),
                      reads=[P(b), ("gt", 2)], writes=[("m1", q)])
                sc.op("dve", lambda e, b=b, q=q, sl=sl: e.tensor_tensor(out=m2[q], in0=ps[b + 1][:, :], in1=gtile[3][:, sl], op=ALU.mult),
                      reads=[P(b + 1), ("gt", 3)], writes=[("m2", q)])
                sc.op("dve", lambda e, q=q, o=o, sl=sl: e.tensor_tensor(out=mergedT[:, o, sl], in0=m1[q], in1=m2[q], op=ALU.add),
                      reads=[("m1", q), ("m2", q)], writes=[("mg", o, th)])
        sc.barrier()
        mB.reset()
        wot = [mB.alloc([128, KC, 128], BF16) for _ in range(2)]
        xo = [mB.alloc([128, 512], F32) for _ in range(4)]
        cnt = 0
        for o in range(KC):
            s = o % 2
            sc.dma("pool", ("wot", s), lambda e, o=o, s=s: e.dma_start(
                out=wot[s].rearrange("p a b -> p (a b)"), in_=wo[o], max_dma_last_dim=8192), writes=[("wot", s)])
            for th in range(TH):
                sl = slice(th * 512, (th + 1) * 512)
                b = 4 + cnt % 4
                xs_ = cnt % 4
                cnt += 1

                def mm(e, s=s, sl=sl, b=b):
                    for kc in range(KC):
                        r_ = e.matmul(ps[b][:, :], wot[s][:, kc, :], mergedT[:, kc, sl], start=(kc == 0), stop=(kc == KC - 1))
                    return r_
                sc.op("pe", mm, reads=[("wot", s)], writes=[P(b)])
                sc.dma("sp", ("xo", xs_), lambda e, o=o, sl=sl, xs_=xs_: e.dma_start(out=xo[xs_], in_=xres[o][:, sl]),
                       writes=[("xo", xs_)])
                sc.op("dve", lambda e, b=b, xs_=xs_: e.tensor_tensor(out=xo[xs_], in0=ps[b][:, :], in1=xo[xs_], op=ALU.add),
                      reads=[P(b), ("xo", xs_)], writes=[("xo", xs_)])
                sc.dma("sp", ("xo", xs_), lambda e, o=o, sl=sl, xs_=xs_: e.dma_start(out=xres[o][:, sl], in_=xo[xs_]),
                       reads=[("xo", xs_)])
        sc.barrier()
        mB.reset()
        mC.reset()
        if stop_after == "merge":
            return finish(xres)

        hT = mB.alloc([128, KC, T], BF16)
        rmsnorm(xres, 2, hT)
        ffn(1, xres, hT)
        rmsnorm(xres, 3, None, outT)
        return finish(None)


def make_consts():
    j = np.arange(128)[:, None]
    s = np.arange(128)[None, :]
    ones = np.ones((128, 128), np.float32)
    triN = np.where(j > s, -1.0, 0.0).astype(np.float32)
    restN = np.where(j <= s, -1.0, 0.0).astype(np.float32)
    maskneg = np.where(j <= s, 0.0, NEG).astype(np.float32)
    U = np.where(j < s, 1.0, 0.0).astype(np.float32)
    identf = np.eye(128, dtype=np.float32)
    t = np.arange(512)[None, :]
    masks = [np.where(128 * dj + j < t, 1.0, 0.0).astype(np.float32) for dj in range(4)]
    cf = np.concatenate([ones, triN, restN, maskneg, U, identf] + masks, axis=1)
    return np.ascontiguousarray(cf), np.eye(128, dtype=np.float32).astype(ml_dtypes.bfloat16)


def _lhsT_layout(w, kc, oc):
    K, O = w.shape
    return np.ascontiguousarray(w.reshape(kc, 128, oc, 128).transpose(2, 1, 0, 3).reshape(oc, 128, kc * 128))


def prep_inputs(cfg, inp):
    D, T, KC, FC = cfg.D, cfg.T, cfg.KC, cfg.FC
    f32 = np.float32
    x = np.asarray(inp["x"], f32)[0]
    shared = {}
    for l, sfx in ((0, "a"), (1, "b")):
        n = "ffn1" if l == 0 else "ffn2"
        shared["w1" + sfx] = _lhsT_layout(np.asarray(inp["w1_" + n], f32)[0], KC, FC)
        shared["w3" + sfx] = _lhsT_layout(np.asarray(inp["w3_" + n], f32)[0], KC, FC)
        shared["w2" + sfx] = _lhsT_layout(np.asarray(inp["w2_" + n], f32)[0], FC, KC)
    gs = [np.asarray(inp[k], f32).reshape(-1) for k in ("g_ffn1", "g_mix", "g_ffn2", "g_final")]
    shared["gv"] = np.ascontiguousarray(np.concatenate([g.reshape(KC, 128).T for g in gs], axis=1))
    w_in = np.asarray(inp["w_in"], f32)[0]
    o_mq, o_mk, o_mv, o_mo, o_mi, o_mf = 0, 1024, 2048, 4096, 6144, 6152
    o_sq, o_sk, o_sv = 6160, 8208, 10256
    o_ga = 12304
    o_gb = o_ga + D
    gate_cols = np.concatenate([np.arange(o_mo, o_mo + 2048), np.arange(o_ga, o_ga + D), np.arange(o_gb, o_gb + D)])
    wgfull = w_in[:, gate_cols]
    shared["wG"] = _lhsT_layout(wgfull, KC, 16 + 2 * KC)
    shared["wpa"] = _lhsT_layout(np.asarray(inp["w_proj_a"], f32)[0], 16, KC)
    shared["wpb"] = _lhsT_layout(np.asarray(inp["w_proj_b"], f32)[0], 16, KC)
    shared["wo"] = _lhsT_layout(np.asarray(inp["w_out"], f32)[0], KC, KC)
    cf, idb = make_consts()
    shared["cf32"] = cf
    shared["identb"] = idb
    conv = np.asarray(inp["conv_qk"], f32)[0]
    b_i = np.asarray(inp["b_igate"], f32)[0]
    b_f = np.asarray(inp["b_fgate"], f32)[0]
    gml = np.asarray(inp["g_mlstm_out"], f32)[0]

    def pk(wcols):
        n = wcols.shape[1]
        return np.ascontiguousarray(wcols.reshape(KC, 128, n).transpose(1, 0, 2).reshape(128, KC * n))

    maps = []
    for c in range(NCORES):
        m = dict(shared)
        m["xT"] = np.ascontiguousarray(x[c * T:(c + 1) * T, :].T.reshape(KC, 128, T))
        colsA = np.concatenate([
            np.arange(o_mq + c * 128, o_mq + (c + 1) * 128), np.arange(o_mk + c * 128, o_mk + (c + 1) * 128),
            np.arange(o_sq + (2 * c) * 128, o_sq + (2 * c + 1) * 128), np.arange(o_sk + (2 * c) * 128, o_sk + (2 * c + 1) * 128),
            np.arange(o_sq + (2 * c + 1) * 128, o_sq + (2 * c + 2) * 128), np.arange(o_sk + (2 * c + 1) * 128, o_sk + (2 * c + 2) * 128)])
        m["wA"] = pk(w_in[:, colsA])
        wg = np.zeros((D, 64), f32)
        wg[:, 0] = w_in[:, o_mi + c]
        wg[:, 32] = w_in[:, o_mf + c]
        m["wGt"] = pk(wg)
        colsV = np.concatenate([np.arange(o_mv + c * 256, o_mv + (c + 1) * 256),
                                np.arange(o_sv + (2 * c) * 128, o_sv + (2 * c + 2) * 128)])
        m["wV"] = pk(w_in[:, colsV])
        sm = np.zeros((128, 16 + 256), f32)
        sm[:, 0:4] = conv[:, c * 128:(c + 1) * 128].T
        sm[:, 4:8] = conv[:, 1024 + c * 128:1024 + (c + 1) * 128].T
        sm[:, 8] = b_i[c]
        sm[:, 9] = b_f[c]
        sm[:, 16:272] = gml[c * 256:(c + 1) * 256][None, :]
        m["small"] = sm
        maps.append(m)
    return maps


_PROG_CACHE = {}


def run(cfg, inp, stop_after=None, trace=False):
    key = (cfg.D, cfg.T, cfg.DFF, stop_after)
    if key not in _PROG_CACHE:
        _PROG_CACHE[key] = build_program(cfg, stop_after)
    nc = _PROG_CACHE[key]
    maps = prep_inputs(cfg, inp)
    res = run_bass_kernel_spmd(nc, maps, core_ids=list(range(NCORES)), trace=trace)
    name = "dbg" if stop_after else "outT"
    outs = [r[name].reshape(cfg.D, cfg.T).T for r in res.results]
    full = np.concatenate(outs, axis=0)[None].astype(np.float32)
    if stop_after:
        ya = np.zeros((cfg.S, 2048), np.float32); yb = np.zeros((cfg.S, 2048), np.float32)
        for c, r in enumerate(res.results):
            y = r["dbg2"].astype(np.float32)
            for fc in range(4):
                for dc in range(NCORES):
                    blk = y[fc * cfg.NPF + dc // cfg.ND2][(dc % cfg.ND2) * 128:(dc % cfg.ND2) * 128 + 128, :]
                    if fc < 2:
                        ya[dc * cfg.T:(dc + 1) * cfg.T, c * 256 + fc * 128:c * 256 + (fc + 1) * 128] = blk.T
                    else:
                        yb[dc * cfg.T:(dc + 1) * cfg.T, (2 * c + fc - 2) * 128:(2 * c + fc - 1) * 128] = blk.T
        res.ya, res.yb = ya, yb
    return full, res


def kernel(**inputs):
    out, _ = run(Cfg(), inputs)
    return out
```
